# Optimizing a Trainium2 kernel written in Bass

```python
import jax, jax.numpy as jnp
from jax import lax
import numpy as np

D_MODEL = 2048
BATCH = 2
SEQ = 8192
DEPTH = 2
DEC_BATCH = 16
DEC_SEQ = 16
PAST_LEN = 1024

CHUNK = 64
N_MIXERS = 2
N_CONV_LAYERS = (DEPTH + 1) // 2
N_FOX_LAYERS = DEPTH // 2
HEAD_DIM = 128
MIX_WIDTH = D_MODEL
XATTN_HEADS = 4
XATTN_WIDTH = XATTN_HEADS * HEAD_DIM
MIXER_WIDTH = MIX_WIDTH - XATTN_WIDTH
CONV_CH = MIXER_WIDTH
CONV_WIDTH = 31
CONV_STATE = CONV_WIDTH - 1
FOX_HEADS = MIXER_WIDTH // HEAD_DIM
FOX_WIDTH = FOX_HEADS * HEAD_DIM
CONV_IN = 2 * CONV_CH + XATTN_WIDTH
FOX_IN = 3 * FOX_WIDTH + FOX_HEADS + XATTN_WIDTH
N_MEM = 256
D_FF = 4 * D_MODEL
Q_BLOCK = 128
EPS = 1e-6
ATTN_SCALE = HEAD_DIM ** -0.5

kernel_name = "hybrid_conv_fox_stream_step"


def rmsnorm(x, g):
    xf = x.astype(jnp.float32)
    y = xf * lax.rsqrt(jnp.mean(jnp.square(xf), axis=-1, keepdims=True) + EPS)
    return (y * g.astype(jnp.float32)).astype(x.dtype)


def layernorm(x, g, b):
    xf = x.astype(jnp.float32)
    xc = xf - jnp.mean(xf, axis=-1, keepdims=True)
    y = xc * lax.rsqrt(jnp.mean(jnp.square(xc), axis=-1, keepdims=True) + EPS)
    return (y * g.astype(jnp.float32) + b.astype(jnp.float32)).astype(x.dtype)


def causal_dwconv(u_hist, w, b):
    out = lax.conv_general_dilated(u_hist, w[:, None, :].astype(u_hist.dtype), window_strides=(1,),
                                   padding='VALID', dimension_numbers=('NWC', 'WIO', 'NWC'),
                                   feature_group_count=u_hist.shape[-1])
    return out + b.astype(out.dtype)


def conformer_conv(z, u_prev, conv_w, conv_b, ln_g, ln_b):
    a, gate = jnp.split(z, 2, axis=-1)
    u = a * jax.nn.sigmoid(gate)
    u_hist = jnp.concatenate([u_prev.astype(u.dtype), u], axis=1)
    c = layernorm(causal_dwconv(u_hist, conv_w, conv_b), ln_g, ln_b)
    return jax.nn.silu(c), u_hist[:, u_hist.shape[1] - CONV_STATE:]


def fox_project(h, w_in, b_f):
    bsz, t = h.shape[0], h.shape[1]
    z = h @ w_in
    q = z[..., :FOX_WIDTH].reshape(bsz, t, FOX_HEADS, HEAD_DIM)
    k = z[..., FOX_WIDTH:2 * FOX_WIDTH].reshape(bsz, t, FOX_HEADS, HEAD_DIM)
    v = z[..., 2 * FOX_WIDTH:3 * FOX_WIDTH].reshape(bsz, t, FOX_HEADS, HEAD_DIM)
    f = z[..., 3 * FOX_WIDTH:3 * FOX_WIDTH + FOX_HEADS]
    qx = z[..., 3 * FOX_WIDTH + FOX_HEADS:].reshape(bsz, t, XATTN_HEADS, HEAD_DIM)
    logf = jax.nn.log_sigmoid(f.astype(jnp.float32) + b_f.astype(jnp.float32))
    return q, k, v, logf, qx


def fox_attend(q, k, v, cq, ck, q_pos, k_pos):
    s = jnp.einsum('bqhd,bkhd->bhqk', q, k, preferred_element_type=jnp.float32) * ATTN_SCALE
    bias = jnp.swapaxes(cq, 1, 2)[:, :, :, None] - jnp.swapaxes(ck, 1, 2)[:, :, None, :]
    mask = (k_pos[None, :] <= q_pos[:, None])[None, None]
    s = jnp.where(mask, s + bias, -jnp.inf)
    p = jax.nn.softmax(s, axis=-1)
    return jnp.einsum('bhqk,bkhd->bqhd', p.astype(v.dtype), v)


def fox_prompt_attention(q, k, v, logf):
    bsz, s_len = q.shape[0], q.shape[1]
    n_blk = s_len // Q_BLOCK
    c = jnp.cumsum(logf, axis=1)
    qb = q.reshape(bsz, n_blk, Q_BLOCK, FOX_HEADS, HEAD_DIM).transpose(1, 0, 2, 3, 4)
    cb = c.reshape(bsz, n_blk, Q_BLOCK, FOX_HEADS).transpose(1, 0, 2, 3)
    k_pos = jnp.arange(s_len)

    def one_block(args):
        i, qi, ci = args
        return fox_attend(qi, k, v, ci, c, i * Q_BLOCK + jnp.arange(Q_BLOCK), k_pos)

    ob = lax.map(one_block, (jnp.arange(n_blk), qb, cb))
    return ob.transpose(1, 0, 2, 3, 4).reshape(bsz, s_len, FOX_WIDTH)


def fox_sample_attention(q, k, v, logf, cache_k, cache_v, cache_logf):
    bsz, t = q.shape[0], q.shape[1]
    past = cache_k.shape[1]
    k_all = jnp.concatenate([cache_k.astype(k.dtype), k], axis=1)
    v_all = jnp.concatenate([cache_v.astype(v.dtype), v], axis=1)
    c_all = jnp.cumsum(jnp.concatenate([cache_logf.astype(jnp.float32), logf], axis=1), axis=1)
    o = fox_attend(q, k_all, v_all, c_all[:, past:], c_all, past + jnp.arange(t), jnp.arange(past + t))
    return o.reshape(bsz, t, FOX_WIDTH)


def memory_kv(mem, g, w_k, w_v):
    bsz = mem.shape[0]
    m = rmsnorm(mem, g)
    mk = (m @ w_k).reshape(bsz, N_MEM, XATTN_HEADS, HEAD_DIM)
    mv = (m @ w_v).reshape(bsz, N_MEM, XATTN_HEADS, HEAD_DIM)
    return mk, mv


def mem_attend(qx, mk, mv):
    bsz, t = qx.shape[0], qx.shape[1]
    s = jnp.einsum('bqhd,bmhd->bhqm', qx, mk.astype(qx.dtype), preferred_element_type=jnp.float32) * ATTN_SCALE
    p = jax.nn.softmax(s, axis=-1)
    o = jnp.einsum('bhqm,bmhd->bqhd', p.astype(qx.dtype), mv.astype(qx.dtype))
    return o.reshape(bsz, t, XATTN_WIDTH)


def sq_relu_mlp(x, g, w_up, w_down):
    h = rmsnorm(x, g) @ w_up
    return jnp.square(jax.nn.relu(h)) @ w_down


def setup_inputs(seed: int = 0) -> dict:
    key = jax.random.key(seed)
    ks = jax.random.split(key, 26)
    d = D_MODEL

    def nrm(k, shape, scale=1.0):
        return scale * jax.random.normal(k, shape, jnp.float32)

    return {
        'x_prompt': nrm(ks[0], (BATCH, SEQ, d)),
        'x_sample': nrm(ks[1], (DEC_BATCH, DEC_SEQ, d)),
        'cache_mem_k': nrm(ks[2], (DEPTH, DEC_BATCH, N_MEM, XATTN_HEADS, HEAD_DIM)),
        'cache_mem_v': nrm(ks[3], (DEPTH, DEC_BATCH, N_MEM, XATTN_HEADS, HEAD_DIM)),
        'state_conv': nrm(ks[4], (N_CONV_LAYERS, DEC_BATCH, CONV_STATE, CONV_CH), 0.5),
        'cache_fox_k': nrm(ks[5], (N_FOX_LAYERS, DEC_BATCH, PAST_LEN, FOX_HEADS, HEAD_DIM)),
        'cache_fox_v': nrm(ks[6], (N_FOX_LAYERS, DEC_BATCH, PAST_LEN, FOX_HEADS, HEAD_DIM)),
        'cache_fox_logf': jax.nn.log_sigmoid(nrm(ks[7], (N_FOX_LAYERS, DEC_BATCH, PAST_LEN, FOX_HEADS)) + 2.5),
        'mem_prompt': nrm(ks[8], (BATCH, N_MEM, d)),
        'g_mix': 1.0 + nrm(ks[9], (DEPTH, d), 0.02),
        'g_mem': 1.0 + nrm(ks[10], (DEPTH, d), 0.02),
        'w_mem_k': nrm(ks[11], (DEPTH, d, XATTN_WIDTH), d ** -0.5),
        'w_mem_v': nrm(ks[12], (DEPTH, d, XATTN_WIDTH), d ** -0.5),
        'w_in_conv': nrm(ks[13], (N_CONV_LAYERS, d, CONV_IN), d ** -0.5),
        'conv_w': nrm(ks[14], (N_CONV_LAYERS, CONV_WIDTH, CONV_CH), CONV_WIDTH ** -0.5),
        'conv_b': nrm(ks[15], (N_CONV_LAYERS, CONV_CH), 0.02),
        'conv_ln_g': 1.0 + nrm(ks[16], (N_CONV_LAYERS, CONV_CH), 0.02),
        'conv_ln_b': nrm(ks[17], (N_CONV_LAYERS, CONV_CH), 0.02),
        'w_in_fox': nrm(ks[18], (N_FOX_LAYERS, d, FOX_IN), d ** -0.5),
        'b_fox_f': jax.random.uniform(ks[19], (N_FOX_LAYERS, FOX_HEADS), jnp.float32, 1.0, 4.0),
        'w_out': nrm(ks[20], (DEPTH, MIX_WIDTH, d), MIX_WIDTH ** -0.5),
        'g_mlp': 1.0 + nrm(ks[21], (DEPTH, d), 0.02),
        'w_up': nrm(ks[22], (DEPTH, d, D_FF), d ** -0.5),
        'w_down': nrm(ks[23], (DEPTH, D_FF, d), D_FF ** -0.5),
        'g_final': 1.0 + nrm(ks[24], (d,), 0.02),
    }


def reference(x_prompt, x_sample, cache_mem_k, cache_mem_v, state_conv, cache_fox_k, cache_fox_v, cache_fox_logf,
              mem_prompt, g_mix, g_mem, w_mem_k, w_mem_v, w_in_conv, conv_w, conv_b, conv_ln_g, conv_ln_b,
              w_in_fox, b_fox_f, w_out, g_mlp, w_up, w_down, g_final):
    yp, ys = x_prompt, x_sample
    bp, sp = x_prompt.shape[0], x_prompt.shape[1]
    bs, ss = x_sample.shape[0], x_sample.shape[1]
    mem_k_p, mem_v_p, conv_p, conv_s = [], [], [], []
    fk_p, fv_p, fl_p, fk_s, fv_s, fl_s = [], [], [], [], [], []
    for i in range(DEPTH):
        j = i // N_MIXERS
        mk_p, mv_p = memory_kv(mem_prompt, g_mem[i], w_mem_k[i], w_mem_v[i])
        mem_k_p.append(mk_p)
        mem_v_p.append(mv_p)
        hp = rmsnorm(yp, g_mix[i])
        hs = rmsnorm(ys, g_mix[i])
        if i % N_MIXERS == 0:
            zp = hp @ w_in_conv[j]
            zs = hs @ w_in_conv[j]
            u0 = jnp.zeros((bp, CONV_STATE, CONV_CH), dtype=zp.dtype)
            mix_p, st_p = conformer_conv(zp[..., :2 * CONV_CH], u0, conv_w[j], conv_b[j], conv_ln_g[j], conv_ln_b[j])
            mix_s, st_s = conformer_conv(zs[..., :2 * CONV_CH], state_conv[j], conv_w[j], conv_b[j],
                                         conv_ln_g[j], conv_ln_b[j])
            conv_p.append(st_p)
            conv_s.append(st_s)
            qx_p = zp[..., 2 * CONV_CH:].reshape(bp, sp, XATTN_HEADS, HEAD_DIM)
            qx_s = zs[..., 2 * CONV_CH:].reshape(bs, ss, XATTN_HEADS, HEAD_DIM)
        else:
            q, k, v, lf, qx_p = fox_project(hp, w_in_fox[j], b_fox_f[j])
            mix_p = fox_prompt_attention(q, k, v, lf)
            fk_p.append(k)
            fv_p.append(v)
            fl_p.append(lf)
            q, k, v, lf, qx_s = fox_project(hs, w_in_fox[j], b_fox_f[j])
            mix_s = fox_sample_attention(q, k, v, lf, cache_fox_k[j], cache_fox_v[j], cache_fox_logf[j])
            fk_s.append(k)
            fv_s.append(v)
            fl_s.append(lf)
        xo_p = mem_attend(qx_p, mk_p, mv_p)
        xo_s = mem_attend(qx_s, cache_mem_k[i], cache_mem_v[i])
        yp = yp + jnp.concatenate([mix_p, xo_p.astype(mix_p.dtype)], axis=-1) @ w_out[i]
        ys = ys + jnp.concatenate([mix_s, xo_s.astype(mix_s.dtype)], axis=-1) @ w_out[i]
        yp = yp + sq_relu_mlp(yp, g_mlp[i], w_up[i], w_down[i])
        ys = ys + sq_relu_mlp(ys, g_mlp[i], w_up[i], w_down[i])
    y_prompt = rmsnorm(yp, g_final)
    y_sample = rmsnorm(ys, g_final)
    return (y_prompt, y_sample, jnp.stack(mem_k_p), jnp.stack(mem_v_p), jnp.stack(conv_p), jnp.stack(conv_s),
            jnp.stack(fk_p), jnp.stack(fv_p), jnp.stack(fl_p), jnp.stack(fk_s), jnp.stack(fv_s), jnp.stack(fl_s))
```

```python
import contextlib
import os
import numpy as np
import concourse.bass as bass
import concourse.mybir as mybir
from concourse.bass_utils import run_bass_kernel_spmd

F32 = mybir.dt.float32
BF16 = mybir.dt.bfloat16
AF = mybir.ActivationFunctionType
ALU = mybir.AluOpType
EPOCH = 20000
KSTOP = int(os.environ.get('KSTOP', '0'))
KSUB = int(os.environ.get('KSUB', '0'))
KNOCC = int(os.environ.get('KNOCC', '0'))


MINI_SKIP = ()


class StopEmit(Exception):
    pass
EPS = 1e-6
SCALE = 128 ** -0.5
NEG = -30000.0
TL = [(0, 512), (512, 512), (1024, 16)]
NT = 1040


class Rec:
    __slots__ = ("eng", "sem", "val", "flag")

    def __init__(self, eng):
        self.eng = eng
        self.sem = None
        self.val = None
        self.flag = False


class Buf:
    __slots__ = ("name", "w", "r", "dcount", "excl")

    def __init__(self, name="b", excl=False):
        self.name = name
        self.w = None
        self.r = {}
        self.dcount = 0
        self.excl = excl


class Prog:
    COMPUTE = ("pe", "act", "dve", "pool")
    ENGS = ("pe", "act", "dve", "pool", "sp")

    def __init__(self, nc):
        self.nc = nc
        self.dry = False
        self.stream = {e: [] for e in self.ENGS}
        self.dma_bufs = {}
        self.final_waits = []
        self.all_dma = {}
        self.nops = 0

    def _deps(self, eng, reads, writes, semkey=None):
        deps = []
        for b in reads:
            if b.w is not None:
                deps.append(b.w)
            if b.excl:
                deps.extend(r for k, r in b.r.items() if k != eng)
        for b in writes:
            if b.w is not None:
                if not (semkey is not None and b.w.sem == semkey):
                    deps.append(b.w)
            deps.extend(b.r.values())
        out = []
        seen = set()
        for d in deps:
            if id(d) in seen:
                continue
            seen.add(id(d))
            if d.eng == "pe" and eng == "pe":
                continue
            out.append(d)
        return out

    def op(self, eng, fn, reads=(), writes=()):
        if self.dry:
            return None
        deps = self._deps(eng, reads, writes)
        st = self.stream[eng]
        for d in deps:
            d.flag = True
            st.append(("wait", d))
        rec = Rec(eng)
        st.append(("op", fn, rec))
        for b in reads:
            b.r[eng] = rec
        for b in writes:
            b.w = rec
            b.r = {}
        self.nops += 1
        return rec

    def dma(self, queue, fn, owner, reads=(), writes=(), final=False):
        if self.dry:
            return None
        key = ("dma", id(owner))
        deps = self._deps("dma", reads, writes, semkey=key)
        st = self.stream[queue]
        for d in deps:
            d.flag = True
            st.append(("wait", d))
        rec = Rec("dma")
        owner.dcount += 1
        self.dma_bufs[id(owner)] = owner
        rec.sem = key
        rec.val = 16 * owner.dcount
        rec.flag = True
        st.append(("dma", fn, rec))
        for b in reads:
            b.r[key] = rec
        for b in writes:
            b.w = rec
            b.r = {}
        self.all_dma[key] = rec
        if final:
            self.final_waits.append(rec)
        return rec

    def barrier(self):
        if self.dry:
            return
        recs = []
        for e in self.COMPUTE:
            last = None
            for it in reversed(self.stream[e]):
                if it[0] == "op":
                    last = it[2]
                    break
            if last is not None:
                last.flag = True
                recs.append(last)
        recs.extend(self.all_dma.values())
        for e in self.ENGS:
            for r in recs:
                self.stream[e].append(("wait", r))

    def build(self):
        nc = self.nc
        nepoch = {}
        for e in self.COMPUTE:
            cnt = 0
            for it in self.stream[e]:
                if it[0] == "op" and it[2].flag:
                    rec = it[2]
                    ep = cnt // EPOCH
                    rec.sem = (e, ep)
                    rec.val = cnt - ep * EPOCH + 1
                    cnt += 1
            nepoch[e] = (cnt + EPOCH - 1) // EPOCH if cnt else 0
        for rec in self.final_waits:
            self.stream["sp"].append(("wait", rec))
        with contextlib.ExitStack() as es:
            sems = {}
            for e in self.COMPUTE:
                for ep in range(nepoch[e]):
                    sems[(e, ep)] = es.enter_context(nc.semaphore(f"c_{e}_{ep}"))
            for i, (k, b) in enumerate(self.dma_bufs.items()):
                sems[("dma", k)] = es.enter_context(nc.semaphore(f"d{i}"))
            print("PROG stats: sems", len(sems), "ops", {e: len(v) for e, v in self.stream.items()}, flush=True)
            block = es.enter_context(nc.Block())
            handles = {"pe": "tensor", "act": "scalar", "dve": "vector", "pool": "gpsimd", "sp": "sync"}

            def replay(engname):
                def run(eng):
                    waited = {}
                    for it in self.stream[engname]:
                        if it[0] == "wait":
                            d = it[1]
                            if waited.get(d.sem, 0) >= d.val:
                                continue
                            if d.eng != "dma" and d.sem[0] == engname and d.eng == "pe":
                                continue
                            waited[d.sem] = d.val
                            if d.eng != "dma":
                                for ep in range(d.sem[1]):
                                    waited[(d.sem[0], ep)] = EPOCH
                            eng.wait_ge(sems[d.sem], d.val)
                        else:
                            _, fn, rec = it
                            ins = fn(eng)
                            if rec.flag:
                                if rec.eng == "dma":
                                    ins.then_inc(sems[rec.sem], 16)
                                else:
                                    ins.then_inc(sems[rec.sem], 1)
                return run

            for engname in self.ENGS:
                getattr(block, handles[engname])(replay(engname))


class Ring:
    def __init__(self, items):
        self.items = items
        self.i = 0

    def next(self):
        it = self.items[self.i % len(self.items)]
        self.i += 1
        return it


class Grid:
    def __init__(self, name):
        self.name = name
        self.d = {}

    def __call__(self, *key):
        b = self.d.get(key)
        if b is None:
            b = Buf(self.name)
            self.d[key] = b
        return b

    def reset(self):
        self.d = {}


PV_GMIX0, PV_GMLP0, PV_GMIX1, PV_GMLP1, PV_GFIN, PV_GMEM0, PV_GMEM1 = 0, 16, 32, 48, 64, 80, 96
PV_CB, PV_LNG, PV_LNB, PV_CW = 112, 124, 136, 148
PV_N = 148 + 31 * 12


def build_nc():
    nc = bass.Bass("TRN2", target_bir_lowering=False)

    def din(name, shape):
        if KSTOP == 1 and name in MINI_SKIP:
            return nc.dram_tensor(name, list(shape), F32).ap()
        return nc.dram_tensor(name, list(shape), F32, kind="ExternalInput").ap()

    def dout(name, shape):
        return nc.dram_tensor(name, list(shape), F32, kind="ExternalOutput").ap()

    xp = din("xp", [2, 2048, 1056]); xs = din("xs", [2, 2048, 16]); memT = din("memT", [2048, 256])
    pvec_d = din("pvec", [128, PV_N]); bfox_d = din("bfox", [1, 12])
    vmask_d = din("vmask", [1, 48]); smask_d = din("smask", [1, 48])
    convst = din("convst", [2, 1536, 30])
    cmk = din("cmk", [2, 2, 512, 256]); cmv = din("cmv", [2, 2, 256, 512])
    cfk = din("cfk", [2, 1536, 1024]); cfv = din("cfv", [2, 1024, 1536]); cfl = din("cfl", [2, 1024, 12])
    wf_d = din("wf_d", [128, 192])
    w_mem_k = ("w_mem_k", 0), ("w_mem_k", 1)
    w_mem_v = ("w_mem_v", 0), ("w_mem_v", 1)
    w_in_conv = ("w_in_conv", 0)
    w_in_fox = ("w_in_fox", 0)
    w_out = ("w_out", 0), ("w_out", 1)
    w_up = ("w_up", 0), ("w_up", 1)
    WT = {}

    yp_o = dout("yp_o", [2, 2048, 1024]); ys_o = dout("ys_o", [2, 2048, 16])
    mk_o = dout("mk_o", [2, 512, 256]); mv_o = dout("mv_o", [2, 256, 512])
    convp_o = dout("convp_o", [2, 1536, 30]); convs_o = dout("convs_o", [2, 1536, 30])
    fk_o = dout("fk_o", [2, 1536, 1024]); fv_o = dout("fv_o", [2, 1024, 1536]); fl_o = dout("fl_o", [2, 1024, 12])
    fks_o = dout("fks_o", [2, 1536, 16]); fvs_o = dout("fvs_o", [2, 16, 1536]); fls_o = dout("fls_o", [2, 16, 12])

    KVN = 3072 * 1024
    kst = [[nc.dram_tensor(f"kst{h}_{q}", [512, 1024], BF16).ap() for q in range(3)] for h in range(2)]
    vst = [[nc.dram_tensor(f"vst{h}_{q}", [512, 1024], BF16).ap() for q in range(3)] for h in range(2)]
    kga = [[nc.dram_tensor(f"kga{h}_{q}", [2048, 1024], BF16).ap() for q in range(3)] for h in range(2)]
    vga = [[nc.dram_tensor(f"vga{h}_{q}", [2048, 1024], BF16).ap() for q in range(3)] for h in range(2)]
    lgst = [nc.dram_tensor(f"lgst{h}", [1024, 12], F32).ap() for h in range(2)]
    lgga = [nc.dram_tensor(f"lgga{h}", [4096, 12], F32).ap() for h in range(2)]

    p = Prog(nc)
    es = contextlib.ExitStack()
    with es:
        def sb(name, shape, dt):
            return es.enter_context(nc.sbuf_tensor("s_" + name, list(shape), dt))

        Y = sb("Y", [128, 16, NT], F32)
        XN = sb("XN", [128, 16, NT], BF16)
        QX = sb("QX", [128, 4, NT], BF16)
        NSLOT = 3
        WS = [sb(f"ws{i}", [128, 4096], BF16) for i in range(NSLOT)]
        MK = sb("MK", [128, 2, 4, 256], BF16)
        MV = sb("MV", [128, 2, 2, 512], BF16)
        ones_bf = sb("ones_bf", [128, 128], BF16)
        ones_f = sb("ones_f", [128, 128], F32)
        tri_f = sb("tri_f", [128, 128], F32)
        ident_f = sb("ident_f", [128, 128], F32)
        trimask = sb("trimask", [128, 128], F32)
        pvec = sb("pvec", [128, PV_N], F32)
        bfox = sb("bfox", [128, 12], F32)
        vmask = sb("vmask", [128, 4, 12], F32)
        smask = sb("smask", [128, 4, 12], F32)
        Wf = sb("Wf", [128, 16, 12], BF16)
        SQ = [sb(f"sq{i}", [128, 512], BF16) for i in range(3)]
        R32 = [sb(f"r32_{i}", [128, 512], F32) for i in range(3)]
        NRB = 19840
        NRF = 4224
        REGB = sb("REGB", [128, NRB], BF16)
        REGF = sb("REGF", [128, NRF], F32)
        PSB = [es.enter_context(nc.psum_tensor(f"ps{i}", [128, 512], F32)) for i in range(8)]

        Bconst = Buf("const")
        BY = Grid("Y"); BXN = Grid("XN"); BQX = Grid("QX")
        BWS = [Buf(f"ws{i}") for i in range(NSLOT)]
        BMK = Grid("MK"); BMV = Grid("MV"); SM = {}
        BSQ = [Buf("sq") for _ in SQ]; BR32 = [Buf("r32") for _ in R32]
        BPS = [Buf(f"ps{i}", excl=True) for i in range(8)]
        sq_ring = Ring(list(zip(SQ, BSQ)))
        r32_ring = Ring(list(zip(R32, BR32)))
        ps_main = Ring(list(zip(PSB[0:4], BPS[0:4])))
        ps_aux = Ring(list(zip(PSB[4:6], BPS[4:6])))
        Bstage = [[], []]
        Bgath = [Grid("kga0"), Grid("kga1")]
        Blgst = [Buf("lgst0"), Buf("lgst1")]
        Blgga = [Buf("lgga0"), Buf("lgga1")]
        Bout = Grid("out")

        def OP(eng, name, reads, writes, **kw):
            return p.op(eng, lambda e: getattr(e, name)(**kw), reads, writes)

        def DMA(q, out, in_, owner, reads=(), writes=(), final=False):
            return p.dma(q, lambda e: e.dma_start(out=out, in_=in_), owner, reads, writes, final)

        def MM(out, lhsT, rhs, start, stop, reads, writes):
            return p.op("pe", lambda e: e.matmul(out, lhsT=lhsT, rhs=rhs, start=start, stop=stop), reads, writes)

        def pcol(c):
            return pvec[:, c:c + 1]

        class WStream:
            def __init__(self):
                self.order = []
                self.uniq = {}
                self.pos = 0
                self.issued = 0

            def reset(self):
                self.pos = 0
                self.issued = 0

            def _issue(self, i):
                s_ = i % NSLOT
                tid = self.uniq[self.order[i]]
                DMA("pool", WS[s_][:, :], WT["ap"][tid], BWS[s_], writes=[BWS[s_]])

            def next(self, key):
                if p.dry:
                    self.order.append(key)
                    self.uniq.setdefault(key, len(self.uniq))
                    return WS[0], BWS[0]
                while self.issued < min(len(self.order), self.pos + NSLOT):
                    self._issue(self.issued)
                    self.issued += 1
                s_ = self.pos % NSLOT
                self.pos += 1
                return WS[s_], BWS[s_]

        ws = WStream()

        def wtile_cols(wref, pieces):
            return ("cols", wref[0], wref[1], tuple(pieces)), sum(n for _, n in pieces)

        def rmsnorm(src, dst, gcol0, tiles, nkc=16, denom=2048.0):
            for (off, w) in tiles:
                ss, ssb = ps_aux.next()
                for kc in range(nkc):
                    sq, sqb = sq_ring.next()
                    a, ab = src(kc, off, w)
                    OP("act", "activation", [ab], [sqb], out=sq[:, :w], in_=a, func=AF.Square)
                    MM(ss[:, :w], ones_bf[:], sq[:, :w], kc == 0, kc == nkc - 1, [sqb, Bconst], [ssb])
                rs, rsb = r32_ring.next()
                OP("act", "activation", [ssb], [rsb], out=rs[:, :w], in_=ss[:, :w], func=AF.Sqrt, scale=1.0 / denom, bias=EPS)
                OP("dve", "reciprocal", [rsb], [rsb], out=rs[:, :w], in_=rs[:, :w])
                for kc in range(nkc):
                    a, ab = src(kc, off, w)
                    d, db = dst(kc, off, w)
                    OP("dve", "scalar_tensor_tensor", [ab, rsb, Bconst], [db], out=d, in0=a, scalar=pcol(gcol0 + kc),
                       in1=rs[:, :w], op0=ALU.mult, op1=ALU.mult)

        def srcY(kc, off, w):
            return Y[:, kc, off:off + w], BY(kc, off)

        def dstXN(kc, off, w):
            return XN[:, kc, off:off + w], BXN(kc, off)

        def mem_attend(layer, prompt_tiles=True):
            for h in range(4):
                for (off, w) in TL:
                    smp = off >= 1024
                    pts = []
                    for blk in range(2):
                        sp_, spb = ps_main.next()
                        if smp:
                            lh, lb = SM["MKS"][:, h, blk * 128:(blk + 1) * 128], SM["BMKS"]
                        else:
                            lh, lb = MK[:, layer, h, blk * 128:(blk + 1) * 128], BMK(layer)
                        MM(sp_[:, :w], lh, QX[:, h, off:off + w], True, True, [lb, BQX(h, off)], [spb])
                        pt, ptb = sq_ring.next()
                        OP("act", "activation", [spb], [ptb], out=pt[:, :w], in_=sp_[:, :w], func=AF.Exp, scale=SCALE)
                        pts.append((pt, ptb))
                    o_, ob = ps_aux.next()
                    l_, lb2 = ps_aux.next()
                    for blk in range(2):
                        pt, ptb = pts[blk]
                        if smp:
                            vh, vb = SM["MVS"][:, blk, h * 128:(h + 1) * 128], SM["BMVS"]
                        else:
                            vh, vb = MV[:, layer, blk, h * 128:(h + 1) * 128], BMV(layer)
                        MM(o_[:, :w], vh, pt[:, :w], blk == 0, blk == 1, [vb, ptb], [ob])
                    for blk in range(2):
                        pt, ptb = pts[blk]
                        MM(l_[:, :w], ones_bf[:], pt[:, :w], blk == 0, blk == 1, [Bconst, ptb], [lb2])
                    rc, rcb = r32_ring.next()
                    OP("dve", "reciprocal", [lb2], [rcb], out=rc[:, :w], in_=l_[:, :w])
                    OP("dve", "tensor_tensor", [ob, rcb], [BXN(12 + h, off)], out=XN[:, 12 + h, off:off + w], in0=o_[:, :w],
                       in1=rc[:, :w], op=ALU.mult)

        def load_sample_mem(hf, layer):
            DMA("pool", SM["MKS"], cmk[layer, hf].rearrange("(h d) m -> d h m", d=128), SM["BMKS"], writes=[SM["BMKS"]])
            DMA("pool", SM["MVS"], cmv[layer, hf].rearrange("(b m) c -> m b c", m=128), SM["BMVS"], writes=[SM["BMVS"]])

        def out_proj(layer):
            for wt in range(8):
                desc, _ = wtile_cols(w_out[layer], [(wt * 256, 256)])
                slot, sbuf_ = ws.next(desc)
                sv = slot[:, :].rearrange("p (a b) -> p a b", a=16, b=256)
                for mc in range(2):
                    m = wt * 2 + mc
                    for (off, w) in TL:
                        pb, pbb = ps_main.next()
                        for kc in range(16):
                            MM(pb[:, :w], sv[:, kc, mc * 128:(mc + 1) * 128], XN[:, kc, off:off + w], kc == 0, kc == 15,
                               [sbuf_, BXN(kc, off)], [pbb])
                        OP("dve", "tensor_tensor", [pbb, BY(m, off)], [BY(m, off)], out=Y[:, m, off:off + w], in0=pb[:, :w],
                           in1=Y[:, m, off:off + w], op=ALU.add)

        def mlp(layer, gcol0):
            rmsnorm(srcY, dstXN, gcol0, TL)
            H = REGB[:, 0:4 * NT].rearrange("p (a b) -> p a b", a=4, b=NT)
            BH = Grid("H")
            for hg in range(16):
                for ut in range(2):
                    desc, _ = wtile_cols(w_up[layer], [(hg * 512 + ut * 256, 256)])
                    slot, sbuf_ = ws.next(desc)
                    sv = slot[:, :].rearrange("p (a b) -> p a b", a=16, b=256)
                    for mc in range(2):
                        hc = ut * 2 + mc
                        for (off, w) in TL:
                            pb, pbb = ps_main.next()
                            for kc in range(16):
                                MM(pb[:, :w], sv[:, kc, mc * 128:(mc + 1) * 128], XN[:, kc, off:off + w], kc == 0, kc == 15,
                                   [sbuf_, BXN(kc, off)], [pbb])
                            r, rb = r32_ring.next()
                            OP("act", "activation", [pbb], [rb], out=r[:, :w], in_=pb[:, :w], func=AF.Relu)
                            OP("dve", "tensor_tensor", [rb], [BH(hc, off)], out=H[:, hc, off:off + w], in0=r[:, :w], in1=r[:, :w],
                               op=ALU.mult)
                for dt_ in range(2):
                    slot, sbuf_ = ws.next(("down", layer, hg, dt_))
                    sv = slot[:, :].rearrange("p (a b) -> p a b", a=4, b=1024)
                    for mc in range(8):
                        m = dt_ * 8 + mc
                        for (off, w) in TL:
                            pb, pbb = ps_main.next()
                            for kc in range(4):
                                MM(pb[:, :w], sv[:, kc, mc * 128:(mc + 1) * 128], H[:, kc, off:off + w], kc == 0, kc == 3,
                                   [sbuf_, BH(kc, off)], [pbb])
                            OP("dve", "tensor_tensor", [pbb, BY(m, off)], [BY(m, off)], out=Y[:, m, off:off + w], in0=pb[:, :w],
                               in1=Y[:, m, off:off + w], op=ALU.add)

        def stop_at(n, hf=0, dump=True):
            if KSTOP == n and hf == 0:
                if dump:
                    Bd = Buf("dump")
                    DMA("sp", yp_o[hf].rearrange("(kc p) w -> p kc w", p=128), Y[:, :, 0:1024], Bd,
                        reads=[BY(kc, off) for kc in range(16) for off in (0, 512)], final=True)
                    DMA("sp", ys_o[hf].rearrange("(kc p) w -> p kc w", p=128), Y[:, :, 1024:1040], Bd,
                        reads=[BY(kc, 1024) for kc in range(16)], final=True)
                raise StopEmit()

        def emit():
            ws.reset()
            for g in (BY, BXN, BQX, BMK, BMV, Bout, Bgath[0], Bgath[1]):
                g.reset()
            Bstage[0] = []
            Bstage[1] = []
            OP("pool", "memset", [], [Bconst], ap=ones_bf[:], constant=1.0)
            OP("pool", "memset", [], [Bconst], ap=ones_f[:], constant=1.0)
            OP("pool", "memset", [], [Bconst], ap=tri_f[:], constant=1.0)
            OP("pool", "affine_select", [Bconst], [Bconst], out=tri_f[:], in_=tri_f[:], pattern=[[1, 128]], compare_op=ALU.is_ge,
               fill=0.0, base=0, channel_multiplier=-1)
            OP("pool", "memset", [], [Bconst], ap=ident_f[:], constant=1.0)
            OP("pool", "affine_select", [Bconst], [Bconst], out=ident_f[:], in_=ident_f[:], pattern=[[1, 128]], compare_op=ALU.is_equal,
               fill=0.0, base=0, channel_multiplier=-1)
            OP("pool", "memset", [], [Bconst], ap=trimask[:], constant=0.0)
            OP("pool", "affine_select", [Bconst], [Bconst], out=trimask[:], in_=trimask[:], pattern=[[1, 128]], compare_op=ALU.is_ge,
               fill=NEG, base=0, channel_multiplier=-1)
            Bpv = Buf("pv")
            DMA("sp", pvec[:], pvec_d, Bpv, writes=[Bconst])
            DMA("sp", bfox[:], bfox_d.partition_broadcast(128), Bpv, writes=[Bconst])
            DMA("sp", vmask[:].rearrange("p a b -> p (a b)"), vmask_d.partition_broadcast(128), Bpv, writes=[Bconst])
            DMA("sp", smask[:].rearrange("p a b -> p (a b)"), smask_d.partition_broadcast(128), Bpv, writes=[Bconst])
            DMA("pool", Wf[:].rearrange("p a b -> p (a b)"), wf_d, Bpv, writes=[Bconst])

            if KSUB == 1:
                raise StopEmit()
            DMA("sp", Y[:, :, 0:256], memT.rearrange("(kc p) m -> p kc m", p=128), BY("mem"), writes=[BY(kc, 0) for kc in range(16)])
            for layer in range(2):
                rmsnorm(srcY, dstXN, PV_GMEM0 + 16 * layer, [(0, 256)])
                if KSUB == 2:
                    raise StopEmit()
                for wt in range(2):
                    desc, _ = wtile_cols(w_mem_k[layer], [(wt * 256, 256)])
                    slot, sbuf_ = ws.next(desc)
                    sv = slot[:, :].rearrange("p (a b) -> p a b", a=16, b=256)
                    for mc in range(2):
                        h = wt * 2 + mc
                        pb, pbb = ps_main.next()
                        for kc in range(16):
                            MM(pb[:, :256], sv[:, kc, mc * 128:(mc + 1) * 128], XN[:, kc, 0:256], kc == 0, kc == 15,
                               [sbuf_, BXN(kc, 0)], [pbb])
                        OP("act", "copy", [pbb], [BMK(layer)], out=MK[:, layer, h, :], in_=pb[:, :256])
                        r, rb = r32_ring.next()
                        OP("dve", "tensor_copy", [pbb], [rb], out=r[:, :256], in_=pb[:, :256])
                        DMA("sp", mk_o[layer, h * 128:(h + 1) * 128, :], r[:, :256], rb, reads=[rb], final=True)
                for wt in range(2):
                    desc, _ = wtile_cols(w_mem_v[layer], [(wt * 256, 256)])
                    slot, sbuf_ = ws.next(desc)
                    sv = slot[:, :].rearrange("p (a b) -> p a b", a=16, b=256)
                    for blk in range(2):
                        pb, pbb = ps_main.next()
                        for kc in range(16):
                            MM(pb[:, :256], XN[:, kc, blk * 128:(blk + 1) * 128], sv[:, kc, :], kc == 0, kc == 15,
                               [sbuf_, BXN(kc, 0)], [pbb])
                        OP("act", "copy", [pbb], [BMV(layer)], out=MV[:, layer, blk, wt * 256:(wt + 1) * 256], in_=pb[:, :256])
                        r, rb = r32_ring.next()
                        OP("dve", "tensor_copy", [pbb], [rb], out=r[:, :256], in_=pb[:, :256])
                        DMA("sp", mv_o[layer, blk * 128:(blk + 1) * 128, wt * 256:(wt + 1) * 256], r[:, :256], rb, reads=[rb], final=True)

            stop_at(1, dump=False)
            for hf in range(2):
                emit_half(hf)

        def emit_half(hf):
            p.barrier()
            CB = REGB[:, 0:12 * NT].rearrange("p (a b) -> p a b", a=12, b=NT)
            XHN = REGB[:, 12 * NT:12 * NT + 512].rearrange("p (a b) -> p a b", a=16, b=32)
            UH = [REGF[:, i * 1056:(i + 1) * 1056] for i in range(2)]
            UHS = REGF[:, 2112:2112 + 552].rearrange("p (a b) -> p a b", a=12, b=46)
            ACC = REGF[:, 2664:2664 + 1024]
            ACCS = REGF[:, 3688:3688 + 16]
            XH = REGF[:, 3704:3704 + 512].rearrange("p (a b) -> p a b", a=16, b=32)
            BCB = Grid("CB"); BXHN = Grid("XHN"); BUH = [Buf("uh0"), Buf("uh1")]; BUHS = Grid("UHS")
            BACC_A = Buf("acca"); BACC_B = Buf("accb"); BACC2 = [BACC_A, BACC_B]; BACCS = Buf("accs"); BXH = Grid("XH")
            uh_ring = Ring(list(zip(UH, BUH)))
            SM["MKS"] = REGB[:, 12992:14016].rearrange("p (a b) -> p a b", a=4, b=256)
            SM["MVS"] = REGB[:, 14016:15040].rearrange("p (a b) -> p a b", a=2, b=512)
            SM["BMKS"] = Buf("mks"); SM["BMVS"] = Buf("mvs")

            xv = xp[hf].rearrange("(kc p) w -> p kc w", p=128)
            DMA("sp", XH[:, :, :], xv[:, :, 0:32], BXH("ld"), writes=[BXH(kc) for kc in range(16)])
            for q4 in range(4):
                for (off, w) in TL[:2]:
                    DMA("sp", Y[:, q4 * 4:(q4 + 1) * 4, off:off + w], xv[:, q4 * 4:(q4 + 1) * 4, 32 + off:32 + off + w], BY("ld", q4, off),
                        writes=[BY(kc, off) for kc in range(q4 * 4, q4 * 4 + 4)])
            DMA("sp", Y[:, :, 1024:1040], xs[hf].rearrange("(kc p) w -> p kc w", p=128), BY("lds"),
                writes=[BY(kc, 1024) for kc in range(16)])
            DMA("sp", UHS[:, :, 0:30], convst[hf].rearrange("(m p) w -> p m w", p=128), BUHS("ld"), writes=[BUHS(m) for m in range(12)])
            load_sample_mem(hf, 0)

            rmsnorm(srcY, dstXN, PV_GMIX0, TL)
            rmsnorm(lambda kc, off, w: (XH[:, kc, off:off + w], BXH(kc)), lambda kc, off, w: (XHN[:, kc, off:off + w], BXHN(kc)),
                    PV_GMIX0, [(0, 32)])
            for m in range(12):
                desc, _ = wtile_cols(w_in_conv, [(m * 128, 128), (1536 + m * 128, 128)])
                slot, sbuf_ = ws.next(desc)
                sva = slot[:, 0:2048].rearrange("p (a b) -> p a b", a=16, b=128)
                svg = slot[:, 2048:4096].rearrange("p (a b) -> p a b", a=16, b=128)
                uh, uhb = uh_ring.next()
                for ti in range(4):
                    if ti == 0:
                        w = 32
                        rhs = lambda kc: (XHN[:, kc, 0:32], BXHN(kc))
                        dst, dstb = uh[:, 0:32], uhb
                    else:
                        off, w = TL[ti - 1]
                        rhs = (lambda off, w: (lambda kc: (XN[:, kc, off:off + w], BXN(kc, off))))(off, w)
                        if ti < 3:
                            dst, dstb = uh[:, 32 + off:32 + off + w], uhb
                        else:
                            dst, dstb = UHS[:, m, 30:46], BUHS(m)
                    pa, pab = ps_main.next()
                    pg, pgb = ps_main.next()
                    for kc in range(16):
                        r_, rb_ = rhs(kc)
                        MM(pa[:, :w], sva[:, kc, :], r_, kc == 0, kc == 15, [sbuf_, rb_], [pab])
                    for kc in range(16):
                        r_, rb_ = rhs(kc)
                        MM(pg[:, :w], svg[:, kc, :], r_, kc == 0, kc == 15, [sbuf_, rb_], [pgb])
                    sg, sgb = r32_ring.next()
                    OP("act", "activation", [pgb], [sgb], out=sg[:, :w], in_=pg[:, :w], func=AF.Sigmoid)
                    OP("dve", "tensor_tensor", [pab, sgb], [dstb], out=dst, in0=pa[:, :w], in1=sg[:, :w], op=ALU.mult)
                cwc = lambda wi: pcol(PV_CW + wi * 12 + m)
                CS = 768
                TMP = XH.rearrange("p a b -> p (a b)")[:, 0:1024 - CS]
                BXHall = [BXH(kc) for kc in range(16)]
                chains = [("dve", ACC[:, 0:CS], BACC_A, lambda wi: uh[:, 2 + wi:2 + CS + wi], uhb),
                          ("pool", ACC[:, CS:1024], BACC_B, lambda wi: uh[:, 2 + CS + wi:1026 + wi], uhb),
                          ("dve", ACCS, BACCS, lambda wi: UHS[:, m, wi:wi + 16], BUHS(m))]
                for wi in range(31):
                    for (eng_, acc_, accb_, src_, srcb_) in chains:
                        if wi == 0:
                            OP(eng_, "tensor_scalar", [srcb_, Bconst], [accb_], out=acc_, in0=src_(0), scalar1=cwc(0), scalar2=pcol(PV_CB + m),
                               op0=ALU.mult, op1=ALU.add)
                        elif eng_ == "dve":
                            OP(eng_, "scalar_tensor_tensor", [srcb_, accb_, Bconst], [accb_], out=acc_, in0=src_(wi), scalar=cwc(wi),
                               in1=acc_, op0=ALU.mult, op1=ALU.add)
                        else:
                            OP(eng_, "tensor_scalar", [srcb_, Bconst], BXHall, out=TMP, in0=src_(wi), scalar1=cwc(wi), scalar2=0.0,
                               op0=ALU.mult, op1=ALU.add)
                            OP(eng_, "tensor_tensor", [accb_] + BXHall, [accb_], out=acc_, in0=acc_, in1=TMP, op=ALU.add)
                OP("act", "copy", [BACC_A], [BCB(m, 0)], out=CB[:, m, 0:CS], in_=ACC[:, 0:CS])
                OP("act", "copy", [BACC_B], [BCB(m, 0)], out=CB[:, m, CS:1024], in_=ACC[:, CS:1024])
                DMA("sp", convp_o[hf, m * 128:(m + 1) * 128, :], uh[:, 1026:1056], uhb, reads=[uhb], final=True)
                OP("act", "copy", [BACCS], [BCB(m, 1024)], out=CB[:, m, 1024:1040], in_=ACCS)
            DMA("sp", convs_o[hf].rearrange("(m p) w -> p m w", p=128), UHS[:, :, 16:46], BUHS("st"),
                reads=[BUHS(m) for m in range(12)], final=True)
            for wt in range(2):
                desc, _ = wtile_cols(w_in_conv, [(3072 + wt * 256, 256)])
                slot, sbuf_ = ws.next(desc)
                sv = slot[:, :].rearrange("p (a b) -> p a b", a=16, b=256)
                for mc in range(2):
                    h = wt * 2 + mc
                    for (off, w) in TL:
                        pb, pbb = ps_main.next()
                        for kc in range(16):
                            MM(pb[:, :w], sv[:, kc, mc * 128:(mc + 1) * 128], XN[:, kc, off:off + w], kc == 0, kc == 15,
                               [sbuf_, BXN(kc, off)], [pbb])
                        OP("act", "copy", [pbb], [BQX(h, off)], out=QX[:, h, off:off + w], in_=pb[:, :w])
            stop_at(2, hf)
            MEANB = ACC[:, 0:512]
            RSTDB = ACC[:, 512:1024]
            for (off, w) in TL:
                cboff = 0 if off < 1024 else 1024
                s_, sb_ = ps_aux.next()
                q_, qb_ = ps_aux.next()
                for m in range(12):
                    MM(s_[:, :w], ones_bf[:], CB[:, m, off:off + w], m == 0, m == 11, [Bconst, BCB(m, cboff)], [sb_])
                for m in range(12):
                    sq, sqb = sq_ring.next()
                    OP("act", "activation", [BCB(m, cboff)], [sqb], out=sq[:, :w], in_=CB[:, m, off:off + w], func=AF.Square)
                    MM(q_[:, :w], ones_bf[:], sq[:, :w], m == 0, m == 11, [Bconst, sqb], [qb_])
                OP("dve", "tensor_scalar", [sb_, *BACC2], [*BACC2], out=MEANB[:, :w], in0=s_[:, :w], scalar1=1.0 / 1536, scalar2=0.0,
                   op0=ALU.mult, op1=ALU.add)
                t_, tb_ = r32_ring.next()
                OP("dve", "tensor_tensor", [*BACC2], [tb_], out=t_[:, :w], in0=MEANB[:, :w], in1=MEANB[:, :w], op=ALU.mult)
                OP("dve", "scalar_tensor_tensor", [qb_, tb_, *BACC2], [*BACC2], out=RSTDB[:, :w], in0=q_[:, :w], scalar=1.0 / 1536, in1=t_[:, :w],
                   op0=ALU.mult, op1=ALU.subtract)
                OP("act", "activation", [*BACC2], [*BACC2], out=RSTDB[:, :w], in_=RSTDB[:, :w], func=AF.Sqrt, bias=EPS, scale=1.0)
                OP("dve", "reciprocal", [*BACC2], [*BACC2], out=RSTDB[:, :w], in_=RSTDB[:, :w])
                for m in range(12):
                    t_, tb_ = r32_ring.next()
                    OP("dve", "tensor_tensor", [BCB(m, cboff), *BACC2], [tb_], out=t_[:, :w], in0=CB[:, m, off:off + w], in1=MEANB[:, :w],
                       op=ALU.subtract)
                    OP("dve", "tensor_tensor", [tb_, *BACC2], [tb_], out=t_[:, :w], in0=t_[:, :w], in1=RSTDB[:, :w], op=ALU.mult)
                    OP("act", "activation", [tb_, Bconst], [BXN(m, off)], out=XN[:, m, off:off + w], in_=t_[:, :w], func=AF.Silu,
                       scale=pcol(PV_LNG + m), bias=pcol(PV_LNB + m))
            mem_attend(0)
            out_proj(0)
            stop_at(3, hf)
            p.barrier()
            mlp(0, PV_GMLP0)
            stop_at(4, hf)

            p.barrier()
            QT = REGB[:, 0:12 * NT].rearrange("p (a b) -> p a b", a=12, b=NT)
            o_ = 12 * NT
            KT = [REGB[:, o_ + i * 1024:o_ + (i + 1) * 1024] for i in range(2)]; o_ += 2048
            VT = [REGB[:, o_ + i * 1024:o_ + (i + 1) * 1024].rearrange("p (a b) -> p a b", a=8, b=128) for i in range(2)]; o_ += 2048
            PT = [REGB[:, o_ + i * 512:o_ + (i + 1) * 512] for i in range(3)]; o_ += 1536
            KSN = REGB[:, o_:o_ + 192].rearrange("p (a b) -> p a b", a=12, b=16); o_ += 192
            VSN = REGB[:, o_:o_ + 1536]; o_ += 1536
            assert o_ <= NRB, o_
            f_ = 0
            CQ = [REGF[:, f_ + i * 512:f_ + (i + 1) * 512] for i in range(2)]; f_ += 1024
            NCH = 4 if hf == 0 else 8
            LCS = REGF[:, f_:f_ + NCH * 96].rearrange("p (c t h) -> p c t h", c=NCH, t=8, h=12); f_ += 768
            OFF = REGF[:, f_:f_ + NCH * 96].rearrange("p (c t h) -> p c t h", c=NCH, t=8, h=12); f_ += 768
            TOT = REGF[:, f_:f_ + NCH * 12].rearrange("p (c h) -> p c h", c=NCH, h=12); f_ += 96
            GS = REGF[:, f_:f_ + 96].rearrange("p (c h) -> p c h", c=8, h=12); f_ += 96
            BASE = REGF[:, f_:f_ + 192].rearrange("p (c q h) -> p c q h", c=8, q=2, h=12); f_ += 192
            LGO = REGF[:, f_:f_ + 96].rearrange("p (t h) -> p t h", t=8, h=12); f_ += 96
            OFFO = REGF[:, f_:f_ + 108].rearrange("p (t h) -> p t h", t=9, h=12); f_ += 108
            CQREL = REGF[:, f_:f_ + 96].rearrange("p (t h) -> p t h", t=8, h=12); f_ += 96
            KBO = REGF[:, f_:f_ + 192].rearrange("p (q t h) -> p q t h", q=2, t=8, h=12); f_ += 192
            LGS = REGF[:, f_:f_ + 12]; f_ += 12
            LCSS = REGF[:, f_:f_ + 12]; f_ += 12
            NLCSS = REGF[:, f_:f_ + 12]; f_ += 12
            FB = REGF[:, f_:f_ + 12]; f_ += 12
            KBT = [REGF[:, f_ + i * 16:f_ + (i + 1) * 16].rearrange("p (q t) -> p q t", q=2, t=8) for i in range(2)]; f_ += 32
            DG = [REGF[:, f_ + i * 128:f_ + (i + 1) * 128] for i in range(2)]; f_ += 256
            VTMP = REGF[:, f_:f_ + 48].rearrange("p (c h) -> p c h", c=4, h=12); f_ += 48
            assert f_ <= NRF, f_
            BQT = Grid("QT"); BKT = [Buf("kt0"), Buf("kt1")]; BVT = [Buf("vt0"), Buf("vt1")]
            BPT = [Buf("pt") for _ in PT]; BCQ = [Buf("cq0"), Buf("cq1")]
            BL = Buf("logf"); BKSN = Buf("ksn"); BVSN = Buf("vsn"); BKBT = [Buf("kbt0"), Buf("kbt1")]; BDG = [Buf("dg0"), Buf("dg1")]
            kt_ring = Ring(list(zip(KT, BKT, VT, BVT)))
            pt_ring = Ring(list(zip(PT, BPT)))
            kbt_ring = Ring(list(zip(KBT, BKBT)))
            dg_ring = Ring(list(zip(DG, BDG)))
            ps_s = Ring(list(zip(PSB[0:3], BPS[0:3])))

            rmsnorm(srcY, dstXN, PV_GMIX1, TL)
            for wt in range(6):
                desc, _ = wtile_cols(w_in_fox, [(wt * 256, 256)])
                slot, sbuf_ = ws.next(desc)
                sv = slot[:, :].rearrange("p (a b) -> p a b", a=16, b=256)
                for mc in range(2):
                    h = wt * 2 + mc
                    for (off, w) in TL:
                        pb, pbb = ps_main.next()
                        for kc in range(16):
                            MM(pb[:, :w], sv[:, kc, mc * 128:(mc + 1) * 128], XN[:, kc, off:off + w], kc == 0, kc == 15,
                               [sbuf_, BXN(kc, off)], [pbb])
                        OP("act", "copy", [pbb], [BQT(h, off)], out=QT[:, h, off:off + w], in_=pb[:, :w])
            for wt in range(6):
                desc, _ = wtile_cols(w_in_fox, [(1536 + wt * 256, 256)])
                slot, sbuf_ = ws.next(desc)
                sv = slot[:, :].rearrange("p (a b) -> p a b", a=16, b=256)
                for mc in range(2):
                    h = wt * 2 + mc
                    for (off, w) in TL:
                        pb, pbb = ps_main.next()
                        for kc in range(16):
                            MM(pb[:, :w], sv[:, kc, mc * 128:(mc + 1) * 128], XN[:, kc, off:off + w], kc == 0, kc == 15,
                               [sbuf_, BXN(kc, off)], [pbb])
                        r, rb = r32_ring.next()
                        OP("dve", "tensor_copy", [pbb], [rb], out=r[:, :w], in_=pb[:, :w])
                        if off < 1024:
                            DMA("sp", fk_o[hf, h * 128:(h + 1) * 128, off:off + w], r[:, :w], rb, reads=[rb], final=True)
                            kb_, kbb = sq_ring.next()
                            OP("act", "copy", [pbb], [kbb], out=kb_[:, :w], in_=pb[:, :w])
                            Bstage[hf].append(Buf("stw"))
                            DMA("sp", kst[hf][h // 4][(h % 4) * 128:(h % 4 + 1) * 128, off:off + w], kb_[:, :w], kbb, reads=[kbb], writes=[Bstage[hf][-1]])
                        else:
                            DMA("sp", fks_o[hf, h * 128:(h + 1) * 128, :], r[:, :w], rb, reads=[rb], final=True)
                            OP("act", "copy", [pbb], [BKSN], out=KSN[:, h, :], in_=pb[:, :w])
            for wt in range(6):
                desc, _ = wtile_cols(w_in_fox, [(3072 + wt * 256, 256)])
                slot, sbuf_ = ws.next(desc)
                sv = slot[:, :].rearrange("p (a b) -> p a b", a=16, b=256)
                for tb in range(9):
                    off, mm = (tb * 128, 128) if tb < 8 else (1024, 16)
                    toff = 0 if off < 512 else (512 if off < 1024 else 1024)
                    pb, pbb = ps_main.next()
                    for kc in range(16):
                        MM(pb[0:mm, 0:256], XN[:, kc, off:off + mm], sv[:, kc, :], kc == 0, kc == 15, [sbuf_, BXN(kc, toff)], [pbb])
                    r, rb = r32_ring.next()
                    OP("dve", "tensor_copy", [pbb], [rb], out=r[0:mm, 0:256], in_=pb[0:mm, 0:256])
                    if tb < 8:
                        DMA("sp", fv_o[hf, off:off + 128, wt * 256:(wt + 1) * 256], r[:, 0:256], rb, reads=[rb], final=True)
                        vb_, vbb = sq_ring.next()
                        OP("act", "copy", [pbb], [vbb], out=vb_[:, 0:256], in_=pb[:, 0:256])
                        Bstage[hf].append(Buf("stw"))
                        DMA("sp", vview(vst[hf][wt // 2], 0)[off:off + 128, (wt % 2) * 256:(wt % 2 + 1) * 256], vb_[:, 0:256], vbb, reads=[vbb],
                            writes=[Bstage[hf][-1]])
                    else:
                        DMA("sp", fvs_o[hf, :, wt * 256:(wt + 1) * 256], r[0:16, 0:256], rb, reads=[rb], final=True)
                        OP("act", "copy", [pbb], [BVSN], out=VSN[0:16, wt * 256:(wt + 1) * 256], in_=pb[0:16, 0:256])
            for tb in range(9):
                off, mm = (tb * 128, 128) if tb < 8 else (1024, 16)
                toff = 0 if off < 512 else (512 if off < 1024 else 1024)
                pb, pbb = ps_aux.next()
                for kc in range(16):
                    MM(pb[0:mm, 0:12], XN[:, kc, off:off + mm], Wf[:, kc, :], kc == 0, kc == 15, [Bconst, BXN(kc, toff)], [pbb])
                dst = LGO[:, tb, :] if tb < 8 else LGS[0:16, :]
                OP("dve", "tensor_tensor", [pbb, Bconst], [BL], out=FB[0:mm, :], in0=pb[0:mm, 0:12], in1=bfox[0:mm, :], op=ALU.add)
                OP("act", "activation", [BL], [BL], out=FB[0:mm, :], in_=FB[0:mm, :], func=AF.Exp, scale=-1.0)
                OP("act", "activation", [BL], [BL], out=FB[0:mm, :], in_=FB[0:mm, :], func=AF.Ln, bias=1.0, scale=1.0)
                OP("dve", "tensor_scalar", [BL], [BL], out=dst if tb < 8 else LGS[0:16, :], in0=FB[0:mm, :], scalar1=-1.0, scalar2=0.0,
                   op0=ALU.mult, op1=ALU.add)
            Bl2 = Buf("lgo_st")
            DMA("sp", fl_o[hf].rearrange("(t p) h -> p t h", p=128), LGO, Bl2, reads=[BL], final=True)
            DMA("sp", fls_o[hf], LGS[0:16, :], Bl2, reads=[BL], final=True)
            DMA("sp", lgst[hf].rearrange("(t p) h -> p t h", p=128), LGO, Bl2, reads=[BL], writes=[Blgst[hf]])
            for wt in range(2):
                desc, _ = wtile_cols(w_in_fox, [(4620 + wt * 256, 256)])
                slot, sbuf_ = ws.next(desc)
                sv = slot[:, :].rearrange("p (a b) -> p a b", a=16, b=256)
                for mc in range(2):
                    h = wt * 2 + mc
                    for (off, w) in TL:
                        pb, pbb = ps_main.next()
                        for kc in range(16):
                            MM(pb[:, :w], sv[:, kc, mc * 128:(mc + 1) * 128], XN[:, kc, off:off + w], kc == 0, kc == 15,
                               [sbuf_, BXN(kc, off)], [pbb])
                        OP("act", "copy", [pbb], [BQX(h, off)], out=QX[:, h, off:off + w], in_=pb[:, :w])
            for q in range(3 if KNOCC != 1 else 0):
                for kv_, (a_, b_) in enumerate(((kst[hf][q], kga[hf][q]), (vst[hf][q], vga[hf][q]))):
                    rec_ = p.op("pool", (lambda a_, b_: lambda e: e.collective_compute("AllGather", ALU.bypass, replica_groups=[[0, 1, 2, 3], [4, 5, 6, 7]],
                                                                                        ins=[a_.opt()], outs=[b_.opt()]))(a_, b_), list(Bstage[hf]), [Bgath[hf](q, kv_)])
                    if rec_ is not None:
                        rec_.flag = True
            if KNOCC == 0:
                rec_ = p.op("pool", lambda e: e.collective_compute("AllGather", ALU.bypass, replica_groups=[[0, 1, 2, 3], [4, 5, 6, 7]],
                                                                   ins=[lgst[hf].opt()], outs=[lgga[hf].opt()]), [Blgst[hf]], [Blgga[hf]])
                if rec_ is not None:
                    rec_.flag = True

            stop_at(5, hf)
            if hf == 0:
                srcs = [(0, r) for r in range(3)]
            else:
                srcs = [(0, r) for r in range(4)] + [(1, r) for r in range(3)]
            nsrc = len(srcs)
            LG = LCS
            Bld = Buf("lgld")
            for ci, (g, r) in enumerate(srcs):
                DMA("sp", LG[:, ci, :, :], lgga[g][r * 1024:(r + 1) * 1024, :].rearrange("(t p) h -> p t h", p=128), Bld,
                    reads=[Blgga[g]], writes=[BL])
            DMA("sp", LG[:, nsrc, :, :], cfl[hf].rearrange("(t p) h -> p t h", p=128), Bld, writes=[BL])
            nch = nsrc + 1
            for c0 in range(0, nch, 4):
                c1 = min(nch, c0 + 4)
                ncol = (c1 - c0) * 96
                w_, wb_ = ps_aux.next()
                t_, tb_ = ps_aux.next()
                lgv = LG[:, c0:c1, :, :].rearrange("p c t h -> p (c t h)")
                MM(w_[:, :ncol], tri_f[:], lgv, True, True, [Bconst, BL], [wb_])
                MM(t_[:, :ncol], ones_f[:], lgv, True, True, [Bconst, BL], [tb_])
                tv = t_[:, :ncol].rearrange("p (c t h) -> p c t h", c=c1 - c0, t=8, h=12)
                OP("dve", "memset", [], [BL], ap=OFF[:, c0:c1, 0, :], constant=0.0)
                for tl in range(7):
                    OP("dve", "tensor_tensor", [tb_, BL], [BL], out=OFF[:, c0:c1, tl + 1, :], in0=tv[:, :, tl, :], in1=OFF[:, c0:c1, tl, :],
                       op=ALU.add)
                OP("dve", "tensor_tensor", [tb_, BL], [BL], out=TOT[:, c0:c1, :], in0=tv[:, :, 7, :], in1=OFF[:, c0:c1, 7, :], op=ALU.add)
                OP("dve", "tensor_tensor", [wb_, BL], [BL], out=lgv, in0=w_[:, :ncol],
                   in1=OFF[:, c0:c1, :, :].rearrange("p c t h -> p (c t h)"), op=ALU.add)
            w_, wb_ = ps_aux.next()
            t_, tb_ = ps_aux.next()
            lgov = LGO.rearrange("p t h -> p (t h)")
            MM(w_[:, :96], tri_f[:], lgov, True, True, [Bconst, BL], [wb_])
            MM(t_[:, :96], ones_f[:], lgov, True, True, [Bconst, BL], [tb_])
            tv = t_[:, :96].rearrange("p (t h) -> p t h", t=8, h=12)
            OP("dve", "memset", [], [BL], ap=OFFO[:, 0, :], constant=0.0)
            for tl in range(8):
                OP("dve", "tensor_tensor", [tb_, BL], [BL], out=OFFO[:, tl + 1, :], in0=tv[:, tl, :], in1=OFFO[:, tl, :], op=ALU.add)
            OP("dve", "tensor_tensor", [wb_, BL], [BL], out=lgov, in0=w_[:, :96], in1=OFFO[:, 0:8, :].rearrange("p t h -> p (t h)"),
               op=ALU.add)
            LCO = LGO
            w_, wb_ = ps_aux.next()
            MM(w_[0:16, 0:12], tri_f[0:16, 0:16], LGS[0:16, :], True, True, [Bconst, BL], [wb_])
            OP("dve", "tensor_copy", [wb_], [BL], out=LCSS[0:16, :], in_=w_[0:16, 0:12])
            OP("dve", "tensor_scalar", [BL], [BL], out=NLCSS[0:16, :], in0=LCSS[0:16, :], scalar1=-1.0, scalar2=0.0, op0=ALU.mult, op1=ALU.add)
            def vt(ci, r):
                OP("dve", "tensor_tensor", [BL, Bconst], [BL], out=VTMP[:, r, :], in0=TOT[:, ci, :], in1=vmask[:, r, :], op=ALU.mult)
                return VTMP[:, r, :]
            last = None
            for ci in range(nsrc - 1, -1, -1):
                g, r = srcs[ci]
                masked = (hf == 0) or (g == 1)
                term = vt(ci, r) if masked else TOT[:, ci, :]
                if last is None:
                    OP("dve", "tensor_copy", [BL], [BL], out=GS[:, ci, :], in_=term)
                else:
                    OP("dve", "tensor_tensor", [BL], [BL], out=GS[:, ci, :], in0=term, in1=GS[:, last, :], op=ALU.add)
                last = ci
            for ci in range(nsrc):
                g, r = srcs[ci]
                masked = (hf == 0) or (g == 1)
                for qt in range(2):
                    OP("dve", "tensor_tensor", [BL], [BL], out=BASE[:, ci, qt, :], in0=GS[:, ci, :], in1=OFFO[:, 4 * qt, :], op=ALU.add)
                    if masked:
                        OP("dve", "tensor_tensor", [BL, Bconst], [BL], out=BASE[:, ci, qt, :], in0=BASE[:, ci, qt, :], in1=smask[:, r, :],
                           op=ALU.add)
            for qt in range(2):
                OP("dve", "tensor_tensor", [BL], [BL], out=KBO[:, qt, :, :], in0=OFFO[:, 4 * qt, :].unsqueeze(1).broadcast_to([128, 8, 12]),
                   in1=LCO, op=ALU.subtract)
                OP("dve", "tensor_tensor", [BL], [BL], out=CQREL[:, 4 * qt:4 * qt + 4, :], in0=LCO[:, 4 * qt:4 * qt + 4, :],
                   in1=OFFO[:, 4 * qt, :].unsqueeze(1).broadcast_to([128, 4, 12]), op=ALU.subtract)

            stop_at(6, hf)
            OPS = [(PSB[4], BPS[4]), (PSB[5], BPS[5])]
            LPS = [(PSB[6], BPS[6]), (PSB[7], BPS[7])]
            AUXP = (PSB[3], BPS[3])

            for h in range(12):
                for qt in range(2):
                    cqp, cqb = AUXP
                    for tt in range(4):
                        dg, dgb = dg_ring.next()
                        OP("dve", "tensor_scalar", [Bconst, BL], [dgb], out=dg, in0=ident_f[:], scalar1=CQREL[:, 4 * qt + tt, h:h + 1],
                           scalar2=0.0, op0=ALU.mult, op1=ALU.add)
                        MM(cqp[:, tt * 128:(tt + 1) * 128], ones_f[:], dg, True, True, [Bconst, dgb], [cqb])
                    OP("act", "activation", [cqb], [BCQ[qt]], out=CQ[qt], in_=cqp[:, :], func=AF.Copy)
                first = [True, True]

                pending = []

                def flush(keep=0):
                    while len(pending) > keep:
                        pending.pop(0)()

                def tile_step(kt_ap, ktb, vt_ap, vtb, tl, qt, bias_ap, bias_bufs, c0, diag, lastflag):
                    sp_, spb = ps_s.next()
                    MM(sp_[:, c0:512], kt_ap[:, tl * 128:(tl + 1) * 128], QT[:, h, qt * 512 + c0:(qt + 1) * 512], True, True,
                       [ktb, BQT(h, qt * 512)], [spb])
                    OP("dve", "scalar_tensor_tensor", [spb, BCQ[qt]], [spb], out=sp_[:, c0:512], in0=sp_[:, c0:512], scalar=SCALE,
                       in1=CQ[qt][:, c0:512], op0=ALU.mult, op1=ALU.add)
                    if diag:
                        OP("dve", "tensor_tensor", [spb, Bconst], [spb], out=sp_[:, c0:c0 + 128], in0=sp_[:, c0:c0 + 128], in1=trimask[:],
                           op=ALU.add)
                    pt, ptb = pt_ring.next()
                    OP("act", "activation", [spb] + bias_bufs, [ptb], out=pt[:, c0:512], in_=sp_[:, c0:512], func=AF.Exp, bias=bias_ap,
                       scale=1.0)

                    def back():
                        o_, ob = OPS[qt]
                        l_, lb = LPS[qt]
                        MM(o_[:, c0:512], vt_ap[:, tl, :], pt[:, c0:512], first[qt], lastflag, [vtb, ptb], [ob])
                        MM(l_[:, c0:512], ones_bf[:], pt[:, c0:512], first[qt], lastflag, [Bconst, ptb], [lb])
                        first[qt] = False
                    pending.append(back)
                    flush(keep=2)

                for ci, (g, r) in enumerate(srcs):
                    kt_ap, ktb, vt_ap, vtb = kt_ring.next()
                    DMA("sp", kt_ap, kga[g][h // 4][r * 512 + (h % 4) * 128:r * 512 + (h % 4 + 1) * 128, :], ktb, reads=[Bgath[g](h // 4, 0)], writes=[ktb])
                    DMA("sp", vt_ap, vview(vga[g][h // 4], r)[:, (h % 4) * 128:(h % 4 + 1) * 128].rearrange("(t p) d -> p t d", p=128), vtb,
                        reads=[Bgath[g](h // 4, 1)], writes=[vtb])
                    kbt, kbtb = kbt_ring.next()
                    for qt in range(2):
                        OP("dve", "tensor_scalar", [BL], [kbtb], out=kbt[:, qt, :], in0=LCS[:, ci, :, h], scalar1=-1.0,
                           scalar2=BASE[:, ci, qt, h:h + 1], op0=ALU.mult, op1=ALU.add)
                    for tl in range(8):
                        for qt in range(2):
                            tile_step(kt_ap, ktb, vt_ap, vtb, tl, qt, kbt[:, qt, tl:tl + 1], [kbtb], 0, False, False)
                kt_ap, ktb, vt_ap, vtb = kt_ring.next()
                DMA("sp", kt_ap, kst[hf][h // 4][(h % 4) * 128:(h % 4 + 1) * 128, :], ktb, reads=[Bgath[hf](h // 4, 0)], writes=[ktb])
                DMA("sp", vt_ap, vview(vst[hf][h // 4], 0)[:, (h % 4) * 128:(h % 4 + 1) * 128].rearrange("(t p) d -> p t d", p=128), vtb,
                    reads=[Bgath[hf](h // 4, 1)], writes=[vtb])
                for qt in range(2):
                    for tl in range(4 * qt):
                        tile_step(kt_ap, ktb, vt_ap, vtb, tl, qt, KBO[:, qt, tl, h:h + 1], [BL], 0, False, False)
                    for j in range(4):
                        tl = 4 * qt + j
                        tile_step(kt_ap, ktb, vt_ap, vtb, tl, qt, KBO[:, qt, tl, h:h + 1], [BL], 128 * j, True, j == 3)
                    flush()
                    o_, ob = OPS[qt]
                    l_, lb = LPS[qt]
                    rc, rcb = r32_ring.next()
                    OP("dve", "reciprocal", [lb], [rcb], out=rc[:, :], in_=l_[:, :])
                    OP("dve", "tensor_tensor", [ob, rcb], [BXN(h, qt * 512)], out=XN[:, h, qt * 512:(qt + 1) * 512], in0=o_[:, :],
                       in1=rc[:, :], op=ALU.mult)

            stop_at(7, hf)
            for h in range(12):
                cqp, cqb = AUXP
                dg, dgb = dg_ring.next()
                OP("dve", "tensor_scalar", [Bconst, BL], [dgb], out=dg[0:16, 0:16], in0=ident_f[0:16, 0:16], scalar1=LCSS[0:16, h:h + 1],
                   scalar2=0.0, op0=ALU.mult, op1=ALU.add)
                MM(cqp[:, 0:16], ones_f[0:16, :], dg[0:16, 0:16], True, True, [Bconst, dgb], [cqb])
                OP("act", "activation", [cqb], [BCQ[0]], out=CQ[0][:, 0:16], in_=cqp[:, 0:16], func=AF.Copy)
                kt_ap, ktb, vt_ap, vtb = kt_ring.next()
                DMA("pool", kt_ap, cfk[hf, h * 128:(h + 1) * 128, :], ktb, writes=[ktb])
                DMA("pool", vt_ap, cfv[hf][:, h * 128:(h + 1) * 128].rearrange("(t p) d -> p t d", p=128), vtb, writes=[vtb])
                kbt, kbtb = kbt_ring.next()
                OP("dve", "tensor_scalar", [BL], [kbtb], out=kbt[:, 0, :], in0=LCS[:, nsrc, :, h], scalar1=-1.0,
                   scalar2=TOT[:, nsrc, h:h + 1], op0=ALU.mult, op1=ALU.add)
                o_, ob = OPS[0]
                l_, lb = LPS[0]
                pend = []
                for tl in range(9):
                    kk = 128 if tl < 8 else 16
                    sp_, spb = ps_s.next()
                    if tl < 8:
                        MM(sp_[:, 0:16], kt_ap[:, tl * 128:(tl + 1) * 128], QT[:, h, 1024:1040], True, True, [ktb, BQT(h, 1024)], [spb])
                    else:
                        MM(sp_[0:16, 0:16], KSN[:, h, :], QT[:, h, 1024:1040], True, True, [BKSN, BQT(h, 1024)], [spb])
                    OP("dve", "scalar_tensor_tensor", [spb, BCQ[0]], [spb], out=sp_[0:kk, 0:16], in0=sp_[0:kk, 0:16], scalar=SCALE,
                       in1=CQ[0][0:kk, 0:16], op0=ALU.mult, op1=ALU.add)
                    if tl == 8:
                        OP("dve", "tensor_tensor", [spb, Bconst], [spb], out=sp_[0:16, 0:16], in0=sp_[0:16, 0:16], in1=trimask[0:16, 0:16],
                           op=ALU.add)
                    pt, ptb = pt_ring.next()
                    bias_ap = kbt[:, 0, tl:tl + 1] if tl < 8 else NLCSS[0:16, h:h + 1]
                    OP("act", "activation", [spb, kbtb, BL], [ptb], out=pt[0:kk, 0:16], in_=sp_[0:kk, 0:16], func=AF.Exp, bias=bias_ap,
                       scale=1.0)

                    def back(tl=tl, pt=pt, ptb=ptb):
                        if tl < 8:
                            MM(o_[:, 0:16], vt_ap[:, tl, :], pt[:, 0:16], tl == 0, False, [vtb, ptb], [ob])
                            MM(l_[:, 0:16], ones_bf[:], pt[:, 0:16], tl == 0, False, [Bconst, ptb], [lb])
                        else:
                            MM(o_[:, 0:16], VSN[0:16, h * 128:(h + 1) * 128], pt[0:16, 0:16], False, True, [BVSN, ptb], [ob])
                            MM(l_[:, 0:16], ones_bf[0:16, :], pt[0:16, 0:16], False, True, [Bconst, ptb], [lb])
                    pend.append(back)
                    while len(pend) > 2:
                        pend.pop(0)()
                while pend:
                    pend.pop(0)()
                rc, rcb = r32_ring.next()
                OP("dve", "reciprocal", [lb], [rcb], out=rc[:, 0:16], in_=l_[:, 0:16])
                OP("dve", "tensor_tensor", [ob, rcb], [BXN(h, 1024)], out=XN[:, h, 1024:1040], in0=o_[:, 0:16], in1=rc[:, 0:16], op=ALU.mult)

            stop_at(8, hf)
            SM["MKS"] = KT[0].rearrange("p (a b) -> p a b", a=4, b=256)
            SM["MVS"] = KT[1].rearrange("p (a b) -> p a b", a=2, b=512)
            SM["BMKS"] = BKT[0]; SM["BMVS"] = BKT[1]
            load_sample_mem(hf, 1)
            mem_attend(1)
            out_proj(1)
            p.barrier()
            stop_at(9, hf)
            mlp(1, PV_GMLP1)
            stop_at(10, hf)
            for (off, w) in TL:
                ss, ssb = ps_aux.next()
                for kc in range(16):
                    sq, sqb = sq_ring.next()
                    OP("act", "activation", [BY(kc, off)], [sqb], out=sq[:, :w], in_=Y[:, kc, off:off + w], func=AF.Square)
                    MM(ss[:, :w], ones_bf[:], sq[:, :w], kc == 0, kc == 15, [sqb, Bconst], [ssb])
                rs = MEAN_FIN[:, :w]
                OP("act", "activation", [ssb], [BFIN], out=rs, in_=ss[:, :w], func=AF.Sqrt, scale=1.0 / 2048.0, bias=EPS)
                OP("dve", "reciprocal", [BFIN], [BFIN], out=rs, in_=rs)
                for kc in range(16):
                    r, rb = r32_ring.next()
                    OP("dve", "scalar_tensor_tensor", [BY(kc, off), BFIN, Bconst], [rb], out=r[:, :w], in0=Y[:, kc, off:off + w],
                       scalar=pcol(PV_GFIN + kc), in1=rs, op0=ALU.mult, op1=ALU.mult)
                    if off < 1024:
                        DMA("sp", yp_o[hf, kc * 128:(kc + 1) * 128, off:off + w], r[:, :w], rb, reads=[rb], final=True)
                    else:
                        DMA("sp", ys_o[hf, kc * 128:(kc + 1) * 128, :], r[:, :w], rb, reads=[rb], final=True)

        def vview(ap2d, r):
            flat = ap2d[r * 512:(r + 1) * 512, :].rearrange("a b -> (a b)")
            return flat.rearrange("(t c) -> t c", c=512)

        MEAN_FIN = REGF[:, 0:512]
        BFIN = Buf("fin")

        p.dry = True
        try:
            emit()
        except StopEmit:
            pass
        p.dry = False
        wt_t = nc.dram_tensor("wt", [len(ws.uniq), 128, 4096], F32, kind="ExternalInput").ap()
        WT["ap"] = [wt_t[i] for i in range(len(ws.uniq))]
        try:
            emit()
        except StopEmit:
            pass
        p.build()
    nc._w_uniq = list(ws.uniq.keys())
    return nc


_NC = None


def kernel(x_prompt, x_sample, cache_mem_k, cache_mem_v, state_conv, cache_fox_k, cache_fox_v, cache_fox_logf,
           mem_prompt, g_mix, g_mem, w_mem_k, w_mem_v, w_in_conv, conv_w, conv_b, conv_ln_g, conv_ln_b,
           w_in_fox, b_fox_f, w_out, g_mlp, w_up, w_down, g_final):
    global _NC
    f32 = np.float32
    A = lambda a: np.ascontiguousarray(np.asarray(a, dtype=f32))
    x_prompt = A(x_prompt); x_sample = A(x_sample)
    if _NC is None:
        _NC = build_nc()
    nc = _NC

    def cols16(v):
        return np.asarray(v, f32).reshape(16, 128).T

    def cols12(v):
        return np.asarray(v, f32).reshape(12, 128).T

    pv = np.zeros((128, PV_N), f32)
    pv[:, PV_GMIX0:PV_GMIX0 + 16] = cols16(g_mix[0]); pv[:, PV_GMLP0:PV_GMLP0 + 16] = cols16(g_mlp[0])
    pv[:, PV_GMIX1:PV_GMIX1 + 16] = cols16(g_mix[1]); pv[:, PV_GMLP1:PV_GMLP1 + 16] = cols16(g_mlp[1])
    pv[:, PV_GFIN:PV_GFIN + 16] = cols16(g_final)
    pv[:, PV_GMEM0:PV_GMEM0 + 16] = cols16(g_mem[0]); pv[:, PV_GMEM1:PV_GMEM1 + 16] = cols16(g_mem[1])
    pv[:, PV_CB:PV_CB + 12] = cols12(conv_b[0]); pv[:, PV_LNG:PV_LNG + 12] = cols12(conv_ln_g[0]); pv[:, PV_LNB:PV_LNB + 12] = cols12(conv_ln_b[0])
    cw = np.asarray(conv_w[0], f32)
    for wi in range(31):
        pv[:, PV_CW + wi * 12:PV_CW + (wi + 1) * 12] = cols12(cw[wi])

    Wsrc = {"w_mem_k": A(w_mem_k), "w_mem_v": A(w_mem_v), "w_in_conv": A(w_in_conv), "w_in_fox": A(w_in_fox),
            "w_out": A(w_out), "w_up": A(w_up)}
    Wdown = A(w_down)
    uniq = nc._w_uniq
    wt = np.empty((len(uniq), 128, 4096), f32)
    for i, key in enumerate(uniq):
        if key[0] == "cols":
            W = Wsrc[key[1]][key[2]]
            off = 0
            for (c0, n) in key[3]:
                wt[i, :, off:off + 16 * n] = W[:, c0:c0 + n].reshape(16, 128, n).transpose(1, 0, 2).reshape(128, 16 * n)
                off += 16 * n
        else:
            _, layer, hg, dt_ = key
            wt[i] = Wdown[layer][hg * 512:(hg + 1) * 512, dt_ * 1024:(dt_ + 1) * 1024].reshape(4, 128, 1024).transpose(1, 0, 2).reshape(128, 4096)
    wf = Wsrc["w_in_fox"][0][:, 4608:4620].reshape(16, 128, 12).transpose(1, 0, 2).reshape(128, 192)
    shared = dict(pvec=pv, bfox=A(b_fox_f).reshape(1, 12), wt=wt, wf_d=np.ascontiguousarray(wf))
    in_maps = []
    for c in range(8):
        b, j = c // 4, c % 4
        xp = np.zeros((2, 2048, 1056), f32)
        xs = np.zeros((2, 2048, 16), f32)
        cst = np.zeros((2, 1536, 30), f32)
        cmk = np.zeros((2, 2, 512, 256), f32); cmv = np.zeros((2, 2, 256, 512), f32)
        cfk = np.zeros((2, 1536, 1024), f32); cfv = np.zeros((2, 1024, 1536), f32); cfl = np.zeros((2, 1024, 12), f32)
        for hf in range(2):
            ci = 4 * hf + j
            t0 = 1024 * ci
            xp[hf, :, 32:] = x_prompt[b, t0:t0 + 1024, :].T
            if ci > 0:
                xp[hf, :, 0:32] = x_prompt[b, t0 - 32:t0, :].T
            s = 2 * c + hf
            xs[hf] = x_sample[s].T
            cst[hf] = np.asarray(state_conv[0, s], f32).T
            for l in range(2):
                cmk[l, hf] = np.asarray(cache_mem_k[l, s], f32).transpose(1, 2, 0).reshape(512, 256)
                cmv[l, hf] = np.asarray(cache_mem_v[l, s], f32).reshape(256, 512)
            cfk[hf] = np.asarray(cache_fox_k[0, s], f32).transpose(1, 2, 0).reshape(1536, 1024)
            cfv[hf] = np.asarray(cache_fox_v[0, s], f32).reshape(1024, 1536)
            cfl[hf] = np.asarray(cache_fox_logf[0, s], f32)
        vm = np.zeros((1, 4, 12), f32); sm = np.zeros((1, 4, 12), f32)
        for i in range(4):
            vm[0, i, :] = 1.0 if i < j else 0.0
            sm[0, i, :] = 0.0 if i < j else -1.0e5
        m = dict(shared)
        m.update(xp=xp, xs=xs, memT=np.ascontiguousarray(np.asarray(mem_prompt[b], f32).T), vmask=vm.reshape(1, 48), smask=sm.reshape(1, 48),
                 convst=cst, cmk=cmk, cmv=cmv, cfk=cfk, cfv=cfv, cfl=cfl)
        if KSTOP == 1:
            for k in MINI_SKIP:
                m.pop(k)
        in_maps.append(m)

    res = run_bass_kernel_spmd(nc, in_maps, core_ids=list(range(8)))
    R = res.results

    y_prompt = np.zeros((2, 8192, 2048), f32); y_sample = np.zeros((16, 16, 2048), f32)
    new_mem_k = np.zeros((2, 2, 256, 4, 128), f32); new_mem_v = np.zeros((2, 2, 256, 4, 128), f32)
    conv_p = np.zeros((1, 2, 30, 1536), f32); conv_s = np.zeros((1, 16, 30, 1536), f32)
    fk_p = np.zeros((1, 2, 8192, 12, 128), f32); fv_p = np.zeros((1, 2, 8192, 12, 128), f32); fl_p = np.zeros((1, 2, 8192, 12), f32)
    fk_s = np.zeros((1, 16, 16, 12, 128), f32); fv_s = np.zeros((1, 16, 16, 12, 128), f32); fl_s = np.zeros((1, 16, 16, 12), f32)
    for c in range(8):
        b, j = c // 4, c % 4
        r = R[c]
        for hf in range(2):
            ci = 4 * hf + j
            t0 = 1024 * ci
            s = 2 * c + hf
            y_prompt[b, t0:t0 + 1024, :] = r["yp_o"][hf].T
            y_sample[s] = r["ys_o"][hf].T
            fk_p[0, b, t0:t0 + 1024] = r["fk_o"][hf].T.reshape(1024, 12, 128)
            fv_p[0, b, t0:t0 + 1024] = r["fv_o"][hf].reshape(1024, 12, 128)
            fl_p[0, b, t0:t0 + 1024] = r["fl_o"][hf]
            fk_s[0, s] = r["fks_o"][hf].T.reshape(16, 12, 128)
            fv_s[0, s] = r["fvs_o"][hf].reshape(16, 12, 128)
            fl_s[0, s] = r["fls_o"][hf]
            conv_s[0, s] = r["convs_o"][hf].T
            if ci == 7:
                conv_p[0, b] = r["convp_o"][hf].T
        if j == 0:
            for l in range(2):
                new_mem_k[l, b] = r["mk_o"][l].reshape(4, 128, 256).transpose(2, 0, 1)
                new_mem_v[l, b] = r["mv_o"][l].reshape(256, 4, 128)
    return (y_prompt, y_sample, new_mem_k, new_mem_v, conv_p, conv_s, fk_p, fv_p, fl_p, fk_s, fv_s, fl_s)
```

```python
import contextlib
import os
import numpy as np
import concourse.bass as bass
import concourse.mybir as mybir
from concourse.bass_utils import run_bass_kernel_spmd

F32 = mybir.dt.float32
BF16 = mybir.dt.bfloat16
AF = mybir.ActivationFunctionType
ALU = mybir.AluOpType
EPOCH = 20000
KSTOP = int(os.environ.get('KSTOP', '0'))
KSUB = int(os.environ.get('KSUB', '0'))
KNOCC = int(os.environ.get('KNOCC', '0'))


MINI_SKIP = ()


class StopEmit(Exception):
    pass
EPS = 1e-6
SCALE = 128 ** -0.5
NEG = -30000.0
TL = [(0, 512), (512, 512), (1024, 16)]
NT = 1040


class Rec:
    __slots__ = ("eng", "sem", "val", "flag")

    def __init__(self, eng):
        self.eng = eng
        self.sem = None
        self.val = None
        self.flag = False


class Buf:
    __slots__ = ("name", "w", "r", "dcount", "excl")

    def __init__(self, name="b", excl=False):
        self.name = name
        self.w = None
        self.r = {}
        self.dcount = 0
        self.excl = excl


class Prog:
    COMPUTE = ("pe", "act", "dve", "pool")
    ENGS = ("pe", "act", "dve", "pool", "sp")

    def __init__(self, nc):
        self.nc = nc
        self.dry = False
        self.stream = {e: [] for e in self.ENGS}
        self.dma_bufs = {}
        self.final_waits = []
        self.all_dma = {}
        self.nops = 0

    def _deps(self, eng, reads, writes, semkey=None):
        deps = []
        for b in reads:
            if b.w is not None:
                deps.append(b.w)
            if b.excl:
                deps.extend(r for k, r in b.r.items() if k != eng)
        for b in writes:
            if b.w is not None:
                if not (semkey is not None and b.w.sem == semkey):
                    deps.append(b.w)
            deps.extend(b.r.values())
        out = []
        seen = set()
        for d in deps:
            if id(d) in seen:
                continue
            seen.add(id(d))
            if d.eng == "pe" and eng == "pe":
                continue
            out.append(d)
        return out

    def op(self, eng, fn, reads=(), writes=()):
        if self.dry:
            return None
        deps = self._deps(eng, reads, writes)
        st = self.stream[eng]
        for d in deps:
            d.flag = True
            st.append(("wait", d))
        rec = Rec(eng)
        st.append(("op", fn, rec))
        for b in reads:
            b.r[eng] = rec
        for b in writes:
            b.w = rec
            b.r = {}
        self.nops += 1
        return rec

    def dma(self, queue, fn, owner, reads=(), writes=(), final=False):
        if self.dry:
            return None
        key = ("dma", id(owner))
        deps = self._deps("dma", reads, writes, semkey=key)
        st = self.stream[queue]
        for d in deps:
            d.flag = True
            st.append(("wait", d))
        rec = Rec("dma")
        owner.dcount += 1
        self.dma_bufs[id(owner)] = owner
        rec.sem = key
        rec.val = 16 * owner.dcount
        rec.flag = True
        st.append(("dma", fn, rec))
        for b in reads:
            b.r[key] = rec
        for b in writes:
            b.w = rec
            b.r = {}
        self.all_dma[key] = rec
        if final:
            self.final_waits.append(rec)
        return rec

    def barrier(self):
        if self.dry:
            return
        recs = []
        for e in self.COMPUTE:
            last = None
            for it in reversed(self.stream[e]):
                if it[0] == "op":
                    last = it[2]
                    break
            if last is not None:
                last.flag = True
                recs.append(last)
        recs.extend(self.all_dma.values())
        for e in self.ENGS:
            for r in recs:
                self.stream[e].append(("wait", r))

    def build(self):
        nc = self.nc
        nepoch = {}
        for e in self.COMPUTE:
            cnt = 0
            for it in self.stream[e]:
                if it[0] == "op" and it[2].flag:
                    rec = it[2]
                    ep = cnt // EPOCH
                    rec.sem = (e, ep)
                    rec.val = cnt - ep * EPOCH + 1
                    cnt += 1
            nepoch[e] = (cnt + EPOCH - 1) // EPOCH if cnt else 0
        for rec in self.final_waits:
            self.stream["sp"].append(("wait", rec))
        with contextlib.ExitStack() as es:
            sems = {}
            for e in self.COMPUTE:
                for ep in range(nepoch[e]):
                    sems[(e, ep)] = es.enter_context(nc.semaphore(f"c_{e}_{ep}"))
            for i, (k, b) in enumerate(self.dma_bufs.items()):
                sems[("dma", k)] = es.enter_context(nc.semaphore(f"d{i}"))
            print("PROG stats: sems", len(sems), "ops", {e: len(v) for e, v in self.stream.items()}, flush=True)
            block = es.enter_context(nc.Block())
            handles = {"pe": "tensor", "act": "scalar", "dve": "vector", "pool": "gpsimd", "sp": "sync"}

            def replay(engname):
                def run(eng):
                    waited = {}
                    for it in self.stream[engname]:
                        if it[0] == "wait":
                            d = it[1]
                            if waited.get(d.sem, 0) >= d.val:
                                continue
                            if d.eng != "dma" and d.sem[0] == engname and d.eng == "pe":
                                continue
                            waited[d.sem] = d.val
                            if d.eng != "dma":
                                for ep in range(d.sem[1]):
                                    waited[(d.sem[0], ep)] = EPOCH
                            eng.wait_ge(sems[d.sem], d.val)
                        else:
                            _, fn, rec = it
                            ins = fn(eng)
                            if rec.flag:
                                if rec.eng == "dma":
                                    ins.then_inc(sems[rec.sem], 16)
                                else:
                                    ins.then_inc(sems[rec.sem], 1)
                return run

            for engname in self.ENGS:
                getattr(block, handles[engname])(replay(engname))


class Ring:
    def __init__(self, items):
        self.items = items
        self.i = 0

    def next(self):
        it = self.items[self.i % len(self.items)]
        self.i += 1
        return it


class Grid:
    def __init__(self, name):
        self.name = name
        self.d = {}

    def __call__(self, *key):
        b = self.d.get(key)
        if b is None:
            b = Buf(self.name)
            self.d[key] = b
        return b

    def reset(self):
        self.d = {}


PV_GMIX0, PV_GMLP0, PV_GMIX1, PV_GMLP1, PV_GFIN, PV_GMEM0, PV_GMEM1 = 0, 16, 32, 48, 64, 80, 96
PV_CB, PV_LNG, PV_LNB, PV_CW = 112, 124, 136, 148
PV_N = 148 + 31 * 12


def build_nc():
    nc = bass.Bass("TRN2", target_bir_lowering=False)

    def din(name, shape):
        if KSTOP == 1 and name in MINI_SKIP:
            return nc.dram_tensor(name, list(shape), F32).ap()
        return nc.dram_tensor(name, list(shape), F32, kind="ExternalInput").ap()

    def dout(name, shape):
        return nc.dram_tensor(name, list(shape), F32, kind="ExternalOutput").ap()

    xp = din("xp", [2, 2048, 1056]); xs = din("xs", [2, 2048, 16]); memT = din("memT", [2048, 256])
    pvec_d = din("pvec", [128, PV_N]); bfox_d = din("bfox", [1, 12])
    vmask_d = din("vmask", [1, 48]); smask_d = din("smask", [1, 48])
    convst = din("convst", [2, 1536, 30])
    cmk = din("cmk", [2, 2, 512, 256]); cmv = din("cmv", [2, 2, 256, 512])
    cfk = din("cfk", [2, 1536, 1024]); cfv = din("cfv", [2, 1024, 1536]); cfl = din("cfl", [2, 1024, 12])
    wf_d = din("wf_d", [128, 192])
    w_mem_k = ("w_mem_k", 0), ("w_mem_k", 1)
    w_mem_v = ("w_mem_v", 0), ("w_mem_v", 1)
    w_in_conv = ("w_in_conv", 0)
    w_in_fox = ("w_in_fox", 0)
    w_out = ("w_out", 0), ("w_out", 1)
    w_up = ("w_up", 0), ("w_up", 1)
    WT = {}

    yp_o = dout("yp_o", [2, 2048, 1024]); ys_o = dout("ys_o", [2, 2048, 16])
    mk_o = dout("mk_o", [2, 512, 256]); mv_o = dout("mv_o", [2, 256, 512])
    convp_o = dout("convp_o", [2, 1536, 30]); convs_o = dout("convs_o", [2, 1536, 30])
    fk_o = dout("fk_o", [2, 1536, 1024]); fv_o = dout("fv_o", [2, 1024, 1536]); fl_o = dout("fl_o", [2, 1024, 12])
    fks_o = dout("fks_o", [2, 1536, 16]); fvs_o = dout("fvs_o", [2, 16, 1536]); fls_o = dout("fls_o", [2, 16, 12])

    KVN = 3072 * 1024
    kst = [[nc.dram_tensor(f"kst{h}_{q}", [512, 1024], BF16).ap() for q in range(3)] for h in range(2)]
    vst = [[nc.dram_tensor(f"vst{h}_{q}", [512, 1024], BF16).ap() for q in range(3)] for h in range(2)]
    kga = [[nc.dram_tensor(f"kga{h}_{q}", [2048, 1024], BF16).ap() for q in range(3)] for h in range(2)]
    vga = [[nc.dram_tensor(f"vga{h}_{q}", [2048, 1024], BF16).ap() for q in range(3)] for h in range(2)]
    lgst = [nc.dram_tensor(f"lgst{h}", [1024, 12], F32).ap() for h in range(2)]
    lgga = [nc.dram_tensor(f"lgga{h}", [4096, 12], F32).ap() for h in range(2)]

    p = Prog(nc)
    es = contextlib.ExitStack()
    with es:
        def sb(name, shape, dt):
            return es.enter_context(nc.sbuf_tensor("s_" + name, list(shape), dt))

        Y = sb("Y", [128, 16, NT], F32)
        XN = sb("XN", [128, 16, NT], BF16)
        QX = sb("QX", [128, 4, NT], BF16)
        NSLOT = 3
        WS = [sb(f"ws{i}", [128, 4096], BF16) for i in range(NSLOT)]
        MK = sb("MK", [128, 2, 4, 256], BF16)
        MV = sb("MV", [128, 2, 2, 512], BF16)
        ones_bf = sb("ones_bf", [128, 128], BF16)
        ones_f = sb("ones_f", [128, 128], F32)
        tri_f = sb("tri_f", [128, 128], F32)
        ident_f = sb("ident_f", [128, 128], F32)
        trimask = sb("trimask", [128, 128], F32)
        pvec = sb("pvec", [128, PV_N], F32)
        bfox = sb("bfox", [128, 12], F32)
        vmask = sb("vmask", [128, 4, 12], F32)
        smask = sb("smask", [128, 4, 12], F32)
        Wf = sb("Wf", [128, 16, 12], BF16)
        SQ = [sb(f"sq{i}", [128, 512], BF16) for i in range(3)]
        R32 = [sb(f"r32_{i}", [128, 512], F32) for i in range(3)]
        NRB = 19840
        NRF = 4224
        REGB = sb("REGB", [128, NRB], BF16)
        REGF = sb("REGF", [128, NRF], F32)
        PSB = [es.enter_context(nc.psum_tensor(f"ps{i}", [128, 512], F32)) for i in range(8)]

        Bconst = Buf("const")
        BY = Grid("Y"); BXN = Grid("XN"); BQX = Grid("QX")
        BWS = [Buf(f"ws{i}") for i in range(NSLOT)]
        BMK = Grid("MK"); BMV = Grid("MV"); SM = {}
        BSQ = [Buf("sq") for _ in SQ]; BR32 = [Buf("r32") for _ in R32]
        BPS = [Buf(f"ps{i}", excl=True) for i in range(8)]
        sq_ring = Ring(list(zip(SQ, BSQ)))
        r32_ring = Ring(list(zip(R32, BR32)))
        ps_main = Ring(list(zip(PSB[0:4], BPS[0:4])))
        ps_aux = Ring(list(zip(PSB[4:6], BPS[4:6])))
        Bstage = [[], []]
        Bgath = [Grid("kga0"), Grid("kga1")]
        Blgst = [Buf("lgst0"), Buf("lgst1")]
        Blgga = [Buf("lgga0"), Buf("lgga1")]
        Bout = Grid("out")

        def OP(eng, name, reads, writes, **kw):
            return p.op(eng, lambda e: getattr(e, name)(**kw), reads, writes)

        def DMA(q, out, in_, owner, reads=(), writes=(), final=False):
            return p.dma(q, lambda e: e.dma_start(out=out, in_=in_), owner, reads, writes, final)

        def MM(out, lhsT, rhs, start, stop, reads, writes):
            return p.op("pe", lambda e: e.matmul(out, lhsT=lhsT, rhs=rhs, start=start, stop=stop), reads, writes)

        def pcol(c):
            return pvec[:, c:c + 1]

        class WStream:
            def __init__(self):
                self.order = []
                self.uniq = {}
                self.pos = 0
                self.issued = 0

            def reset(self):
                self.pos = 0
                self.issued = 0

            def _issue(self, i):
                s_ = i % NSLOT
                tid = self.uniq[self.order[i]]
                DMA("pool", WS[s_][:, :], WT["ap"][tid], BWS[s_], writes=[BWS[s_]])

            def next(self, key):
                if p.dry:
                    self.order.append(key)
                    self.uniq.setdefault(key, len(self.uniq))
                    return WS[0], BWS[0]
                while self.issued < min(len(self.order), self.pos + NSLOT):
                    self._issue(self.issued)
                    self.issued += 1
                s_ = self.pos % NSLOT
                self.pos += 1
                return WS[s_], BWS[s_]

        ws = WStream()

        def wtile_cols(wref, pieces):
            return ("cols", wref[0], wref[1], tuple(pieces)), sum(n for _, n in pieces)

        def rmsnorm(src, dst, gcol0, tiles, nkc=16, denom=2048.0):
            for (off, w) in tiles:
                ss, ssb = ps_aux.next()
                for kc in range(nkc):
                    sq, sqb = sq_ring.next()
                    a, ab = src(kc, off, w)
                    OP("act", "activation", [ab], [sqb], out=sq[:, :w], in_=a, func=AF.Square)
                    MM(ss[:, :w], ones_bf[:], sq[:, :w], kc == 0, kc == nkc - 1, [sqb, Bconst], [ssb])
                rs, rsb = r32_ring.next()
                OP("act", "activation", [ssb], [rsb], out=rs[:, :w], in_=ss[:, :w], func=AF.Sqrt, scale=1.0 / denom, bias=EPS)
                OP("dve", "reciprocal", [rsb], [rsb], out=rs[:, :w], in_=rs[:, :w])
                for kc in range(nkc):
                    a, ab = src(kc, off, w)
                    d, db = dst(kc, off, w)
                    OP("dve", "scalar_tensor_tensor", [ab, rsb, Bconst], [db], out=d, in0=a, scalar=pcol(gcol0 + kc),
                       in1=rs[:, :w], op0=ALU.mult, op1=ALU.mult)

        def srcY(kc, off, w):
            return Y[:, kc, off:off + w], BY(kc, off)

        def dstXN(kc, off, w):
            return XN[:, kc, off:off + w], BXN(kc, off)

        def mem_attend(layer, prompt_tiles=True):
            for h in range(4):
                for (off, w) in TL:
                    smp = off >= 1024
                    pts = []
                    for blk in range(2):
                        sp_, spb = ps_main.next()
                        if smp:
                            lh, lb = SM["MKS"][:, h, blk * 128:(blk + 1) * 128], SM["BMKS"]
                        else:
                            lh, lb = MK[:, layer, h, blk * 128:(blk + 1) * 128], BMK(layer)
                        MM(sp_[:, :w], lh, QX[:, h, off:off + w], True, True, [lb, BQX(h, off)], [spb])
                        pt, ptb = sq_ring.next()
                        OP("act", "activation", [spb], [ptb], out=pt[:, :w], in_=sp_[:, :w], func=AF.Exp, scale=SCALE)
                        pts.append((pt, ptb))
                    o_, ob = ps_aux.next()
                    l_, lb2 = ps_aux.next()
                    for blk in range(2):
                        pt, ptb = pts[blk]
                        if smp:
                            vh, vb = SM["MVS"][:, blk, h * 128:(h + 1) * 128], SM["BMVS"]
                        else:
                            vh, vb = MV[:, layer, blk, h * 128:(h + 1) * 128], BMV(layer)
                        MM(o_[:, :w], vh, pt[:, :w], blk == 0, blk == 1, [vb, ptb], [ob])
                    for blk in range(2):
                        pt, ptb = pts[blk]
                        MM(l_[:, :w], ones_bf[:], pt[:, :w], blk == 0, blk == 1, [Bconst, ptb], [lb2])
                    rc, rcb = r32_ring.next()
                    OP("dve", "reciprocal", [lb2], [rcb], out=rc[:, :w], in_=l_[:, :w])
                    OP("dve", "tensor_tensor", [ob, rcb], [BXN(12 + h, off)], out=XN[:, 12 + h, off:off + w], in0=o_[:, :w],
                       in1=rc[:, :w], op=ALU.mult)

        def load_sample_mem(hf, layer):
            DMA("pool", SM["MKS"], cmk[layer, hf].rearrange("(h d) m -> d h m", d=128), SM["BMKS"], writes=[SM["BMKS"]])
            DMA("pool", SM["MVS"], cmv[layer, hf].rearrange("(b m) c -> m b c", m=128), SM["BMVS"], writes=[SM["BMVS"]])

        def out_proj(layer):
            for wt in range(8):
                desc, _ = wtile_cols(w_out[layer], [(wt * 256, 256)])
                slot, sbuf_ = ws.next(desc)
                sv = slot[:, :].rearrange("p (a b) -> p a b", a=16, b=256)
                for mc in range(2):
                    m = wt * 2 + mc
                    for (off, w) in TL:
                        pb, pbb = ps_main.next()
                        for kc in range(16):
                            MM(pb[:, :w], sv[:, kc, mc * 128:(mc + 1) * 128], XN[:, kc, off:off + w], kc == 0, kc == 15,
                               [sbuf_, BXN(kc, off)], [pbb])
                        OP("dve", "tensor_tensor", [pbb, BY(m, off)], [BY(m, off)], out=Y[:, m, off:off + w], in0=pb[:, :w],
                           in1=Y[:, m, off:off + w], op=ALU.add)

        def mlp(layer, gcol0):
            rmsnorm(srcY, dstXN, gcol0, TL)
            H = REGB[:, 0:4 * NT].rearrange("p (a b) -> p a b", a=4, b=NT)
            BH = Grid("H")
            for hg in range(16):
                for ut in range(2):
                    desc, _ = wtile_cols(w_up[layer], [(hg * 512 + ut * 256, 256)])
                    slot, sbuf_ = ws.next(desc)
                    sv = slot[:, :].rearrange("p (a b) -> p a b", a=16, b=256)
                    for mc in range(2):
                        hc = ut * 2 + mc
                        for (off, w) in TL:
                            pb, pbb = ps_main.next()
                            for kc in range(16):
                                MM(pb[:, :w], sv[:, kc, mc * 128:(mc + 1) * 128], XN[:, kc, off:off + w], kc == 0, kc == 15,
                                   [sbuf_, BXN(kc, off)], [pbb])
                            r, rb = r32_ring.next()
                            OP("act", "activation", [pbb], [rb], out=r[:, :w], in_=pb[:, :w], func=AF.Relu)
                            OP("dve", "tensor_tensor", [rb], [BH(hc, off)], out=H[:, hc, off:off + w], in0=r[:, :w], in1=r[:, :w],
                               op=ALU.mult)
                for dt_ in range(2):
                    slot, sbuf_ = ws.next(("down", layer, hg, dt_))
                    sv = slot[:, :].rearrange("p (a b) -> p a b", a=4, b=1024)
                    for mc in range(8):
                        m = dt_ * 8 + mc
                        for (off, w) in TL:
                            pb, pbb = ps_main.next()
                            for kc in range(4):
                                MM(pb[:, :w], sv[:, kc, mc * 128:(mc + 1) * 128], H[:, kc, off:off + w], kc == 0, kc == 3,
                                   [sbuf_, BH(kc, off)], [pbb])
                            OP("dve", "tensor_tensor", [pbb, BY(m, off)], [BY(m, off)], out=Y[:, m, off:off + w], in0=pb[:, :w],
                               in1=Y[:, m, off:off + w], op=ALU.add)

        def stop_at(n, hf=0, dump=True):
            if KSTOP == n and hf == 0:
                if dump:
                    Bd = Buf("dump")
                    DMA("sp", yp_o[hf].rearrange("(kc p) w -> p kc w", p=128), Y[:, :, 0:1024], Bd,
                        reads=[BY(kc, off) for kc in range(16) for off in (0, 512)], final=True)
                    DMA("sp", ys_o[hf].rearrange("(kc p) w -> p kc w", p=128), Y[:, :, 1024:1040], Bd,
                        reads=[BY(kc, 1024) for kc in range(16)], final=True)
                raise StopEmit()

        def emit():
            ws.reset()
            for g in (BY, BXN, BQX, BMK, BMV, Bout, Bgath[0], Bgath[1]):
                g.reset()
            Bstage[0] = []
            Bstage[1] = []
            OP("pool", "memset", [], [Bconst], ap=ones_bf[:], constant=1.0)
            OP("pool", "memset", [], [Bconst], ap=ones_f[:], constant=1.0)
            OP("pool", "memset", [], [Bconst], ap=tri_f[:], constant=1.0)
            OP("pool", "affine_select", [Bconst], [Bconst], out=tri_f[:], in_=tri_f[:], pattern=[[1, 128]], compare_op=ALU.is_ge,
               fill=0.0, base=0, channel_multiplier=-1)
            OP("pool", "memset", [], [Bconst], ap=ident_f[:], constant=1.0)
            OP("pool", "affine_select", [Bconst], [Bconst], out=ident_f[:], in_=ident_f[:], pattern=[[1, 128]], compare_op=ALU.is_equal,
               fill=0.0, base=0, channel_multiplier=-1)
            OP("pool", "memset", [], [Bconst], ap=trimask[:], constant=0.0)
            OP("pool", "affine_select", [Bconst], [Bconst], out=trimask[:], in_=trimask[:], pattern=[[1, 128]], compare_op=ALU.is_ge,
               fill=NEG, base=0, channel_multiplier=-1)
            Bpv = Buf("pv")
            DMA("sp", pvec[:], pvec_d, Bpv, writes=[Bconst])
            DMA("sp", bfox[:], bfox_d.partition_broadcast(128), Bpv, writes=[Bconst])
            DMA("sp", vmask[:].rearrange("p a b -> p (a b)"), vmask_d.partition_broadcast(128), Bpv, writes=[Bconst])
            DMA("sp", smask[:].rearrange("p a b -> p (a b)"), smask_d.partition_broadcast(128), Bpv, writes=[Bconst])
            DMA("pool", Wf[:].rearrange("p a b -> p (a b)"), wf_d, Bpv, writes=[Bconst])

            if KSUB == 1:
                raise StopEmit()
            DMA("sp", Y[:, :, 0:256], memT.rearrange("(kc p) m -> p kc m", p=128), BY("mem"), writes=[BY(kc, 0) for kc in range(16)])
            for layer in range(2):
                rmsnorm(srcY, dstXN, PV_GMEM0 + 16 * layer, [(0, 256)])
                if KSUB == 2:
                    raise StopEmit()
                for wt in range(2):
                    desc, _ = wtile_cols(w_mem_k[layer], [(wt * 256, 256)])
                    slot, sbuf_ = ws.next(desc)
                    sv = slot[:, :].rearrange("p (a b) -> p a b", a=16, b=256)
                    for mc in range(2):
                        h = wt * 2 + mc
                        pb, pbb = ps_main.next()
                        for kc in range(16):
                            MM(pb[:, :256], sv[:, kc, mc * 128:(mc + 1) * 128], XN[:, kc, 0:256], kc == 0, kc == 15,
                               [sbuf_, BXN(kc, 0)], [pbb])
                        OP("act", "copy", [pbb], [BMK(layer)], out=MK[:, layer, h, :], in_=pb[:, :256])
                        r, rb = r32_ring.next()
                        OP("dve", "tensor_copy", [pbb], [rb], out=r[:, :256], in_=pb[:, :256])
                        DMA("sp", mk_o[layer, h * 128:(h + 1) * 128, :], r[:, :256], rb, reads=[rb], final=True)
                for wt in range(2):
                    desc, _ = wtile_cols(w_mem_v[layer], [(wt * 256, 256)])
                    slot, sbuf_ = ws.next(desc)
                    sv = slot[:, :].rearrange("p (a b) -> p a b", a=16, b=256)
                    for blk in range(2):
                        pb, pbb = ps_main.next()
                        for kc in range(16):
                            MM(pb[:, :256], XN[:, kc, blk * 128:(blk + 1) * 128], sv[:, kc, :], kc == 0, kc == 15,
                               [sbuf_, BXN(kc, 0)], [pbb])
                        OP("act", "copy", [pbb], [BMV(layer)], out=MV[:, layer, blk, wt * 256:(wt + 1) * 256], in_=pb[:, :256])
                        r, rb = r32_ring.next()
                        OP("dve", "tensor_copy", [pbb], [rb], out=r[:, :256], in_=pb[:, :256])
                        DMA("sp", mv_o[layer, blk * 128:(blk + 1) * 128, wt * 256:(wt + 1) * 256], r[:, :256], rb, reads=[rb], final=True)

            stop_at(1, dump=False)
            for hf in range(2):
                emit_half(hf)

        def emit_half(hf):
            p.barrier()
            CB = REGB[:, 0:12 * NT].rearrange("p (a b) -> p a b", a=12, b=NT)
            XHN = REGB[:, 12 * NT:12 * NT + 512].rearrange("p (a b) -> p a b", a=16, b=32)
            UH = [REGF[:, i * 1056:(i + 1) * 1056] for i in range(2)]
            UHS = REGF[:, 2112:2112 + 552].rearrange("p (a b) -> p a b", a=12, b=46)
            ACC = REGF[:, 2664:2664 + 1024]
            ACCS = REGF[:, 3688:3688 + 16]
            XH = REGF[:, 3704:3704 + 512].rearrange("p (a b) -> p a b", a=16, b=32)
            BCB = Grid("CB"); BXHN = Grid("XHN"); BUH = [Buf("uh0"), Buf("uh1")]; BUHS = Grid("UHS")
            BACC = Buf("acc"); BACCS = Buf("accs"); BXH = Grid("XH")
            uh_ring = Ring(list(zip(UH, BUH)))
            SM["MKS"] = REGB[:, 12992:14016].rearrange("p (a b) -> p a b", a=4, b=256)
            SM["MVS"] = REGB[:, 14016:15040].rearrange("p (a b) -> p a b", a=2, b=512)
            SM["BMKS"] = Buf("mks"); SM["BMVS"] = Buf("mvs")

            xv = xp[hf].rearrange("(kc p) w -> p kc w", p=128)
            DMA("sp", XH[:, :, :], xv[:, :, 0:32], BXH("ld"), writes=[BXH(kc) for kc in range(16)])
            for q4 in range(4):
                for (off, w) in TL[:2]:
                    DMA("sp", Y[:, q4 * 4:(q4 + 1) * 4, off:off + w], xv[:, q4 * 4:(q4 + 1) * 4, 32 + off:32 + off + w], BY("ld", q4, off),
                        writes=[BY(kc, off) for kc in range(q4 * 4, q4 * 4 + 4)])
            DMA("sp", Y[:, :, 1024:1040], xs[hf].rearrange("(kc p) w -> p kc w", p=128), BY("lds"),
                writes=[BY(kc, 1024) for kc in range(16)])
            DMA("sp", UHS[:, :, 0:30], convst[hf].rearrange("(m p) w -> p m w", p=128), BUHS("ld"), writes=[BUHS(m) for m in range(12)])
            load_sample_mem(hf, 0)

            rmsnorm(srcY, dstXN, PV_GMIX0, TL)
            rmsnorm(lambda kc, off, w: (XH[:, kc, off:off + w], BXH(kc)), lambda kc, off, w: (XHN[:, kc, off:off + w], BXHN(kc)),
                    PV_GMIX0, [(0, 32)])
            for m in range(12):
                desc, _ = wtile_cols(w_in_conv, [(m * 128, 128), (1536 + m * 128, 128)])
                slot, sbuf_ = ws.next(desc)
                sva = slot[:, 0:2048].rearrange("p (a b) -> p a b", a=16, b=128)
                svg = slot[:, 2048:4096].rearrange("p (a b) -> p a b", a=16, b=128)
                uh, uhb = uh_ring.next()
                for ti in range(4):
                    if ti == 0:
                        w = 32
                        rhs = lambda kc: (XHN[:, kc, 0:32], BXHN(kc))
                        dst, dstb = uh[:, 0:32], uhb
                    else:
                        off, w = TL[ti - 1]
                        rhs = (lambda off, w: (lambda kc: (XN[:, kc, off:off + w], BXN(kc, off))))(off, w)
                        if ti < 3:
                            dst, dstb = uh[:, 32 + off:32 + off + w], uhb
                        else:
                            dst, dstb = UHS[:, m, 30:46], BUHS(m)
                    pa, pab = ps_main.next()
                    pg, pgb = ps_main.next()
                    for kc in range(16):
                        r_, rb_ = rhs(kc)
                        MM(pa[:, :w], sva[:, kc, :], r_, kc == 0, kc == 15, [sbuf_, rb_], [pab])
                    for kc in range(16):
                        r_, rb_ = rhs(kc)
                        MM(pg[:, :w], svg[:, kc, :], r_, kc == 0, kc == 15, [sbuf_, rb_], [pgb])
                    sg, sgb = r32_ring.next()
                    OP("act", "activation", [pgb], [sgb], out=sg[:, :w], in_=pg[:, :w], func=AF.Sigmoid)
                    OP("dve", "tensor_tensor", [pab, sgb], [dstb], out=dst, in0=pa[:, :w], in1=sg[:, :w], op=ALU.mult)
                cwc = lambda wi: pcol(PV_CW + wi * 12 + m)
                OP("dve", "tensor_scalar", [uhb, Bconst], [BACC], out=ACC, in0=uh[:, 2:1026], scalar1=cwc(0), scalar2=pcol(PV_CB + m),
                   op0=ALU.mult, op1=ALU.add)
                for wi in range(1, 31):
                    OP("dve", "scalar_tensor_tensor", [uhb, BACC, Bconst], [BACC], out=ACC, in0=uh[:, 2 + wi:1026 + wi], scalar=cwc(wi),
                       in1=ACC, op0=ALU.mult, op1=ALU.add)
                OP("act", "copy", [BACC], [BCB(m, 0)], out=CB[:, m, 0:1024], in_=ACC)
                DMA("sp", convp_o[hf, m * 128:(m + 1) * 128, :], uh[:, 1026:1056], uhb, reads=[uhb], final=True)
                OP("dve", "tensor_scalar", [BUHS(m), Bconst], [BACCS], out=ACCS, in0=UHS[:, m, 0:16], scalar1=cwc(0), scalar2=pcol(PV_CB + m),
                   op0=ALU.mult, op1=ALU.add)
                for wi in range(1, 31):
                    OP("dve", "scalar_tensor_tensor", [BUHS(m), BACCS, Bconst], [BACCS], out=ACCS, in0=UHS[:, m, wi:wi + 16],
                       scalar=cwc(wi), in1=ACCS, op0=ALU.mult, op1=ALU.add)
                OP("act", "copy", [BACCS], [BCB(m, 1024)], out=CB[:, m, 1024:1040], in_=ACCS)
            DMA("sp", convs_o[hf].rearrange("(m p) w -> p m w", p=128), UHS[:, :, 16:46], BUHS("st"),
                reads=[BUHS(m) for m in range(12)], final=True)
            for wt in range(2):
                desc, _ = wtile_cols(w_in_conv, [(3072 + wt * 256, 256)])
                slot, sbuf_ = ws.next(desc)
                sv = slot[:, :].rearrange("p (a b) -> p a b", a=16, b=256)
                for mc in range(2):
                    h = wt * 2 + mc
                    for (off, w) in TL:
                        pb, pbb = ps_main.next()
                        for kc in range(16):
                            MM(pb[:, :w], sv[:, kc, mc * 128:(mc + 1) * 128], XN[:, kc, off:off + w], kc == 0, kc == 15,
                               [sbuf_, BXN(kc, off)], [pbb])
                        OP("act", "copy", [pbb], [BQX(h, off)], out=QX[:, h, off:off + w], in_=pb[:, :w])
            stop_at(2, hf)
            MEANB = ACC[:, 0:512]
            RSTDB = ACC[:, 512:1024]
            for (off, w) in TL:
                cboff = 0 if off < 1024 else 1024
                s_, sb_ = ps_aux.next()
                q_, qb_ = ps_aux.next()
                for m in range(12):
                    MM(s_[:, :w], ones_bf[:], CB[:, m, off:off + w], m == 0, m == 11, [Bconst, BCB(m, cboff)], [sb_])
                for m in range(12):
                    sq, sqb = sq_ring.next()
                    OP("act", "activation", [BCB(m, cboff)], [sqb], out=sq[:, :w], in_=CB[:, m, off:off + w], func=AF.Square)
                    MM(q_[:, :w], ones_bf[:], sq[:, :w], m == 0, m == 11, [Bconst, sqb], [qb_])
                OP("dve", "tensor_scalar", [sb_, BACC], [BACC], out=MEANB[:, :w], in0=s_[:, :w], scalar1=1.0 / 1536, scalar2=0.0,
                   op0=ALU.mult, op1=ALU.add)
                t_, tb_ = r32_ring.next()
                OP("dve", "tensor_tensor", [BACC], [tb_], out=t_[:, :w], in0=MEANB[:, :w], in1=MEANB[:, :w], op=ALU.mult)
                OP("dve", "scalar_tensor_tensor", [qb_, tb_, BACC], [BACC], out=RSTDB[:, :w], in0=q_[:, :w], scalar=1.0 / 1536, in1=t_[:, :w],
                   op0=ALU.mult, op1=ALU.subtract)
                OP("act", "activation", [BACC], [BACC], out=RSTDB[:, :w], in_=RSTDB[:, :w], func=AF.Sqrt, bias=EPS, scale=1.0)
                OP("dve", "reciprocal", [BACC], [BACC], out=RSTDB[:, :w], in_=RSTDB[:, :w])
                for m in range(12):
                    t_, tb_ = r32_ring.next()
                    OP("dve", "tensor_tensor", [BCB(m, cboff), BACC], [tb_], out=t_[:, :w], in0=CB[:, m, off:off + w], in1=MEANB[:, :w],
                       op=ALU.subtract)
                    OP("dve", "tensor_tensor", [tb_, BACC], [tb_], out=t_[:, :w], in0=t_[:, :w], in1=RSTDB[:, :w], op=ALU.mult)
                    OP("act", "activation", [tb_, Bconst], [BXN(m, off)], out=XN[:, m, off:off + w], in_=t_[:, :w], func=AF.Silu,
                       scale=pcol(PV_LNG + m), bias=pcol(PV_LNB + m))
            mem_attend(0)
            out_proj(0)
            stop_at(3, hf)
            p.barrier()
            mlp(0, PV_GMLP0)
            stop_at(4, hf)

            p.barrier()
            QT = REGB[:, 0:12 * NT].rearrange("p (a b) -> p a b", a=12, b=NT)
            o_ = 12 * NT
            KT = [REGB[:, o_ + i * 1024:o_ + (i + 1) * 1024] for i in range(2)]; o_ += 2048
            VT = [REGB[:, o_ + i * 1024:o_ + (i + 1) * 1024].rearrange("p (a b) -> p a b", a=8, b=128) for i in range(2)]; o_ += 2048
            PT = [REGB[:, o_ + i * 512:o_ + (i + 1) * 512] for i in range(3)]; o_ += 1536
            KSN = REGB[:, o_:o_ + 192].rearrange("p (a b) -> p a b", a=12, b=16); o_ += 192
            VSN = REGB[:, o_:o_ + 1536]; o_ += 1536
            assert o_ <= NRB, o_
            f_ = 0
            CQ = [REGF[:, f_ + i * 512:f_ + (i + 1) * 512] for i in range(2)]; f_ += 1024
            NCH = 4 if hf == 0 else 8
            LCS = REGF[:, f_:f_ + NCH * 96].rearrange("p (c t h) -> p c t h", c=NCH, t=8, h=12); f_ += 768
            OFF = REGF[:, f_:f_ + NCH * 96].rearrange("p (c t h) -> p c t h", c=NCH, t=8, h=12); f_ += 768
            TOT = REGF[:, f_:f_ + NCH * 12].rearrange("p (c h) -> p c h", c=NCH, h=12); f_ += 96
            GS = REGF[:, f_:f_ + 96].rearrange("p (c h) -> p c h", c=8, h=12); f_ += 96
            BASE = REGF[:, f_:f_ + 192].rearrange("p (c q h) -> p c q h", c=8, q=2, h=12); f_ += 192
            LGO = REGF[:, f_:f_ + 96].rearrange("p (t h) -> p t h", t=8, h=12); f_ += 96
            OFFO = REGF[:, f_:f_ + 108].rearrange("p (t h) -> p t h", t=9, h=12); f_ += 108
            CQREL = REGF[:, f_:f_ + 96].rearrange("p (t h) -> p t h", t=8, h=12); f_ += 96
            KBO = REGF[:, f_:f_ + 192].rearrange("p (q t h) -> p q t h", q=2, t=8, h=12); f_ += 192
            LGS = REGF[:, f_:f_ + 12]; f_ += 12
            LCSS = REGF[:, f_:f_ + 12]; f_ += 12
            NLCSS = REGF[:, f_:f_ + 12]; f_ += 12
            FB = REGF[:, f_:f_ + 12]; f_ += 12
            KBT = [REGF[:, f_ + i * 16:f_ + (i + 1) * 16].rearrange("p (q t) -> p q t", q=2, t=8) for i in range(2)]; f_ += 32
            DG = [REGF[:, f_ + i * 128:f_ + (i + 1) * 128] for i in range(2)]; f_ += 256
            VTMP = REGF[:, f_:f_ + 48].rearrange("p (c h) -> p c h", c=4, h=12); f_ += 48
            assert f_ <= NRF, f_
            BQT = Grid("QT"); BKT = [Buf("kt0"), Buf("kt1")]; BVT = [Buf("vt0"), Buf("vt1")]
            BPT = [Buf("pt") for _ in PT]; BCQ = [Buf("cq0"), Buf("cq1")]
            BL = Buf("logf"); BKSN = Buf("ksn"); BVSN = Buf("vsn"); BKBT = [Buf("kbt0"), Buf("kbt1")]; BDG = [Buf("dg0"), Buf("dg1")]
            kt_ring = Ring(list(zip(KT, BKT, VT, BVT)))
            pt_ring = Ring(list(zip(PT, BPT)))
            kbt_ring = Ring(list(zip(KBT, BKBT)))
            dg_ring = Ring(list(zip(DG, BDG)))
            ps_s = Ring(list(zip(PSB[0:3], BPS[0:3])))

            rmsnorm(srcY, dstXN, PV_GMIX1, TL)
            BLO = Buf("lo"); BLS = Buf("ls"); BLG = Buf("lg"); BFB = Buf("fb"); BstageK = []; BstageV = []
            for wt in range(6):
                desc, _ = wtile_cols(w_in_fox, [(1536 + wt * 256, 256)])
                slot, sbuf_ = ws.next(desc)
                sv = slot[:, :].rearrange("p (a b) -> p a b", a=16, b=256)
                for mc in range(2):
                    h = wt * 2 + mc
                    for (off, w) in TL:
                        pb, pbb = ps_main.next()
                        for kc in range(16):
                            MM(pb[:, :w], sv[:, kc, mc * 128:(mc + 1) * 128], XN[:, kc, off:off + w], kc == 0, kc == 15,
                               [sbuf_, BXN(kc, off)], [pbb])
                        r, rb = r32_ring.next()
                        OP("dve", "tensor_copy", [pbb], [rb], out=r[:, :w], in_=pb[:, :w])
                        if off < 1024:
                            DMA("sp", fk_o[hf, h * 128:(h + 1) * 128, off:off + w], r[:, :w], rb, reads=[rb], final=True)
                            kb_, kbb = sq_ring.next()
                            OP("act", "copy", [pbb], [kbb], out=kb_[:, :w], in_=pb[:, :w])
                            BstageK.append(Buf("stw"))
                            DMA("sp", kst[hf][h // 4][(h % 4) * 128:(h % 4 + 1) * 128, off:off + w], kb_[:, :w], kbb, reads=[kbb], writes=[BstageK[-1]])
                        else:
                            DMA("sp", fks_o[hf, h * 128:(h + 1) * 128, :], r[:, :w], rb, reads=[rb], final=True)
                            OP("act", "copy", [pbb], [BKSN], out=KSN[:, h, :], in_=pb[:, :w])
            for q in range(3):
                rec_ = p.op("pool", (lambda a_, b_: lambda e: e.collective_compute("AllGather", ALU.bypass, replica_groups=[[0, 1, 2, 3], [4, 5, 6, 7]],
                                                                                    ins=[a_.opt()], outs=[b_.opt()]))(kst[hf][q], kga[hf][q]), list(BstageK), [Bgath[hf](q, 0)])
                if rec_ is not None:
                    rec_.flag = True
            for wt in range(6):
                desc, _ = wtile_cols(w_in_fox, [(3072 + wt * 256, 256)])
                slot, sbuf_ = ws.next(desc)
                sv = slot[:, :].rearrange("p (a b) -> p a b", a=16, b=256)
                for tb in range(9):
                    off, mm = (tb * 128, 128) if tb < 8 else (1024, 16)
                    toff = 0 if off < 512 else (512 if off < 1024 else 1024)
                    pb, pbb = ps_main.next()
                    for kc in range(16):
                        MM(pb[0:mm, 0:256], XN[:, kc, off:off + mm], sv[:, kc, :], kc == 0, kc == 15, [sbuf_, BXN(kc, toff)], [pbb])
                    r, rb = r32_ring.next()
                    OP("dve", "tensor_copy", [pbb], [rb], out=r[0:mm, 0:256], in_=pb[0:mm, 0:256])
                    if tb < 8:
                        DMA("sp", fv_o[hf, off:off + 128, wt * 256:(wt + 1) * 256], r[:, 0:256], rb, reads=[rb], final=True)
                        vb_, vbb = sq_ring.next()
                        OP("act", "copy", [pbb], [vbb], out=vb_[:, 0:256], in_=pb[:, 0:256])
                        BstageV.append(Buf("stw"))
                        DMA("sp", vview(vst[hf][wt // 2], 0)[off:off + 128, (wt % 2) * 256:(wt % 2 + 1) * 256], vb_[:, 0:256], vbb, reads=[vbb],
                            writes=[BstageV[-1]])
                    else:
                        DMA("sp", fvs_o[hf, :, wt * 256:(wt + 1) * 256], r[0:16, 0:256], rb, reads=[rb], final=True)
                        OP("act", "copy", [pbb], [BVSN], out=VSN[0:16, wt * 256:(wt + 1) * 256], in_=pb[0:16, 0:256])
            for q in range(3):
                rec_ = p.op("pool", (lambda a_, b_: lambda e: e.collective_compute("AllGather", ALU.bypass, replica_groups=[[0, 1, 2, 3], [4, 5, 6, 7]],
                                                                                    ins=[a_.opt()], outs=[b_.opt()]))(vst[hf][q], vga[hf][q]), list(BstageV), [Bgath[hf](q, 1)])
                if rec_ is not None:
                    rec_.flag = True
            for tb in range(9):
                off, mm = (tb * 128, 128) if tb < 8 else (1024, 16)
                toff = 0 if off < 512 else (512 if off < 1024 else 1024)
                pb, pbb = ps_aux.next()
                for kc in range(16):
                    MM(pb[0:mm, 0:12], XN[:, kc, off:off + mm], Wf[:, kc, :], kc == 0, kc == 15, [Bconst, BXN(kc, toff)], [pbb])
                dst = LGO[:, tb, :] if tb < 8 else LGS[0:16, :]
                OP("dve", "tensor_tensor", [pbb, Bconst], [BFB], out=FB[0:mm, :], in0=pb[0:mm, 0:12], in1=bfox[0:mm, :], op=ALU.add)
                OP("act", "activation", [BFB], [BFB], out=FB[0:mm, :], in_=FB[0:mm, :], func=AF.Exp, scale=-1.0)
                OP("act", "activation", [BFB], [BFB], out=FB[0:mm, :], in_=FB[0:mm, :], func=AF.Ln, bias=1.0, scale=1.0)
                OP("dve", "tensor_scalar", [BFB], [BLO if tb < 8 else BLS], out=dst if tb < 8 else LGS[0:16, :], in0=FB[0:mm, :], scalar1=-1.0, scalar2=0.0,
                   op0=ALU.mult, op1=ALU.add)
            Bl2 = Buf("lgo_st")
            DMA("sp", fl_o[hf].rearrange("(t p) h -> p t h", p=128), LGO, Bl2, reads=[BLO], final=True)
            DMA("sp", fls_o[hf], LGS[0:16, :], Bl2, reads=[BLS], final=True)
            DMA("sp", lgst[hf].rearrange("(t p) h -> p t h", p=128), LGO, Bl2, reads=[BLO], writes=[Blgst[hf]])
            rec_ = p.op("pool", lambda e: e.collective_compute("AllGather", ALU.bypass, replica_groups=[[0, 1, 2, 3], [4, 5, 6, 7]],
                                                               ins=[lgst[hf].opt()], outs=[lgga[hf].opt()]), [Blgst[hf]], [Blgga[hf]])
            if rec_ is not None:
                rec_.flag = True
            for wt in range(6):
                desc, _ = wtile_cols(w_in_fox, [(wt * 256, 256)])
                slot, sbuf_ = ws.next(desc)
                sv = slot[:, :].rearrange("p (a b) -> p a b", a=16, b=256)
                for mc in range(2):
                    h = wt * 2 + mc
                    for (off, w) in TL:
                        pb, pbb = ps_main.next()
                        for kc in range(16):
                            MM(pb[:, :w], sv[:, kc, mc * 128:(mc + 1) * 128], XN[:, kc, off:off + w], kc == 0, kc == 15,
                               [sbuf_, BXN(kc, off)], [pbb])
                        OP("act", "copy", [pbb], [BQT(h, off)], out=QT[:, h, off:off + w], in_=pb[:, :w])
            for wt in range(2):
                desc, _ = wtile_cols(w_in_fox, [(4620 + wt * 256, 256)])
                slot, sbuf_ = ws.next(desc)
                sv = slot[:, :].rearrange("p (a b) -> p a b", a=16, b=256)
                for mc in range(2):
                    h = wt * 2 + mc
                    for (off, w) in TL:
                        pb, pbb = ps_main.next()
                        for kc in range(16):
                            MM(pb[:, :w], sv[:, kc, mc * 128:(mc + 1) * 128], XN[:, kc, off:off + w], kc == 0, kc == 15,
                               [sbuf_, BXN(kc, off)], [pbb])
                        OP("act", "copy", [pbb], [BQX(h, off)], out=QX[:, h, off:off + w], in_=pb[:, :w])
            stop_at(5, hf)
            if hf == 0:
                srcs = [(0, r) for r in range(3)]
            else:
                srcs = [(0, r) for r in range(4)] + [(1, r) for r in range(3)]
            nsrc = len(srcs)
            LG = LCS

            def chunk_cumsum(c0, c1, B_):
                ncol = (c1 - c0) * 96
                w_, wb_ = ps_aux.next()
                t_, tb_ = ps_aux.next()
                lgv = LG[:, c0:c1, :, :].rearrange("p c t h -> p (c t h)")
                MM(w_[:, :ncol], tri_f[:], lgv, True, True, [Bconst, B_], [wb_])
                MM(t_[:, :ncol], ones_f[:], lgv, True, True, [Bconst, B_], [tb_])
                tv = t_[:, :ncol].rearrange("p (c t h) -> p c t h", c=c1 - c0, t=8, h=12)
                OP("dve", "memset", [], [B_], ap=OFF[:, c0:c1, 0, :], constant=0.0)
                for tl in range(7):
                    OP("dve", "tensor_tensor", [tb_, B_], [B_], out=OFF[:, c0:c1, tl + 1, :], in0=tv[:, :, tl, :], in1=OFF[:, c0:c1, tl, :],
                       op=ALU.add)
                OP("dve", "tensor_tensor", [tb_, B_], [B_], out=TOT[:, c0:c1, :], in0=tv[:, :, 7, :], in1=OFF[:, c0:c1, 7, :], op=ALU.add)
                OP("dve", "tensor_tensor", [wb_, B_], [B_], out=lgv, in0=w_[:, :ncol],
                   in1=OFF[:, c0:c1, :, :].rearrange("p c t h -> p (c t h)"), op=ALU.add)

            w_, wb_ = ps_aux.next()
            t_, tb_ = ps_aux.next()
            lgov = LGO.rearrange("p t h -> p (t h)")
            MM(w_[:, :96], tri_f[:], lgov, True, True, [Bconst, BLO], [wb_])
            MM(t_[:, :96], ones_f[:], lgov, True, True, [Bconst, BLO], [tb_])
            tv = t_[:, :96].rearrange("p (t h) -> p t h", t=8, h=12)
            OP("dve", "memset", [], [BLO], ap=OFFO[:, 0, :], constant=0.0)
            for tl in range(8):
                OP("dve", "tensor_tensor", [tb_, BLO], [BLO], out=OFFO[:, tl + 1, :], in0=tv[:, tl, :], in1=OFFO[:, tl, :], op=ALU.add)
            OP("dve", "tensor_tensor", [wb_, BLO], [BLO], out=lgov, in0=w_[:, :96], in1=OFFO[:, 0:8, :].rearrange("p t h -> p (t h)"),
               op=ALU.add)
            LCO = LGO
            for qt in range(2):
                OP("dve", "tensor_tensor", [BLO], [BLO], out=KBO[:, qt, :, :], in0=OFFO[:, 4 * qt, :].unsqueeze(1).broadcast_to([128, 8, 12]),
                   in1=LCO, op=ALU.subtract)
                OP("dve", "tensor_tensor", [BLO], [BLO], out=CQREL[:, 4 * qt:4 * qt + 4, :], in0=LCO[:, 4 * qt:4 * qt + 4, :],
                   in1=OFFO[:, 4 * qt, :].unsqueeze(1).broadcast_to([128, 4, 12]), op=ALU.subtract)
            Bld = Buf("lgld")
            DMA("sp", LG[:, nsrc, :, :], cfl[hf].rearrange("(t p) h -> p t h", p=128), Bld, writes=[BLS])
            chunk_cumsum(nsrc, nsrc + 1, BLS)
            w_, wb_ = ps_aux.next()
            MM(w_[0:16, 0:12], tri_f[0:16, 0:16], LGS[0:16, :], True, True, [Bconst, BLS], [wb_])
            OP("dve", "tensor_copy", [wb_], [BLS], out=LCSS[0:16, :], in_=w_[0:16, 0:12])
            OP("dve", "tensor_scalar", [BLS], [BLS], out=NLCSS[0:16, :], in0=LCSS[0:16, :], scalar1=-1.0, scalar2=0.0, op0=ALU.mult, op1=ALU.add)
            stop_at(6, hf)
            OPS = [(PSB[4], BPS[4]), (PSB[5], BPS[5])]
            LPS = [(PSB[6], BPS[6]), (PSB[7], BPS[7])]
            AUXP = (PSB[3], BPS[3])

            for h in range(12):
                cqp, cqb = AUXP
                dg, dgb = dg_ring.next()
                OP("dve", "tensor_scalar", [Bconst, BLS], [dgb], out=dg[0:16, 0:16], in0=ident_f[0:16, 0:16], scalar1=LCSS[0:16, h:h + 1],
                   scalar2=0.0, op0=ALU.mult, op1=ALU.add)
                MM(cqp[:, 0:16], ones_f[0:16, :], dg[0:16, 0:16], True, True, [Bconst, dgb], [cqb])
                OP("act", "activation", [cqb], [BCQ[0]], out=CQ[0][:, 0:16], in_=cqp[:, 0:16], func=AF.Copy)
                kt_ap, ktb, vt_ap, vtb = kt_ring.next()
                DMA("pool", kt_ap, cfk[hf, h * 128:(h + 1) * 128, :], ktb, writes=[ktb])
                DMA("pool", vt_ap, cfv[hf][:, h * 128:(h + 1) * 128].rearrange("(t p) d -> p t d", p=128), vtb, writes=[vtb])
                kbt, kbtb = kbt_ring.next()
                OP("dve", "tensor_scalar", [BLS], [kbtb], out=kbt[:, 0, :], in0=LCS[:, nsrc, :, h], scalar1=-1.0,
                   scalar2=TOT[:, nsrc, h:h + 1], op0=ALU.mult, op1=ALU.add)
                o_, ob = OPS[0]
                l_, lb = LPS[0]
                pend = []
                for tl in range(9):
                    kk = 128 if tl < 8 else 16
                    sp_, spb = ps_s.next()
                    if tl < 8:
                        MM(sp_[:, 0:16], kt_ap[:, tl * 128:(tl + 1) * 128], QT[:, h, 1024:1040], True, True, [ktb, BQT(h, 1024)], [spb])
                    else:
                        MM(sp_[0:16, 0:16], KSN[:, h, :], QT[:, h, 1024:1040], True, True, [BKSN, BQT(h, 1024)], [spb])
                    OP("dve", "scalar_tensor_tensor", [spb, BCQ[0]], [spb], out=sp_[0:kk, 0:16], in0=sp_[0:kk, 0:16], scalar=SCALE,
                       in1=CQ[0][0:kk, 0:16], op0=ALU.mult, op1=ALU.add)
                    if tl == 8:
                        OP("dve", "tensor_tensor", [spb, Bconst], [spb], out=sp_[0:16, 0:16], in0=sp_[0:16, 0:16], in1=trimask[0:16, 0:16],
                           op=ALU.add)
                    pt, ptb = pt_ring.next()
                    bias_ap = kbt[:, 0, tl:tl + 1] if tl < 8 else NLCSS[0:16, h:h + 1]
                    OP("act", "activation", [spb, kbtb, BLS], [ptb], out=pt[0:kk, 0:16], in_=sp_[0:kk, 0:16], func=AF.Exp, bias=bias_ap,
                       scale=1.0)

                    def back(tl=tl, pt=pt, ptb=ptb):
                        if tl < 8:
                            MM(o_[:, 0:16], vt_ap[:, tl, :], pt[:, 0:16], tl == 0, False, [vtb, ptb], [ob])
                            MM(l_[:, 0:16], ones_bf[:], pt[:, 0:16], tl == 0, False, [Bconst, ptb], [lb])
                        else:
                            MM(o_[:, 0:16], VSN[0:16, h * 128:(h + 1) * 128], pt[0:16, 0:16], False, True, [BVSN, ptb], [ob])
                            MM(l_[:, 0:16], ones_bf[0:16, :], pt[0:16, 0:16], False, True, [Bconst, ptb], [lb])
                    pend.append(back)
                    while len(pend) > 2:
                        pend.pop(0)()
                while pend:
                    pend.pop(0)()
                rc, rcb = r32_ring.next()
                OP("dve", "reciprocal", [lb], [rcb], out=rc[:, 0:16], in_=l_[:, 0:16])
                OP("dve", "tensor_tensor", [ob, rcb], [BXN(h, 1024)], out=XN[:, h, 1024:1040], in0=o_[:, 0:16], in1=rc[:, 0:16], op=ALU.mult)

            SM["MKS"] = KT[0].rearrange("p (a b) -> p a b", a=4, b=256)
            SM["MVS"] = KT[1].rearrange("p (a b) -> p a b", a=2, b=512)
            SM["BMKS"] = BKT[0]; SM["BMVS"] = BKT[1]
            load_sample_mem(hf, 1)
            mem_attend(1)
            Bld2 = Buf("lgld2")
            for ci, (g, r) in enumerate(srcs):
                DMA("sp", LG[:, ci, :, :], lgga[g][r * 1024:(r + 1) * 1024, :].rearrange("(t p) h -> p t h", p=128), Bld2,
                    reads=[Blgga[g]], writes=[BLG])
            for c0 in range(0, nsrc, 4):
                chunk_cumsum(c0, min(nsrc, c0 + 4), BLG)

            def vt(ci, r):
                OP("dve", "tensor_tensor", [BLG, Bconst], [BLG], out=VTMP[:, r, :], in0=TOT[:, ci, :], in1=vmask[:, r, :], op=ALU.mult)
                return VTMP[:, r, :]
            last = None
            for ci in range(nsrc - 1, -1, -1):
                g, r = srcs[ci]
                masked = (hf == 0) or (g == 1)
                term = vt(ci, r) if masked else TOT[:, ci, :]
                if last is None:
                    OP("dve", "tensor_copy", [BLG], [BLG], out=GS[:, ci, :], in_=term)
                else:
                    OP("dve", "tensor_tensor", [BLG], [BLG], out=GS[:, ci, :], in0=term, in1=GS[:, last, :], op=ALU.add)
                last = ci
            for ci in range(nsrc):
                g, r = srcs[ci]
                masked = (hf == 0) or (g == 1)
                for qt in range(2):
                    OP("dve", "tensor_tensor", [BLG, BLO], [BLG], out=BASE[:, ci, qt, :], in0=GS[:, ci, :], in1=OFFO[:, 4 * qt, :], op=ALU.add)
                    if masked:
                        OP("dve", "tensor_tensor", [BLG, Bconst], [BLG], out=BASE[:, ci, qt, :], in0=BASE[:, ci, qt, :], in1=smask[:, r, :],
                           op=ALU.add)
            stop_at(6, hf)
            for h in range(12):
                for qt in range(2):
                    cqp, cqb = AUXP
                    for tt in range(4):
                        dg, dgb = dg_ring.next()
                        OP("dve", "tensor_scalar", [Bconst, BLO], [dgb], out=dg, in0=ident_f[:], scalar1=CQREL[:, 4 * qt + tt, h:h + 1],
                           scalar2=0.0, op0=ALU.mult, op1=ALU.add)
                        MM(cqp[:, tt * 128:(tt + 1) * 128], ones_f[:], dg, True, True, [Bconst, dgb], [cqb])
                    OP("act", "activation", [cqb], [BCQ[qt]], out=CQ[qt], in_=cqp[:, :], func=AF.Copy)
                first = [True, True]

                pending = []

                def flush(keep=0):
                    while len(pending) > keep:
                        pending.pop(0)()

                def tile_step(kt_ap, ktb, vt_ap, vtb, tl, qt, bias_ap, bias_bufs, c0, diag, lastflag):
                    sp_, spb = ps_s.next()
                    MM(sp_[:, c0:512], kt_ap[:, tl * 128:(tl + 1) * 128], QT[:, h, qt * 512 + c0:(qt + 1) * 512], True, True,
                       [ktb, BQT(h, qt * 512)], [spb])
                    OP("dve", "scalar_tensor_tensor", [spb, BCQ[qt]], [spb], out=sp_[:, c0:512], in0=sp_[:, c0:512], scalar=SCALE,
                       in1=CQ[qt][:, c0:512], op0=ALU.mult, op1=ALU.add)
                    if diag:
                        OP("dve", "tensor_tensor", [spb, Bconst], [spb], out=sp_[:, c0:c0 + 128], in0=sp_[:, c0:c0 + 128], in1=trimask[:],
                           op=ALU.add)
                    pt, ptb = pt_ring.next()
                    OP("act", "activation", [spb] + bias_bufs, [ptb], out=pt[:, c0:512], in_=sp_[:, c0:512], func=AF.Exp, bias=bias_ap,
                       scale=1.0)

                    def back():
                        o_, ob = OPS[qt]
                        l_, lb = LPS[qt]
                        MM(o_[:, c0:512], vt_ap[:, tl, :], pt[:, c0:512], first[qt], lastflag, [vtb, ptb], [ob])
                        MM(l_[:, c0:512], ones_bf[:], pt[:, c0:512], first[qt], lastflag, [Bconst, ptb], [lb])
                        first[qt] = False
                    pending.append(back)
                    flush(keep=2)

                for ci, (g, r) in enumerate(srcs):
                    kt_ap, ktb, vt_ap, vtb = kt_ring.next()
                    DMA("sp", kt_ap, kga[g][h // 4][r * 512 + (h % 4) * 128:r * 512 + (h % 4 + 1) * 128, :], ktb, reads=[Bgath[g](h // 4, 0)], writes=[ktb])
                    DMA("sp", vt_ap, vview(vga[g][h // 4], r)[:, (h % 4) * 128:(h % 4 + 1) * 128].rearrange("(t p) d -> p t d", p=128), vtb,
                        reads=[Bgath[g](h // 4, 1)], writes=[vtb])
                    kbt, kbtb = kbt_ring.next()
                    for qt in range(2):
                        OP("dve", "tensor_scalar", [BLG], [kbtb], out=kbt[:, qt, :], in0=LCS[:, ci, :, h], scalar1=-1.0,
                           scalar2=BASE[:, ci, qt, h:h + 1], op0=ALU.mult, op1=ALU.add)
                    for tl in range(8):
                        for qt in range(2):
                            tile_step(kt_ap, ktb, vt_ap, vtb, tl, qt, kbt[:, qt, tl:tl + 1], [kbtb], 0, False, False)
                kt_ap, ktb, vt_ap, vtb = kt_ring.next()
                DMA("sp", kt_ap, kst[hf][h // 4][(h % 4) * 128:(h % 4 + 1) * 128, :], ktb, reads=[Bgath[hf](h // 4, 0)], writes=[ktb])
                DMA("sp", vt_ap, vview(vst[hf][h // 4], 0)[:, (h % 4) * 128:(h % 4 + 1) * 128].rearrange("(t p) d -> p t d", p=128), vtb,
                    reads=[Bgath[hf](h // 4, 1)], writes=[vtb])
                for qt in range(2):
                    for tl in range(4 * qt):
                        tile_step(kt_ap, ktb, vt_ap, vtb, tl, qt, KBO[:, qt, tl, h:h + 1], [BLO], 0, False, False)
                    for j in range(4):
                        tl = 4 * qt + j
                        tile_step(kt_ap, ktb, vt_ap, vtb, tl, qt, KBO[:, qt, tl, h:h + 1], [BLO], 128 * j, True, j == 3)
                    flush()
                    o_, ob = OPS[qt]
                    l_, lb = LPS[qt]
                    rc, rcb = r32_ring.next()
                    OP("dve", "reciprocal", [lb], [rcb], out=rc[:, :], in_=l_[:, :])
                    OP("dve", "tensor_tensor", [ob, rcb], [BXN(h, qt * 512)], out=XN[:, h, qt * 512:(qt + 1) * 512], in0=o_[:, :],
                       in1=rc[:, :], op=ALU.mult)

            out_proj(1)
            p.barrier()
            stop_at(9, hf)
            mlp(1, PV_GMLP1)
            stop_at(10, hf)
            for (off, w) in TL:
                ss, ssb = ps_aux.next()
                for kc in range(16):
                    sq, sqb = sq_ring.next()
                    OP("act", "activation", [BY(kc, off)], [sqb], out=sq[:, :w], in_=Y[:, kc, off:off + w], func=AF.Square)
                    MM(ss[:, :w], ones_bf[:], sq[:, :w], kc == 0, kc == 15, [sqb, Bconst], [ssb])
                rs = MEAN_FIN[:, :w]
                OP("act", "activation", [ssb], [BFIN], out=rs, in_=ss[:, :w], func=AF.Sqrt, scale=1.0 / 2048.0, bias=EPS)
                OP("dve", "reciprocal", [BFIN], [BFIN], out=rs, in_=rs)
                for kc in range(16):
                    r, rb = r32_ring.next()
                    OP("dve", "scalar_tensor_tensor", [BY(kc, off), BFIN, Bconst], [rb], out=r[:, :w], in0=Y[:, kc, off:off + w],
                       scalar=pcol(PV_GFIN + kc), in1=rs, op0=ALU.mult, op1=ALU.mult)
                    if off < 1024:
                        DMA("sp", yp_o[hf, kc * 128:(kc + 1) * 128, off:off + w], r[:, :w], rb, reads=[rb], final=True)
                    else:
                        DMA("sp", ys_o[hf, kc * 128:(kc + 1) * 128, :], r[:, :w], rb, reads=[rb], final=True)

        def vview(ap2d, r):
            flat = ap2d[r * 512:(r + 1) * 512, :].rearrange("a b -> (a b)")
            return flat.rearrange("(t c) -> t c", c=512)

        MEAN_FIN = REGF[:, 0:512]
        BFIN = Buf("fin")

        p.dry = True
        try:
            emit()
        except StopEmit:
            pass
        p.dry = False
        wt_t = nc.dram_tensor("wt", [len(ws.uniq), 128, 4096], F32, kind="ExternalInput").ap()
        WT["ap"] = [wt_t[i] for i in range(len(ws.uniq))]
        try:
            emit()
        except StopEmit:
            pass
        p.build()
    nc._w_uniq = list(ws.uniq.keys())
    return nc


_NC = None


def kernel(x_prompt, x_sample, cache_mem_k, cache_mem_v, state_conv, cache_fox_k, cache_fox_v, cache_fox_logf,
           mem_prompt, g_mix, g_mem, w_mem_k, w_mem_v, w_in_conv, conv_w, conv_b, conv_ln_g, conv_ln_b,
           w_in_fox, b_fox_f, w_out, g_mlp, w_up, w_down, g_final):
    global _NC
    f32 = np.float32
    A = lambda a: np.ascontiguousarray(np.asarray(a, dtype=f32))
    x_prompt = A(x_prompt); x_sample = A(x_sample)
    if _NC is None:
        _NC = build_nc()
    nc = _NC

    def cols16(v):
        return np.asarray(v, f32).reshape(16, 128).T

    def cols12(v):
        return np.asarray(v, f32).reshape(12, 128).T

    pv = np.zeros((128, PV_N), f32)
    pv[:, PV_GMIX0:PV_GMIX0 + 16] = cols16(g_mix[0]); pv[:, PV_GMLP0:PV_GMLP0 + 16] = cols16(g_mlp[0])
    pv[:, PV_GMIX1:PV_GMIX1 + 16] = cols16(g_mix[1]); pv[:, PV_GMLP1:PV_GMLP1 + 16] = cols16(g_mlp[1])
    pv[:, PV_GFIN:PV_GFIN + 16] = cols16(g_final)
    pv[:, PV_GMEM0:PV_GMEM0 + 16] = cols16(g_mem[0]); pv[:, PV_GMEM1:PV_GMEM1 + 16] = cols16(g_mem[1])
    pv[:, PV_CB:PV_CB + 12] = cols12(conv_b[0]); pv[:, PV_LNG:PV_LNG + 12] = cols12(conv_ln_g[0]); pv[:, PV_LNB:PV_LNB + 12] = cols12(conv_ln_b[0])
    cw = np.asarray(conv_w[0], f32)
    for wi in range(31):
        pv[:, PV_CW + wi * 12:PV_CW + (wi + 1) * 12] = cols12(cw[wi])

    Wsrc = {"w_mem_k": A(w_mem_k), "w_mem_v": A(w_mem_v), "w_in_conv": A(w_in_conv), "w_in_fox": A(w_in_fox),
            "w_out": A(w_out), "w_up": A(w_up)}
    Wdown = A(w_down)
    uniq = nc._w_uniq
    wt = np.empty((len(uniq), 128, 4096), f32)
    for i, key in enumerate(uniq):
        if key[0] == "cols":
            W = Wsrc[key[1]][key[2]]
            off = 0
            for (c0, n) in key[3]:
                wt[i, :, off:off + 16 * n] = W[:, c0:c0 + n].reshape(16, 128, n).transpose(1, 0, 2).reshape(128, 16 * n)
                off += 16 * n
        else:
            _, layer, hg, dt_ = key
            wt[i] = Wdown[layer][hg * 512:(hg + 1) * 512, dt_ * 1024:(dt_ + 1) * 1024].reshape(4, 128, 1024).transpose(1, 0, 2).reshape(128, 4096)
    wf = Wsrc["w_in_fox"][0][:, 4608:4620].reshape(16, 128, 12).transpose(1, 0, 2).reshape(128, 192)
    shared = dict(pvec=pv, bfox=A(b_fox_f).reshape(1, 12), wt=wt, wf_d=np.ascontiguousarray(wf))
    in_maps = []
    for c in range(8):
        b, j = c // 4, c % 4
        xp = np.zeros((2, 2048, 1056), f32)
        xs = np.zeros((2, 2048, 16), f32)
        cst = np.zeros((2, 1536, 30), f32)
        cmk = np.zeros((2, 2, 512, 256), f32); cmv = np.zeros((2, 2, 256, 512), f32)
        cfk = np.zeros((2, 1536, 1024), f32); cfv = np.zeros((2, 1024, 1536), f32); cfl = np.zeros((2, 1024, 12), f32)
        for hf in range(2):
            ci = 4 * hf + j
            t0 = 1024 * ci
            xp[hf, :, 32:] = x_prompt[b, t0:t0 + 1024, :].T
            if ci > 0:
                xp[hf, :, 0:32] = x_prompt[b, t0 - 32:t0, :].T
            s = 2 * c + hf
            xs[hf] = x_sample[s].T
            cst[hf] = np.asarray(state_conv[0, s], f32).T
            for l in range(2):
                cmk[l, hf] = np.asarray(cache_mem_k[l, s], f32).transpose(1, 2, 0).reshape(512, 256)
                cmv[l, hf] = np.asarray(cache_mem_v[l, s], f32).reshape(256, 512)
            cfk[hf] = np.asarray(cache_fox_k[0, s], f32).transpose(1, 2, 0).reshape(1536, 1024)
            cfv[hf] = np.asarray(cache_fox_v[0, s], f32).reshape(1024, 1536)
            cfl[hf] = np.asarray(cache_fox_logf[0, s], f32)
        vm = np.zeros((1, 4, 12), f32); sm = np.zeros((1, 4, 12), f32)
        for i in range(4):
            vm[0, i, :] = 1.0 if i < j else 0.0
            sm[0, i, :] = 0.0 if i < j else -1.0e5
        m = dict(shared)
        m.update(xp=xp, xs=xs, memT=np.ascontiguousarray(np.asarray(mem_prompt[b], f32).T), vmask=vm.reshape(1, 48), smask=sm.reshape(1, 48),
                 convst=cst, cmk=cmk, cmv=cmv, cfk=cfk, cfv=cfv, cfl=cfl)
        if KSTOP == 1:
            for k in MINI_SKIP:
                m.pop(k)
        in_maps.append(m)

    res = run_bass_kernel_spmd(nc, in_maps, core_ids=list(range(8)))
    R = res.results

    y_prompt = np.zeros((2, 8192, 2048), f32); y_sample = np.zeros((16, 16, 2048), f32)
    new_mem_k = np.zeros((2, 2, 256, 4, 128), f32); new_mem_v = np.zeros((2, 2, 256, 4, 128), f32)
    conv_p = np.zeros((1, 2, 30, 1536), f32); conv_s = np.zeros((1, 16, 30, 1536), f32)
    fk_p = np.zeros((1, 2, 8192, 12, 128), f32); fv_p = np.zeros((1, 2, 8192, 12, 128), f32); fl_p = np.zeros((1, 2, 8192, 12), f32)
    fk_s = np.zeros((1, 16, 16, 12, 128), f32); fv_s = np.zeros((1, 16, 16, 12, 128), f32); fl_s = np.zeros((1, 16, 16, 12), f32)
    for c in range(8):
        b, j = c // 4, c % 4
        r = R[c]
        for hf in range(2):
            ci = 4 * hf + j
            t0 = 1024 * ci
            s = 2 * c + hf
            y_prompt[b, t0:t0 + 1024, :] = r["yp_o"][hf].T
            y_sample[s] = r["ys_o"][hf].T
            fk_p[0, b, t0:t0 + 1024] = r["fk_o"][hf].T.reshape(1024, 12, 128)
            fv_p[0, b, t0:t0 + 1024] = r["fv_o"][hf].reshape(1024, 12, 128)
            fl_p[0, b, t0:t0 + 1024] = r["fl_o"][hf]
            fk_s[0, s] = r["fks_o"][hf].T.reshape(16, 12, 128)
            fv_s[0, s] = r["fvs_o"][hf].reshape(16, 12, 128)
            fl_s[0, s] = r["fls_o"][hf]
            conv_s[0, s] = r["convs_o"][hf].T
            if ci == 7:
                conv_p[0, b] = r["convp_o"][hf].T
        if j == 0:
            for l in range(2):
                new_mem_k[l, b] = r["mk_o"][l].reshape(4, 128, 256).transpose(2, 0, 1)
                new_mem_v[l, b] = r["mv_o"][l].reshape(256, 4, 128)
    return (y_prompt, y_sample, new_mem_k, new_mem_v, conv_p, conv_s, fk_p, fv_p, fl_p, fk_s, fv_s, fl_s)
```

```python
import contextlib
import os
import numpy as np
import concourse.bass as bass
import concourse.mybir as mybir
from concourse.bass_utils import run_bass_kernel_spmd

F32 = mybir.dt.float32
BF16 = mybir.dt.bfloat16
AF = mybir.ActivationFunctionType
ALU = mybir.AluOpType
EPOCH = 20000
KSTOP = int(os.environ.get('KSTOP', '0'))
KSUB = int(os.environ.get('KSUB', '0'))
KNOCC = int(os.environ.get('KNOCC', '0'))


MINI_SKIP = ()


class StopEmit(Exception):
    pass
EPS = 1e-6
SCALE = 128 ** -0.5
NEG = -30000.0
TL = [(0, 512), (512, 512), (1024, 16)]
NT = 1040


class Rec:
    __slots__ = ("eng", "sem", "val", "flag")

    def __init__(self, eng):
        self.eng = eng
        self.sem = None
        self.val = None
        self.flag = False


class Buf:
    __slots__ = ("name", "w", "r", "dcount", "excl")

    def __init__(self, name="b", excl=False):
        self.name = name
        self.w = None
        self.r = {}
        self.dcount = 0
        self.excl = excl


class Prog:
    COMPUTE = ("pe", "act", "dve", "pool")
    ENGS = ("pe", "act", "dve", "pool", "sp")

    def __init__(self, nc):
        self.nc = nc
        self.dry = False
        self.stream = {e: [] for e in self.ENGS}
        self.dma_bufs = {}
        self.final_waits = []
        self.all_dma = {}
        self.nops = 0

    def _deps(self, eng, reads, writes, semkey=None):
        deps = []
        for b in reads:
            if b.w is not None:
                deps.append(b.w)
            if b.excl:
                deps.extend(r for k, r in b.r.items() if k != eng)
        for b in writes:
            if b.w is not None:
                if not (semkey is not None and b.w.sem == semkey):
                    deps.append(b.w)
            deps.extend(b.r.values())
        out = []
        seen = set()
        for d in deps:
            if id(d) in seen:
                continue
            seen.add(id(d))
            if d.eng == "pe" and eng == "pe":
                continue
            out.append(d)
        return out

    def op(self, eng, fn, reads=(), writes=()):
        if self.dry:
            return None
        deps = self._deps(eng, reads, writes)
        st = self.stream[eng]
        for d in deps:
            d.flag = True
            st.append(("wait", d))
        rec = Rec(eng)
        st.append(("op", fn, rec))
        for b in reads:
            b.r[eng] = rec
        for b in writes:
            b.w = rec
            b.r = {}
        self.nops += 1
        return rec

    def dma(self, queue, fn, owner, reads=(), writes=(), final=False):
        if self.dry:
            return None
        key = ("dma", id(owner))
        deps = self._deps("dma", reads, writes, semkey=key)
        st = self.stream[queue]
        for d in deps:
            d.flag = True
            st.append(("wait", d))
        rec = Rec("dma")
        owner.dcount += 1
        self.dma_bufs[id(owner)] = owner
        rec.sem = key
        rec.val = 16 * owner.dcount
        rec.flag = True
        st.append(("dma", fn, rec))
        for b in reads:
            b.r[key] = rec
        for b in writes:
            b.w = rec
            b.r = {}
        self.all_dma[key] = rec
        if final:
            self.final_waits.append(rec)
        return rec

    def barrier(self):
        if self.dry:
            return
        recs = []
        for e in self.COMPUTE:
            last = None
            for it in reversed(self.stream[e]):
                if it[0] == "op":
                    last = it[2]
                    break
            if last is not None:
                last.flag = True
                recs.append(last)
        recs.extend(self.all_dma.values())
        for e in self.ENGS:
            for r in recs:
                self.stream[e].append(("wait", r))

    def build(self):
        nc = self.nc
        nepoch = {}
        for e in self.COMPUTE:
            cnt = 0
            for it in self.stream[e]:
                if it[0] == "op" and it[2].flag:
                    rec = it[2]
                    ep = cnt // EPOCH
                    rec.sem = (e, ep)
                    rec.val = cnt - ep * EPOCH + 1
                    cnt += 1
            nepoch[e] = (cnt + EPOCH - 1) // EPOCH if cnt else 0
        for rec in self.final_waits:
            self.stream["sp"].append(("wait", rec))
        with contextlib.ExitStack() as es:
            sems = {}
            for e in self.COMPUTE:
                for ep in range(nepoch[e]):
                    sems[(e, ep)] = es.enter_context(nc.semaphore(f"c_{e}_{ep}"))
            for i, (k, b) in enumerate(self.dma_bufs.items()):
                sems[("dma", k)] = es.enter_context(nc.semaphore(f"d{i}"))
            print("PROG stats: sems", len(sems), "ops", {e: len(v) for e, v in self.stream.items()}, flush=True)
            block = es.enter_context(nc.Block())
            handles = {"pe": "tensor", "act": "scalar", "dve": "vector", "pool": "gpsimd", "sp": "sync"}

            def replay(engname):
                def run(eng):
                    waited = {}
                    for it in self.stream[engname]:
                        if it[0] == "wait":
                            d = it[1]
                            if waited.get(d.sem, 0) >= d.val:
                                continue
                            if d.eng != "dma" and d.sem[0] == engname and d.eng == "pe":
                                continue
                            waited[d.sem] = d.val
                            if d.eng != "dma":
                                for ep in range(d.sem[1]):
                                    waited[(d.sem[0], ep)] = EPOCH
                            eng.wait_ge(sems[d.sem], d.val)
                        else:
                            _, fn, rec = it
                            ins = fn(eng)
                            if rec.flag:
                                if rec.eng == "dma":
                                    ins.then_inc(sems[rec.sem], 16)
                                else:
                                    ins.then_inc(sems[rec.sem], 1)
                return run

            for engname in self.ENGS:
                getattr(block, handles[engname])(replay(engname))


class Ring:
    def __init__(self, items):
        self.items = items
        self.i = 0

    def next(self):
        it = self.items[self.i % len(self.items)]
        self.i += 1
        return it


class Grid:
    def __init__(self, name):
        self.name = name
        self.d = {}

    def __call__(self, *key):
        b = self.d.get(key)
        if b is None:
            b = Buf(self.name)
            self.d[key] = b
        return b

    def reset(self):
        self.d = {}


PV_GMIX0, PV_GMLP0, PV_GMIX1, PV_GMLP1, PV_GFIN, PV_GMEM0, PV_GMEM1 = 0, 16, 32, 48, 64, 80, 96
PV_CB, PV_LNG, PV_LNB, PV_CW = 112, 124, 136, 148
PV_N = 148 + 31 * 12


def build_nc():
    nc = bass.Bass("TRN2", target_bir_lowering=False)

    def din(name, shape):
        if KSTOP == 1 and name in MINI_SKIP:
            return nc.dram_tensor(name, list(shape), F32).ap()
        return nc.dram_tensor(name, list(shape), F32, kind="ExternalInput").ap()

    def dout(name, shape):
        return nc.dram_tensor(name, list(shape), F32, kind="ExternalOutput").ap()

    xp = din("xp", [2, 2048, 1056]); xs = din("xs", [2, 2048, 16]); memT = din("memT", [2048, 256])
    pvec_d = din("pvec", [128, PV_N]); bfox_d = din("bfox", [1, 12])
    vmask_d = din("vmask", [1, 48]); smask_d = din("smask", [1, 48])
    convst = din("convst", [2, 1536, 30])
    cmk = din("cmk", [2, 2, 512, 256]); cmv = din("cmv", [2, 2, 256, 512])
    cfk = din("cfk", [2, 1536, 1024]); cfv = din("cfv", [2, 1024, 1536]); cfl = din("cfl", [2, 1024, 12])
    wf_d = din("wf_d", [128, 192])
    w_mem_k = ("w_mem_k", 0), ("w_mem_k", 1)
    w_mem_v = ("w_mem_v", 0), ("w_mem_v", 1)
    w_in_conv = ("w_in_conv", 0)
    w_in_fox = ("w_in_fox", 0)
    w_out = ("w_out", 0), ("w_out", 1)
    w_up = ("w_up", 0), ("w_up", 1)
    WT = {}

    yp_o = dout("yp_o", [2, 2048, 1024]); ys_o = dout("ys_o", [2, 2048, 16])
    mk_o = dout("mk_o", [2, 512, 256]); mv_o = dout("mv_o", [2, 256, 512])
    convp_o = dout("convp_o", [2, 1536, 30]); convs_o = dout("convs_o", [2, 1536, 30])
    fk_o = dout("fk_o", [2, 1536, 1024]); fv_o = dout("fv_o", [2, 1024, 1536]); fl_o = dout("fl_o", [2, 1024, 12])
    fks_o = dout("fks_o", [2, 1536, 16]); fvs_o = dout("fvs_o", [2, 16, 1536]); fls_o = dout("fls_o", [2, 16, 12])

    KVN = 3072 * 1024
    kst = [[nc.dram_tensor(f"kst{h}_{q}", [512, 1024], BF16).ap() for q in range(3)] for h in range(2)]
    vst = [[nc.dram_tensor(f"vst{h}_{q}", [512, 1024], BF16).ap() for q in range(3)] for h in range(2)]
    kga = [[nc.dram_tensor(f"kga{h}_{q}", [2048, 1024], BF16).ap() for q in range(3)] for h in range(2)]
    vga = [[nc.dram_tensor(f"vga{h}_{q}", [2048, 1024], BF16).ap() for q in range(3)] for h in range(2)]
    lgst = [nc.dram_tensor(f"lgst{h}", [1024, 12], F32).ap() for h in range(2)]
    lgga = [nc.dram_tensor(f"lgga{h}", [4096, 12], F32).ap() for h in range(2)]

    p = Prog(nc)
    es = contextlib.ExitStack()
    with es:
        def sb(name, shape, dt):
            return es.enter_context(nc.sbuf_tensor("s_" + name, list(shape), dt))

        Y = sb("Y", [128, 16, NT], F32)
        XN = sb("XN", [128, 16, NT], BF16)
        QX = sb("QX", [128, 4, NT], BF16)
        NSLOT = 3
        WS = [sb(f"ws{i}", [128, 4096], BF16) for i in range(NSLOT)]
        MK = sb("MK", [128, 2, 4, 256], BF16)
        MV = sb("MV", [128, 2, 2, 512], BF16)
        ones_bf = sb("ones_bf", [128, 128], BF16)
        ones_f = sb("ones_f", [128, 128], F32)
        tri_f = sb("tri_f", [128, 128], F32)
        ident_f = sb("ident_f", [128, 128], F32)
        trimask = sb("trimask", [128, 128], F32)
        pvec = sb("pvec", [128, PV_N], F32)
        bfox = sb("bfox", [128, 12], F32)
        vmask = sb("vmask", [128, 4, 12], F32)
        smask = sb("smask", [128, 4, 12], F32)
        Wf = sb("Wf", [128, 16, 12], BF16)
        SQ = [sb(f"sq{i}", [128, 512], BF16) for i in range(3)]
        R32 = [sb(f"r32_{i}", [128, 512], F32) for i in range(3)]
        NRB = 19840
        NRF = 4224
        REGB = sb("REGB", [128, NRB], BF16)
        REGF = sb("REGF", [128, NRF], F32)
        PSB = [es.enter_context(nc.psum_tensor(f"ps{i}", [128, 512], F32)) for i in range(8)]

        Bconst = Buf("const")
        BY = Grid("Y"); BXN = Grid("XN"); BQX = Grid("QX")
        BWS = [Buf(f"ws{i}") for i in range(NSLOT)]
        BMK = Grid("MK"); BMV = Grid("MV"); SM = {}
        BSQ = [Buf("sq") for _ in SQ]; BR32 = [Buf("r32") for _ in R32]
        BPS = [Buf(f"ps{i}", excl=True) for i in range(8)]
        sq_ring = Ring(list(zip(SQ, BSQ)))
        r32_ring = Ring(list(zip(R32, BR32)))
        ps_main = Ring(list(zip(PSB[0:4], BPS[0:4])))
        ps_aux = Ring(list(zip(PSB[4:6], BPS[4:6])))
        Bstage = [[], []]
        Bgath = [Grid("kga0"), Grid("kga1")]
        Blgst = [Buf("lgst0"), Buf("lgst1")]
        Blgga = [Buf("lgga0"), Buf("lgga1")]
        Bout = Grid("out")

        def OP(eng, name, reads, writes, **kw):
            return p.op(eng, lambda e: getattr(e, name)(**kw), reads, writes)

        def DMA(q, out, in_, owner, reads=(), writes=(), final=False):
            return p.dma(q, lambda e: e.dma_start(out=out, in_=in_), owner, reads, writes, final)

        def MM(out, lhsT, rhs, start, stop, reads, writes):
            return p.op("pe", lambda e: e.matmul(out, lhsT=lhsT, rhs=rhs, start=start, stop=stop), reads, writes)

        def pcol(c):
            return pvec[:, c:c + 1]

        class WStream:
            def __init__(self):
                self.order = []
                self.uniq = {}
                self.pos = 0
                self.issued = 0

            def reset(self):
                self.pos = 0
                self.issued = 0

            def _issue(self, i):
                s_ = i % NSLOT
                tid = self.uniq[self.order[i]]
                DMA("pool", WS[s_][:, :], WT["ap"][tid], BWS[s_], writes=[BWS[s_]])

            def next(self, key):
                if p.dry:
                    self.order.append(key)
                    self.uniq.setdefault(key, len(self.uniq))
                    return WS[0], BWS[0]
                while self.issued < min(len(self.order), self.pos + NSLOT):
                    self._issue(self.issued)
                    self.issued += 1
                s_ = self.pos % NSLOT
                self.pos += 1
                return WS[s_], BWS[s_]

        ws = WStream()

        def wtile_cols(wref, pieces):
            return ("cols", wref[0], wref[1], tuple(pieces)), sum(n for _, n in pieces)

        def rmsnorm(src, dst, gcol0, tiles, nkc=16, denom=2048.0):
            for (off, w) in tiles:
                ss, ssb = ps_aux.next()
                for kc in range(nkc):
                    sq, sqb = sq_ring.next()
                    a, ab = src(kc, off, w)
                    OP("act", "activation", [ab], [sqb], out=sq[:, :w], in_=a, func=AF.Square)
                    MM(ss[:, :w], ones_bf[:], sq[:, :w], kc == 0, kc == nkc - 1, [sqb, Bconst], [ssb])
                rs, rsb = r32_ring.next()
                OP("act", "activation", [ssb], [rsb], out=rs[:, :w], in_=ss[:, :w], func=AF.Sqrt, scale=1.0 / denom, bias=EPS)
                OP("dve", "reciprocal", [rsb], [rsb], out=rs[:, :w], in_=rs[:, :w])
                for kc in range(nkc):
                    a, ab = src(kc, off, w)
                    d, db = dst(kc, off, w)
                    OP("dve", "scalar_tensor_tensor", [ab, rsb, Bconst], [db], out=d, in0=a, scalar=pcol(gcol0 + kc),
                       in1=rs[:, :w], op0=ALU.mult, op1=ALU.mult)

        def srcY(kc, off, w):
            return Y[:, kc, off:off + w], BY(kc, off)

        def dstXN(kc, off, w):
            return XN[:, kc, off:off + w], BXN(kc, off)

        def mem_attend(layer, prompt_tiles=True):
            for h in range(4):
                for (off, w) in TL:
                    smp = off >= 1024
                    pts = []
                    for blk in range(2):
                        sp_, spb = ps_main.next()
                        if smp:
                            lh, lb = SM["MKS"][:, h, blk * 128:(blk + 1) * 128], SM["BMKS"]
                        else:
                            lh, lb = MK[:, layer, h, blk * 128:(blk + 1) * 128], BMK(layer)
                        MM(sp_[:, :w], lh, QX[:, h, off:off + w], True, True, [lb, BQX(h, off)], [spb])
                        pt, ptb = sq_ring.next()
                        OP("act", "activation", [spb], [ptb], out=pt[:, :w], in_=sp_[:, :w], func=AF.Exp, scale=SCALE)
                        pts.append((pt, ptb))
                    o_, ob = ps_aux.next()
                    l_, lb2 = ps_aux.next()
                    for blk in range(2):
                        pt, ptb = pts[blk]
                        if smp:
                            vh, vb = SM["MVS"][:, blk, h * 128:(h + 1) * 128], SM["BMVS"]
                        else:
                            vh, vb = MV[:, layer, blk, h * 128:(h + 1) * 128], BMV(layer)
                        MM(o_[:, :w], vh, pt[:, :w], blk == 0, blk == 1, [vb, ptb], [ob])
                    for blk in range(2):
                        pt, ptb = pts[blk]
                        MM(l_[:, :w], ones_bf[:], pt[:, :w], blk == 0, blk == 1, [Bconst, ptb], [lb2])
                    rc, rcb = r32_ring.next()
                    OP("dve", "reciprocal", [lb2], [rcb], out=rc[:, :w], in_=l_[:, :w])
                    OP("dve", "tensor_tensor", [ob, rcb], [BXN(12 + h, off)], out=XN[:, 12 + h, off:off + w], in0=o_[:, :w],
                       in1=rc[:, :w], op=ALU.mult)

        def load_sample_mem(hf, layer):
            DMA("pool", SM["MKS"], cmk[layer, hf].rearrange("(h d) m -> d h m", d=128), SM["BMKS"], writes=[SM["BMKS"]])
            DMA("pool", SM["MVS"], cmv[layer, hf].rearrange("(b m) c -> m b c", m=128), SM["BMVS"], writes=[SM["BMVS"]])

        def out_proj(layer):
            for wt in range(8):
                desc, _ = wtile_cols(w_out[layer], [(wt * 256, 256)])
                slot, sbuf_ = ws.next(desc)
                sv = slot[:, :].rearrange("p (a b) -> p a b", a=16, b=256)
                for mc in range(2):
                    m = wt * 2 + mc
                    for (off, w) in TL:
                        pb, pbb = ps_main.next()
                        for kc in range(16):
                            MM(pb[:, :w], sv[:, kc, mc * 128:(mc + 1) * 128], XN[:, kc, off:off + w], kc == 0, kc == 15,
                               [sbuf_, BXN(kc, off)], [pbb])
                        OP("dve", "tensor_tensor", [pbb, BY(m, off)], [BY(m, off)], out=Y[:, m, off:off + w], in0=pb[:, :w],
                           in1=Y[:, m, off:off + w], op=ALU.add)

        def mlp(layer, gcol0):
            rmsnorm(srcY, dstXN, gcol0, TL)
            H = REGB[:, 0:4 * NT].rearrange("p (a b) -> p a b", a=4, b=NT)
            BH = Grid("H")
            for hg in range(16):
                for ut in range(2):
                    desc, _ = wtile_cols(w_up[layer], [(hg * 512 + ut * 256, 256)])
                    slot, sbuf_ = ws.next(desc)
                    sv = slot[:, :].rearrange("p (a b) -> p a b", a=16, b=256)
                    for mc in range(2):
                        hc = ut * 2 + mc
                        for (off, w) in TL:
                            pb, pbb = ps_main.next()
                            for kc in range(16):
                                MM(pb[:, :w], sv[:, kc, mc * 128:(mc + 1) * 128], XN[:, kc, off:off + w], kc == 0, kc == 15,
                                   [sbuf_, BXN(kc, off)], [pbb])
                            r, rb = r32_ring.next()
                            OP("act", "activation", [pbb], [rb], out=r[:, :w], in_=pb[:, :w], func=AF.Relu)
                            OP("dve", "tensor_tensor", [rb], [BH(hc, off)], out=H[:, hc, off:off + w], in0=r[:, :w], in1=r[:, :w],
                               op=ALU.mult)
                for dt_ in range(2):
                    slot, sbuf_ = ws.next(("down", layer, hg, dt_))
                    sv = slot[:, :].rearrange("p (a b) -> p a b", a=4, b=1024)
                    for mc in range(8):
                        m = dt_ * 8 + mc
                        for (off, w) in TL:
                            pb, pbb = ps_main.next()
                            for kc in range(4):
                                MM(pb[:, :w], sv[:, kc, mc * 128:(mc + 1) * 128], H[:, kc, off:off + w], kc == 0, kc == 3,
                                   [sbuf_, BH(kc, off)], [pbb])
                            OP("dve", "tensor_tensor", [pbb, BY(m, off)], [BY(m, off)], out=Y[:, m, off:off + w], in0=pb[:, :w],
                               in1=Y[:, m, off:off + w], op=ALU.add)

        def stop_at(n, hf=0, dump=True):
            if KSTOP == n and hf == 0:
                if dump:
                    Bd = Buf("dump")
                    DMA("sp", yp_o[hf].rearrange("(kc p) w -> p kc w", p=128), Y[:, :, 0:1024], Bd,
                        reads=[BY(kc, off) for kc in range(16) for off in (0, 512)], final=True)
                    DMA("sp", ys_o[hf].rearrange("(kc p) w -> p kc w", p=128), Y[:, :, 1024:1040], Bd,
                        reads=[BY(kc, 1024) for kc in range(16)], final=True)
                raise StopEmit()

        def emit():
            ws.reset()
            for g in (BY, BXN, BQX, BMK, BMV, Bout, Bgath[0], Bgath[1]):
                g.reset()
            Bstage[0] = []
            Bstage[1] = []
            OP("pool", "memset", [], [Bconst], ap=ones_bf[:], constant=1.0)
            OP("pool", "memset", [], [Bconst], ap=ones_f[:], constant=1.0)
            OP("pool", "memset", [], [Bconst], ap=tri_f[:], constant=1.0)
            OP("pool", "affine_select", [Bconst], [Bconst], out=tri_f[:], in_=tri_f[:], pattern=[[1, 128]], compare_op=ALU.is_ge,
               fill=0.0, base=0, channel_multiplier=-1)
            OP("pool", "memset", [], [Bconst], ap=ident_f[:], constant=1.0)
            OP("pool", "affine_select", [Bconst], [Bconst], out=ident_f[:], in_=ident_f[:], pattern=[[1, 128]], compare_op=ALU.is_equal,
               fill=0.0, base=0, channel_multiplier=-1)
            OP("pool", "memset", [], [Bconst], ap=trimask[:], constant=0.0)
            OP("pool", "affine_select", [Bconst], [Bconst], out=trimask[:], in_=trimask[:], pattern=[[1, 128]], compare_op=ALU.is_ge,
               fill=NEG, base=0, channel_multiplier=-1)
            Bpv = Buf("pv")
            DMA("sp", pvec[:], pvec_d, Bpv, writes=[Bconst])
            DMA("sp", bfox[:], bfox_d.partition_broadcast(128), Bpv, writes=[Bconst])
            DMA("sp", vmask[:].rearrange("p a b -> p (a b)"), vmask_d.partition_broadcast(128), Bpv, writes=[Bconst])
            DMA("sp", smask[:].rearrange("p a b -> p (a b)"), smask_d.partition_broadcast(128), Bpv, writes=[Bconst])
            DMA("pool", Wf[:].rearrange("p a b -> p (a b)"), wf_d, Bpv, writes=[Bconst])

            if KSUB == 1:
                raise StopEmit()
            DMA("sp", Y[:, :, 0:256], memT.rearrange("(kc p) m -> p kc m", p=128), BY("mem"), writes=[BY(kc, 0) for kc in range(16)])
            for layer in range(2):
                rmsnorm(srcY, dstXN, PV_GMEM0 + 16 * layer, [(0, 256)])
                if KSUB == 2:
                    raise StopEmit()
                for wt in range(2):
                    desc, _ = wtile_cols(w_mem_k[layer], [(wt * 256, 256)])
                    slot, sbuf_ = ws.next(desc)
                    sv = slot[:, :].rearrange("p (a b) -> p a b", a=16, b=256)
                    for mc in range(2):
                        h = wt * 2 + mc
                        pb, pbb = ps_main.next()
                        for kc in range(16):
                            MM(pb[:, :256], sv[:, kc, mc * 128:(mc + 1) * 128], XN[:, kc, 0:256], kc == 0, kc == 15,
                               [sbuf_, BXN(kc, 0)], [pbb])
                        OP("act", "copy", [pbb], [BMK(layer)], out=MK[:, layer, h, :], in_=pb[:, :256])
                        r, rb = r32_ring.next()
                        OP("dve", "tensor_copy", [pbb], [rb], out=r[:, :256], in_=pb[:, :256])
                        DMA("sp", mk_o[layer, h * 128:(h + 1) * 128, :], r[:, :256], rb, reads=[rb], final=True)
                for wt in range(2):
                    desc, _ = wtile_cols(w_mem_v[layer], [(wt * 256, 256)])
                    slot, sbuf_ = ws.next(desc)
                    sv = slot[:, :].rearrange("p (a b) -> p a b", a=16, b=256)
                    for blk in range(2):
                        pb, pbb = ps_main.next()
                        for kc in range(16):
                            MM(pb[:, :256], XN[:, kc, blk * 128:(blk + 1) * 128], sv[:, kc, :], kc == 0, kc == 15,
                               [sbuf_, BXN(kc, 0)], [pbb])
                        OP("act", "copy", [pbb], [BMV(layer)], out=MV[:, layer, blk, wt * 256:(wt + 1) * 256], in_=pb[:, :256])
                        r, rb = r32_ring.next()
                        OP("dve", "tensor_copy", [pbb], [rb], out=r[:, :256], in_=pb[:, :256])
                        DMA("sp", mv_o[layer, blk * 128:(blk + 1) * 128, wt * 256:(wt + 1) * 256], r[:, :256], rb, reads=[rb], final=True)

            stop_at(1, dump=False)
            for hf in range(2):
                emit_half(hf)

        def emit_half(hf):
            p.barrier()
            CB = REGB[:, 0:12 * NT].rearrange("p (a b) -> p a b", a=12, b=NT)
            XHN = REGB[:, 12 * NT:12 * NT + 512].rearrange("p (a b) -> p a b", a=16, b=32)
            UH = [REGF[:, i * 1056:(i + 1) * 1056] for i in range(2)]
            UHS = REGF[:, 2112:2112 + 552].rearrange("p (a b) -> p a b", a=12, b=46)
            ACC = REGF[:, 2664:2664 + 1024]
            ACCS = REGF[:, 3688:3688 + 16]
            XH = REGF[:, 3704:3704 + 512].rearrange("p (a b) -> p a b", a=16, b=32)
            BCB = Grid("CB"); BXHN = Grid("XHN"); BUH = [Buf("uh0"), Buf("uh1")]; BUHS = Grid("UHS")
            BACC_A = Buf("acca"); BACC_B = Buf("accb"); BACC2 = [BACC_A, BACC_B]; BACCS = Buf("accs"); BXH = Grid("XH")
            uh_ring = Ring(list(zip(UH, BUH)))
            SM["MKS"] = REGB[:, 12992:14016].rearrange("p (a b) -> p a b", a=4, b=256)
            SM["MVS"] = REGB[:, 14016:15040].rearrange("p (a b) -> p a b", a=2, b=512)
            SM["BMKS"] = Buf("mks"); SM["BMVS"] = Buf("mvs")

            xv = xp[hf].rearrange("(kc p) w -> p kc w", p=128)
            DMA("sp", XH[:, :, :], xv[:, :, 0:32], BXH("ld"), writes=[BXH(kc) for kc in range(16)])
            for q4 in range(4):
                for (off, w) in TL[:2]:
                    DMA("sp", Y[:, q4 * 4:(q4 + 1) * 4, off:off + w], xv[:, q4 * 4:(q4 + 1) * 4, 32 + off:32 + off + w], BY("ld", q4, off),
                        writes=[BY(kc, off) for kc in range(q4 * 4, q4 * 4 + 4)])
            DMA("sp", Y[:, :, 1024:1040], xs[hf].rearrange("(kc p) w -> p kc w", p=128), BY("lds"),
                writes=[BY(kc, 1024) for kc in range(16)])
            DMA("sp", UHS[:, :, 0:30], convst[hf].rearrange("(m p) w -> p m w", p=128), BUHS("ld"), writes=[BUHS(m) for m in range(12)])
            load_sample_mem(hf, 0)

            rmsnorm(srcY, dstXN, PV_GMIX0, TL)
            rmsnorm(lambda kc, off, w: (XH[:, kc, off:off + w], BXH(kc)), lambda kc, off, w: (XHN[:, kc, off:off + w], BXHN(kc)),
                    PV_GMIX0, [(0, 32)])
            for m in range(12):
                desc, _ = wtile_cols(w_in_conv, [(m * 128, 128), (1536 + m * 128, 128)])
                slot, sbuf_ = ws.next(desc)
                sva = slot[:, 0:2048].rearrange("p (a b) -> p a b", a=16, b=128)
                svg = slot[:, 2048:4096].rearrange("p (a b) -> p a b", a=16, b=128)
                uh, uhb = uh_ring.next()
                for ti in range(4):
                    if ti == 0:
                        w = 32
                        rhs = lambda kc: (XHN[:, kc, 0:32], BXHN(kc))
                        dst, dstb = uh[:, 0:32], uhb
                    else:
                        off, w = TL[ti - 1]
                        rhs = (lambda off, w: (lambda kc: (XN[:, kc, off:off + w], BXN(kc, off))))(off, w)
                        if ti < 3:
                            dst, dstb = uh[:, 32 + off:32 + off + w], uhb
                        else:
                            dst, dstb = UHS[:, m, 30:46], BUHS(m)
                    pa, pab = ps_main.next()
                    pg, pgb = ps_main.next()
                    for kc in range(16):
                        r_, rb_ = rhs(kc)
                        MM(pa[:, :w], sva[:, kc, :], r_, kc == 0, kc == 15, [sbuf_, rb_], [pab])
                    for kc in range(16):
                        r_, rb_ = rhs(kc)
                        MM(pg[:, :w], svg[:, kc, :], r_, kc == 0, kc == 15, [sbuf_, rb_], [pgb])
                    sg, sgb = r32_ring.next()
                    OP("act", "activation", [pgb], [sgb], out=sg[:, :w], in_=pg[:, :w], func=AF.Sigmoid)
                    OP("dve", "tensor_tensor", [pab, sgb], [dstb], out=dst, in0=pa[:, :w], in1=sg[:, :w], op=ALU.mult)
                cwc = lambda wi: pcol(PV_CW + wi * 12 + m)
                chains = [(ACC[:, 0:512], BACC_A, lambda wi: uh[:, 2 + wi:514 + wi], uhb),
                          (ACC[:, 512:1024], BACC_B, lambda wi: uh[:, 514 + wi:1026 + wi], uhb),
                          (ACCS, BACCS, lambda wi: UHS[:, m, wi:wi + 16], BUHS(m))]
                for wi in range(31):
                    for (acc_, accb_, src_, srcb_) in chains:
                        if wi == 0:
                            OP("dve", "tensor_scalar", [srcb_, Bconst], [accb_], out=acc_, in0=src_(0), scalar1=cwc(0), scalar2=pcol(PV_CB + m),
                               op0=ALU.mult, op1=ALU.add)
                        else:
                            OP("dve", "scalar_tensor_tensor", [srcb_, accb_, Bconst], [accb_], out=acc_, in0=src_(wi), scalar=cwc(wi),
                               in1=acc_, op0=ALU.mult, op1=ALU.add)
                OP("act", "copy", [BACC_A], [BCB(m, 0)], out=CB[:, m, 0:512], in_=ACC[:, 0:512])
                OP("act", "copy", [BACC_B], [BCB(m, 0)], out=CB[:, m, 512:1024], in_=ACC[:, 512:1024])
                DMA("sp", convp_o[hf, m * 128:(m + 1) * 128, :], uh[:, 1026:1056], uhb, reads=[uhb], final=True)
                OP("act", "copy", [BACCS], [BCB(m, 1024)], out=CB[:, m, 1024:1040], in_=ACCS)
            DMA("sp", convs_o[hf].rearrange("(m p) w -> p m w", p=128), UHS[:, :, 16:46], BUHS("st"),
                reads=[BUHS(m) for m in range(12)], final=True)
            for wt in range(2):
                desc, _ = wtile_cols(w_in_conv, [(3072 + wt * 256, 256)])
                slot, sbuf_ = ws.next(desc)
                sv = slot[:, :].rearrange("p (a b) -> p a b", a=16, b=256)
                for mc in range(2):
                    h = wt * 2 + mc
                    for (off, w) in TL:
                        pb, pbb = ps_main.next()
                        for kc in range(16):
                            MM(pb[:, :w], sv[:, kc, mc * 128:(mc + 1) * 128], XN[:, kc, off:off + w], kc == 0, kc == 15,
                               [sbuf_, BXN(kc, off)], [pbb])
                        OP("act", "copy", [pbb], [BQX(h, off)], out=QX[:, h, off:off + w], in_=pb[:, :w])
            stop_at(2, hf)
            MEANB = ACC[:, 0:512]
            RSTDB = ACC[:, 512:1024]
            for (off, w) in TL:
                cboff = 0 if off < 1024 else 1024
                s_, sb_ = ps_aux.next()
                q_, qb_ = ps_aux.next()
                for m in range(12):
                    MM(s_[:, :w], ones_bf[:], CB[:, m, off:off + w], m == 0, m == 11, [Bconst, BCB(m, cboff)], [sb_])
                for m in range(12):
                    sq, sqb = sq_ring.next()
                    OP("act", "activation", [BCB(m, cboff)], [sqb], out=sq[:, :w], in_=CB[:, m, off:off + w], func=AF.Square)
                    MM(q_[:, :w], ones_bf[:], sq[:, :w], m == 0, m == 11, [Bconst, sqb], [qb_])
                OP("dve", "tensor_scalar", [sb_, *BACC2], [*BACC2], out=MEANB[:, :w], in0=s_[:, :w], scalar1=1.0 / 1536, scalar2=0.0,
                   op0=ALU.mult, op1=ALU.add)
                t_, tb_ = r32_ring.next()
                OP("dve", "tensor_tensor", [*BACC2], [tb_], out=t_[:, :w], in0=MEANB[:, :w], in1=MEANB[:, :w], op=ALU.mult)
                OP("dve", "scalar_tensor_tensor", [qb_, tb_, *BACC2], [*BACC2], out=RSTDB[:, :w], in0=q_[:, :w], scalar=1.0 / 1536, in1=t_[:, :w],
                   op0=ALU.mult, op1=ALU.subtract)
                OP("act", "activation", [*BACC2], [*BACC2], out=RSTDB[:, :w], in_=RSTDB[:, :w], func=AF.Sqrt, bias=EPS, scale=1.0)
                OP("dve", "reciprocal", [*BACC2], [*BACC2], out=RSTDB[:, :w], in_=RSTDB[:, :w])
                for m in range(12):
                    t_, tb_ = r32_ring.next()
                    OP("dve", "tensor_tensor", [BCB(m, cboff), *BACC2], [tb_], out=t_[:, :w], in0=CB[:, m, off:off + w], in1=MEANB[:, :w],
                       op=ALU.subtract)
                    OP("dve", "tensor_tensor", [tb_, *BACC2], [tb_], out=t_[:, :w], in0=t_[:, :w], in1=RSTDB[:, :w], op=ALU.mult)
                    OP("act", "activation", [tb_, Bconst], [BXN(m, off)], out=XN[:, m, off:off + w], in_=t_[:, :w], func=AF.Silu,
                       scale=pcol(PV_LNG + m), bias=pcol(PV_LNB + m))
            mem_attend(0)
            out_proj(0)
            stop_at(3, hf)
            p.barrier()
            mlp(0, PV_GMLP0)
            stop_at(4, hf)

            p.barrier()
            QT = REGB[:, 0:12 * NT].rearrange("p (a b) -> p a b", a=12, b=NT)
            o_ = 12 * NT
            KT = [REGB[:, o_ + i * 1024:o_ + (i + 1) * 1024] for i in range(2)]; o_ += 2048
            VT = [REGB[:, o_ + i * 1024:o_ + (i + 1) * 1024].rearrange("p (a b) -> p a b", a=8, b=128) for i in range(2)]; o_ += 2048
            PT = [REGB[:, o_ + i * 512:o_ + (i + 1) * 512] for i in range(3)]; o_ += 1536
            KSN = REGB[:, o_:o_ + 192].rearrange("p (a b) -> p a b", a=12, b=16); o_ += 192
            VSN = REGB[:, o_:o_ + 1536]; o_ += 1536
            assert o_ <= NRB, o_
            f_ = 0
            CQ = [REGF[:, f_ + i * 512:f_ + (i + 1) * 512] for i in range(2)]; f_ += 1024
            NCH = 4 if hf == 0 else 8
            LCS = REGF[:, f_:f_ + NCH * 96].rearrange("p (c t h) -> p c t h", c=NCH, t=8, h=12); f_ += 768
            OFF = REGF[:, f_:f_ + NCH * 96].rearrange("p (c t h) -> p c t h", c=NCH, t=8, h=12); f_ += 768
            TOT = REGF[:, f_:f_ + NCH * 12].rearrange("p (c h) -> p c h", c=NCH, h=12); f_ += 96
            GS = REGF[:, f_:f_ + 96].rearrange("p (c h) -> p c h", c=8, h=12); f_ += 96
            BASE = REGF[:, f_:f_ + 192].rearrange("p (c q h) -> p c q h", c=8, q=2, h=12); f_ += 192
            LGO = REGF[:, f_:f_ + 96].rearrange("p (t h) -> p t h", t=8, h=12); f_ += 96
            OFFO = REGF[:, f_:f_ + 108].rearrange("p (t h) -> p t h", t=9, h=12); f_ += 108
            CQREL = REGF[:, f_:f_ + 96].rearrange("p (t h) -> p t h", t=8, h=12); f_ += 96
            KBO = REGF[:, f_:f_ + 192].rearrange("p (q t h) -> p q t h", q=2, t=8, h=12); f_ += 192
            LGS = REGF[:, f_:f_ + 12]; f_ += 12
            LCSS = REGF[:, f_:f_ + 12]; f_ += 12
            NLCSS = REGF[:, f_:f_ + 12]; f_ += 12
            FB = REGF[:, f_:f_ + 12]; f_ += 12
            KBT = [REGF[:, f_ + i * 16:f_ + (i + 1) * 16].rearrange("p (q t) -> p q t", q=2, t=8) for i in range(2)]; f_ += 32
            DG = [REGF[:, f_ + i * 128:f_ + (i + 1) * 128] for i in range(2)]; f_ += 256
            VTMP = REGF[:, f_:f_ + 48].rearrange("p (c h) -> p c h", c=4, h=12); f_ += 48
            assert f_ <= NRF, f_
            BQT = Grid("QT"); BKT = [Buf("kt0"), Buf("kt1")]; BVT = [Buf("vt0"), Buf("vt1")]
            BPT = [Buf("pt") for _ in PT]; BCQ = [Buf("cq0"), Buf("cq1")]
            BL = Buf("logf"); BKSN = Buf("ksn"); BVSN = Buf("vsn"); BKBT = [Buf("kbt0"), Buf("kbt1")]; BDG = [Buf("dg0"), Buf("dg1")]
            kt_ring = Ring(list(zip(KT, BKT, VT, BVT)))
            pt_ring = Ring(list(zip(PT, BPT)))
            kbt_ring = Ring(list(zip(KBT, BKBT)))
            dg_ring = Ring(list(zip(DG, BDG)))
            ps_s = Ring(list(zip(PSB[0:3], BPS[0:3])))

            rmsnorm(srcY, dstXN, PV_GMIX1, TL)
            BLO = Buf("lo"); BLS = Buf("ls"); BLG = Buf("lg"); BFB = Buf("fb"); BstageK = []; BstageV = []
            for wt in range(6):
                desc, _ = wtile_cols(w_in_fox, [(1536 + wt * 256, 256)])
                slot, sbuf_ = ws.next(desc)
                sv = slot[:, :].rearrange("p (a b) -> p a b", a=16, b=256)
                for mc in range(2):
                    h = wt * 2 + mc
                    for (off, w) in TL:
                        pb, pbb = ps_main.next()
                        for kc in range(16):
                            MM(pb[:, :w], sv[:, kc, mc * 128:(mc + 1) * 128], XN[:, kc, off:off + w], kc == 0, kc == 15,
                               [sbuf_, BXN(kc, off)], [pbb])
                        r, rb = r32_ring.next()
                        OP("dve", "tensor_copy", [pbb], [rb], out=r[:, :w], in_=pb[:, :w])
                        if off < 1024:
                            DMA("sp", fk_o[hf, h * 128:(h + 1) * 128, off:off + w], r[:, :w], rb, reads=[rb], final=True)
                            kb_, kbb = sq_ring.next()
                            OP("act", "copy", [pbb], [kbb], out=kb_[:, :w], in_=pb[:, :w])
                            BstageK.append(Buf("stw"))
                            DMA("sp", kst[hf][h // 4][(h % 4) * 128:(h % 4 + 1) * 128, off:off + w], kb_[:, :w], kbb, reads=[kbb], writes=[BstageK[-1]])
                        else:
                            DMA("sp", fks_o[hf, h * 128:(h + 1) * 128, :], r[:, :w], rb, reads=[rb], final=True)
                            OP("act", "copy", [pbb], [BKSN], out=KSN[:, h, :], in_=pb[:, :w])
            for q in range(3):
                rec_ = p.op("pool", (lambda a_, b_: lambda e: e.collective_compute("AllGather", ALU.bypass, replica_groups=[[0, 1, 2, 3], [4, 5, 6, 7]],
                                                                                    ins=[a_.opt()], outs=[b_.opt()]))(kst[hf][q], kga[hf][q]), list(BstageK), [Bgath[hf](q, 0)])
                if rec_ is not None:
                    rec_.flag = True
            for wt in range(6):
                desc, _ = wtile_cols(w_in_fox, [(3072 + wt * 256, 256)])
                slot, sbuf_ = ws.next(desc)
                sv = slot[:, :].rearrange("p (a b) -> p a b", a=16, b=256)
                for tb in range(9):
                    off, mm = (tb * 128, 128) if tb < 8 else (1024, 16)
                    toff = 0 if off < 512 else (512 if off < 1024 else 1024)
                    pb, pbb = ps_main.next()
                    for kc in range(16):
                        MM(pb[0:mm, 0:256], XN[:, kc, off:off + mm], sv[:, kc, :], kc == 0, kc == 15, [sbuf_, BXN(kc, toff)], [pbb])
                    r, rb = r32_ring.next()
                    OP("dve", "tensor_copy", [pbb], [rb], out=r[0:mm, 0:256], in_=pb[0:mm, 0:256])
                    if tb < 8:
                        DMA("sp", fv_o[hf, off:off + 128, wt * 256:(wt + 1) * 256], r[:, 0:256], rb, reads=[rb], final=True)
                        vb_, vbb = sq_ring.next()
                        OP("act", "copy", [pbb], [vbb], out=vb_[:, 0:256], in_=pb[:, 0:256])
                        BstageV.append(Buf("stw"))
                        DMA("sp", vview(vst[hf][wt // 2], 0)[off:off + 128, (wt % 2) * 256:(wt % 2 + 1) * 256], vb_[:, 0:256], vbb, reads=[vbb],
                            writes=[BstageV[-1]])
                    else:
                        DMA("sp", fvs_o[hf, :, wt * 256:(wt + 1) * 256], r[0:16, 0:256], rb, reads=[rb], final=True)
                        OP("act", "copy", [pbb], [BVSN], out=VSN[0:16, wt * 256:(wt + 1) * 256], in_=pb[0:16, 0:256])
            for q in range(3):
                rec_ = p.op("pool", (lambda a_, b_: lambda e: e.collective_compute("AllGather", ALU.bypass, replica_groups=[[0, 1, 2, 3], [4, 5, 6, 7]],
                                                                                    ins=[a_.opt()], outs=[b_.opt()]))(vst[hf][q], vga[hf][q]), list(BstageV), [Bgath[hf](q, 1)])
                if rec_ is not None:
                    rec_.flag = True
            for tb in range(9):
                off, mm = (tb * 128, 128) if tb < 8 else (1024, 16)
                toff = 0 if off < 512 else (512 if off < 1024 else 1024)
                pb, pbb = ps_aux.next()
                for kc in range(16):
                    MM(pb[0:mm, 0:12], XN[:, kc, off:off + mm], Wf[:, kc, :], kc == 0, kc == 15, [Bconst, BXN(kc, toff)], [pbb])
                dst = LGO[:, tb, :] if tb < 8 else LGS[0:16, :]
                OP("dve", "tensor_tensor", [pbb, Bconst], [BFB], out=FB[0:mm, :], in0=pb[0:mm, 0:12], in1=bfox[0:mm, :], op=ALU.add)
                OP("act", "activation", [BFB], [BFB], out=FB[0:mm, :], in_=FB[0:mm, :], func=AF.Exp, scale=-1.0)
                OP("act", "activation", [BFB], [BFB], out=FB[0:mm, :], in_=FB[0:mm, :], func=AF.Ln, bias=1.0, scale=1.0)
                OP("dve", "tensor_scalar", [BFB], [BLO if tb < 8 else BLS], out=dst if tb < 8 else LGS[0:16, :], in0=FB[0:mm, :], scalar1=-1.0, scalar2=0.0,
                   op0=ALU.mult, op1=ALU.add)
            Bl2 = Buf("lgo_st")
            DMA("sp", fl_o[hf].rearrange("(t p) h -> p t h", p=128), LGO, Bl2, reads=[BLO], final=True)
            DMA("sp", fls_o[hf], LGS[0:16, :], Bl2, reads=[BLS], final=True)
            DMA("sp", lgst[hf].rearrange("(t p) h -> p t h", p=128), LGO, Bl2, reads=[BLO], writes=[Blgst[hf]])
            rec_ = p.op("pool", lambda e: e.collective_compute("AllGather", ALU.bypass, replica_groups=[[0, 1, 2, 3], [4, 5, 6, 7]],
                                                               ins=[lgst[hf].opt()], outs=[lgga[hf].opt()]), [Blgst[hf]], [Blgga[hf]])
            if rec_ is not None:
                rec_.flag = True
            for wt in range(6):
                desc, _ = wtile_cols(w_in_fox, [(wt * 256, 256)])
                slot, sbuf_ = ws.next(desc)
                sv = slot[:, :].rearrange("p (a b) -> p a b", a=16, b=256)
                for mc in range(2):
                    h = wt * 2 + mc
                    for (off, w) in TL:
                        pb, pbb = ps_main.next()
                        for kc in range(16):
                            MM(pb[:, :w], sv[:, kc, mc * 128:(mc + 1) * 128], XN[:, kc, off:off + w], kc == 0, kc == 15,
                               [sbuf_, BXN(kc, off)], [pbb])
                        OP("act", "copy", [pbb], [BQT(h, off)], out=QT[:, h, off:off + w], in_=pb[:, :w])
            for wt in range(2):
                desc, _ = wtile_cols(w_in_fox, [(4620 + wt * 256, 256)])
                slot, sbuf_ = ws.next(desc)
                sv = slot[:, :].rearrange("p (a b) -> p a b", a=16, b=256)
                for mc in range(2):
                    h = wt * 2 + mc
                    for (off, w) in TL:
                        pb, pbb = ps_main.next()
                        for kc in range(16):
                            MM(pb[:, :w], sv[:, kc, mc * 128:(mc + 1) * 128], XN[:, kc, off:off + w], kc == 0, kc == 15,
                               [sbuf_, BXN(kc, off)], [pbb])
                        OP("act", "copy", [pbb], [BQX(h, off)], out=QX[:, h, off:off + w], in_=pb[:, :w])
            stop_at(5, hf)
            if hf == 0:
                srcs = [(0, r) for r in range(3)]
            else:
                srcs = [(0, r) for r in range(4)] + [(1, r) for r in range(3)]
            nsrc = len(srcs)
            LG = LCS

            def chunk_cumsum(c0, c1, B_):
                ncol = (c1 - c0) * 96
                w_, wb_ = ps_aux.next()
                t_, tb_ = ps_aux.next()
                lgv = LG[:, c0:c1, :, :].rearrange("p c t h -> p (c t h)")
                MM(w_[:, :ncol], tri_f[:], lgv, True, True, [Bconst, B_], [wb_])
                MM(t_[:, :ncol], ones_f[:], lgv, True, True, [Bconst, B_], [tb_])
                tv = t_[:, :ncol].rearrange("p (c t h) -> p c t h", c=c1 - c0, t=8, h=12)
                OP("dve", "memset", [], [B_], ap=OFF[:, c0:c1, 0, :], constant=0.0)
                for tl in range(7):
                    OP("dve", "tensor_tensor", [tb_, B_], [B_], out=OFF[:, c0:c1, tl + 1, :], in0=tv[:, :, tl, :], in1=OFF[:, c0:c1, tl, :],
                       op=ALU.add)
                OP("dve", "tensor_tensor", [tb_, B_], [B_], out=TOT[:, c0:c1, :], in0=tv[:, :, 7, :], in1=OFF[:, c0:c1, 7, :], op=ALU.add)
                OP("dve", "tensor_tensor", [wb_, B_], [B_], out=lgv, in0=w_[:, :ncol],
                   in1=OFF[:, c0:c1, :, :].rearrange("p c t h -> p (c t h)"), op=ALU.add)

            w_, wb_ = ps_aux.next()
            t_, tb_ = ps_aux.next()
            lgov = LGO.rearrange("p t h -> p (t h)")
            MM(w_[:, :96], tri_f[:], lgov, True, True, [Bconst, BLO], [wb_])
            MM(t_[:, :96], ones_f[:], lgov, True, True, [Bconst, BLO], [tb_])
            tv = t_[:, :96].rearrange("p (t h) -> p t h", t=8, h=12)
            OP("dve", "memset", [], [BLO], ap=OFFO[:, 0, :], constant=0.0)
            for tl in range(8):
                OP("dve", "tensor_tensor", [tb_, BLO], [BLO], out=OFFO[:, tl + 1, :], in0=tv[:, tl, :], in1=OFFO[:, tl, :], op=ALU.add)
            OP("dve", "tensor_tensor", [wb_, BLO], [BLO], out=lgov, in0=w_[:, :96], in1=OFFO[:, 0:8, :].rearrange("p t h -> p (t h)"),
               op=ALU.add)
            LCO = LGO
            for qt in range(2):
                OP("dve", "tensor_tensor", [BLO], [BLO], out=KBO[:, qt, :, :], in0=OFFO[:, 4 * qt, :].unsqueeze(1).broadcast_to([128, 8, 12]),
                   in1=LCO, op=ALU.subtract)
                OP("dve", "tensor_tensor", [BLO], [BLO], out=CQREL[:, 4 * qt:4 * qt + 4, :], in0=LCO[:, 4 * qt:4 * qt + 4, :],
                   in1=OFFO[:, 4 * qt, :].unsqueeze(1).broadcast_to([128, 4, 12]), op=ALU.subtract)
            Bld = Buf("lgld")
            DMA("sp", LG[:, nsrc, :, :], cfl[hf].rearrange("(t p) h -> p t h", p=128), Bld, writes=[BLS])
            chunk_cumsum(nsrc, nsrc + 1, BLS)
            w_, wb_ = ps_aux.next()
            MM(w_[0:16, 0:12], tri_f[0:16, 0:16], LGS[0:16, :], True, True, [Bconst, BLS], [wb_])
            OP("dve", "tensor_copy", [wb_], [BLS], out=LCSS[0:16, :], in_=w_[0:16, 0:12])
            OP("dve", "tensor_scalar", [BLS], [BLS], out=NLCSS[0:16, :], in0=LCSS[0:16, :], scalar1=-1.0, scalar2=0.0, op0=ALU.mult, op1=ALU.add)
            stop_at(6, hf)
            OPS = [(PSB[4], BPS[4]), (PSB[5], BPS[5])]
            LPS = [(PSB[6], BPS[6]), (PSB[7], BPS[7])]
            AUXP = (PSB[3], BPS[3])

            for h in range(12):
                cqp, cqb = AUXP
                dg, dgb = dg_ring.next()
                OP("dve", "tensor_scalar", [Bconst, BLS], [dgb], out=dg[0:16, 0:16], in0=ident_f[0:16, 0:16], scalar1=LCSS[0:16, h:h + 1],
                   scalar2=0.0, op0=ALU.mult, op1=ALU.add)
                MM(cqp[:, 0:16], ones_f[0:16, :], dg[0:16, 0:16], True, True, [Bconst, dgb], [cqb])
                OP("act", "activation", [cqb], [BCQ[0]], out=CQ[0][:, 0:16], in_=cqp[:, 0:16], func=AF.Copy)
                kt_ap, ktb, vt_ap, vtb = kt_ring.next()
                DMA("pool", kt_ap, cfk[hf, h * 128:(h + 1) * 128, :], ktb, writes=[ktb])
                DMA("pool", vt_ap, cfv[hf][:, h * 128:(h + 1) * 128].rearrange("(t p) d -> p t d", p=128), vtb, writes=[vtb])
                kbt, kbtb = kbt_ring.next()
                OP("dve", "tensor_scalar", [BLS], [kbtb], out=kbt[:, 0, :], in0=LCS[:, nsrc, :, h], scalar1=-1.0,
                   scalar2=TOT[:, nsrc, h:h + 1], op0=ALU.mult, op1=ALU.add)
                o_, ob = OPS[0]
                l_, lb = LPS[0]
                pend = []
                for tl in range(9):
                    kk = 128 if tl < 8 else 16
                    sp_, spb = ps_s.next()
                    if tl < 8:
                        MM(sp_[:, 0:16], kt_ap[:, tl * 128:(tl + 1) * 128], QT[:, h, 1024:1040], True, True, [ktb, BQT(h, 1024)], [spb])
                    else:
                        MM(sp_[0:16, 0:16], KSN[:, h, :], QT[:, h, 1024:1040], True, True, [BKSN, BQT(h, 1024)], [spb])
                    OP("dve", "scalar_tensor_tensor", [spb, BCQ[0]], [spb], out=sp_[0:kk, 0:16], in0=sp_[0:kk, 0:16], scalar=SCALE,
                       in1=CQ[0][0:kk, 0:16], op0=ALU.mult, op1=ALU.add)
                    if tl == 8:
                        OP("dve", "tensor_tensor", [spb, Bconst], [spb], out=sp_[0:16, 0:16], in0=sp_[0:16, 0:16], in1=trimask[0:16, 0:16],
                           op=ALU.add)
                    pt, ptb = pt_ring.next()
                    bias_ap = kbt[:, 0, tl:tl + 1] if tl < 8 else NLCSS[0:16, h:h + 1]
                    OP("act", "activation", [spb, kbtb, BLS], [ptb], out=pt[0:kk, 0:16], in_=sp_[0:kk, 0:16], func=AF.Exp, bias=bias_ap,
                       scale=1.0)

                    def back(tl=tl, pt=pt, ptb=ptb):
                        if tl < 8:
                            MM(o_[:, 0:16], vt_ap[:, tl, :], pt[:, 0:16], tl == 0, False, [vtb, ptb], [ob])
                            MM(l_[:, 0:16], ones_bf[:], pt[:, 0:16], tl == 0, False, [Bconst, ptb], [lb])
                        else:
                            MM(o_[:, 0:16], VSN[0:16, h * 128:(h + 1) * 128], pt[0:16, 0:16], False, True, [BVSN, ptb], [ob])
                            MM(l_[:, 0:16], ones_bf[0:16, :], pt[0:16, 0:16], False, True, [Bconst, ptb], [lb])
                    pend.append(back)
                    while len(pend) > 2:
                        pend.pop(0)()
                while pend:
                    pend.pop(0)()
                rc, rcb = r32_ring.next()
                OP("dve", "reciprocal", [lb], [rcb], out=rc[:, 0:16], in_=l_[:, 0:16])
                OP("dve", "tensor_tensor", [ob, rcb], [BXN(h, 1024)], out=XN[:, h, 1024:1040], in0=o_[:, 0:16], in1=rc[:, 0:16], op=ALU.mult)

            SM["MKS"] = KT[0].rearrange("p (a b) -> p a b", a=4, b=256)
            SM["MVS"] = KT[1].rearrange("p (a b) -> p a b", a=2, b=512)
            SM["BMKS"] = BKT[0]; SM["BMVS"] = BKT[1]
            load_sample_mem(hf, 1)
            mem_attend(1)
            Bld2 = Buf("lgld2")
            for ci, (g, r) in enumerate(srcs):
                DMA("sp", LG[:, ci, :, :], lgga[g][r * 1024:(r + 1) * 1024, :].rearrange("(t p) h -> p t h", p=128), Bld2,
                    reads=[Blgga[g]], writes=[BLG])
            for c0 in range(0, nsrc, 4):
                chunk_cumsum(c0, min(nsrc, c0 + 4), BLG)

            def vt(ci, r):
                OP("dve", "tensor_tensor", [BLG, Bconst], [BLG], out=VTMP[:, r, :], in0=TOT[:, ci, :], in1=vmask[:, r, :], op=ALU.mult)
                return VTMP[:, r, :]
            last = None
            for ci in range(nsrc - 1, -1, -1):
                g, r = srcs[ci]
                masked = (hf == 0) or (g == 1)
                term = vt(ci, r) if masked else TOT[:, ci, :]
                if last is None:
                    OP("dve", "tensor_copy", [BLG], [BLG], out=GS[:, ci, :], in_=term)
                else:
                    OP("dve", "tensor_tensor", [BLG], [BLG], out=GS[:, ci, :], in0=term, in1=GS[:, last, :], op=ALU.add)
                last = ci
            for ci in range(nsrc):
                g, r = srcs[ci]
                masked = (hf == 0) or (g == 1)
                for qt in range(2):
                    OP("dve", "tensor_tensor", [BLG, BLO], [BLG], out=BASE[:, ci, qt, :], in0=GS[:, ci, :], in1=OFFO[:, 4 * qt, :], op=ALU.add)
                    if masked:
                        OP("dve", "tensor_tensor", [BLG, Bconst], [BLG], out=BASE[:, ci, qt, :], in0=BASE[:, ci, qt, :], in1=smask[:, r, :],
                           op=ALU.add)
            stop_at(6, hf)
            for h in range(12):
                for qt in range(2):
                    cqp, cqb = AUXP
                    for tt in range(4):
                        dg, dgb = dg_ring.next()
                        OP("dve", "tensor_scalar", [Bconst, BLO], [dgb], out=dg, in0=ident_f[:], scalar1=CQREL[:, 4 * qt + tt, h:h + 1],
                           scalar2=0.0, op0=ALU.mult, op1=ALU.add)
                        MM(cqp[:, tt * 128:(tt + 1) * 128], ones_f[:], dg, True, True, [Bconst, dgb], [cqb])
                    OP("act", "activation", [cqb], [BCQ[qt]], out=CQ[qt], in_=cqp[:, :], func=AF.Copy)
                first = [True, True]

                pending = []

                def flush(keep=0):
                    while len(pending) > keep:
                        pending.pop(0)()

                def tile_step(kt_ap, ktb, vt_ap, vtb, tl, qt, bias_ap, bias_bufs, c0, diag, lastflag):
                    sp_, spb = ps_s.next()
                    MM(sp_[:, c0:512], kt_ap[:, tl * 128:(tl + 1) * 128], QT[:, h, qt * 512 + c0:(qt + 1) * 512], True, True,
                       [ktb, BQT(h, qt * 512)], [spb])
                    OP("dve", "scalar_tensor_tensor", [spb, BCQ[qt]], [spb], out=sp_[:, c0:512], in0=sp_[:, c0:512], scalar=SCALE,
                       in1=CQ[qt][:, c0:512], op0=ALU.mult, op1=ALU.add)
                    if diag:
                        OP("dve", "tensor_tensor", [spb, Bconst], [spb], out=sp_[:, c0:c0 + 128], in0=sp_[:, c0:c0 + 128], in1=trimask[:],
                           op=ALU.add)
                    pt, ptb = pt_ring.next()
                    OP("act", "activation", [spb] + bias_bufs, [ptb], out=pt[:, c0:512], in_=sp_[:, c0:512], func=AF.Exp, bias=bias_ap,
                       scale=1.0)

                    def back():
                        o_, ob = OPS[qt]
                        l_, lb = LPS[qt]
                        MM(o_[:, c0:512], vt_ap[:, tl, :], pt[:, c0:512], first[qt], lastflag, [vtb, ptb], [ob])
                        MM(l_[:, c0:512], ones_bf[:], pt[:, c0:512], first[qt], lastflag, [Bconst, ptb], [lb])
                        first[qt] = False
                    pending.append(back)
                    flush(keep=2)

                for ci, (g, r) in enumerate(srcs):
                    kt_ap, ktb, vt_ap, vtb = kt_ring.next()
                    DMA("sp", kt_ap, kga[g][h // 4][r * 512 + (h % 4) * 128:r * 512 + (h % 4 + 1) * 128, :], ktb, reads=[Bgath[g](h // 4, 0)], writes=[ktb])
                    DMA("sp", vt_ap, vview(vga[g][h // 4], r)[:, (h % 4) * 128:(h % 4 + 1) * 128].rearrange("(t p) d -> p t d", p=128), vtb,
                        reads=[Bgath[g](h // 4, 1)], writes=[vtb])
                    kbt, kbtb = kbt_ring.next()
                    for qt in range(2):
                        OP("dve", "tensor_scalar", [BLG], [kbtb], out=kbt[:, qt, :], in0=LCS[:, ci, :, h], scalar1=-1.0,
                           scalar2=BASE[:, ci, qt, h:h + 1], op0=ALU.mult, op1=ALU.add)
                    for tl in range(8):
                        for qt in range(2):
                            tile_step(kt_ap, ktb, vt_ap, vtb, tl, qt, kbt[:, qt, tl:tl + 1], [kbtb], 0, False, False)
                kt_ap, ktb, vt_ap, vtb = kt_ring.next()
                DMA("sp", kt_ap, kst[hf][h // 4][(h % 4) * 128:(h % 4 + 1) * 128, :], ktb, reads=[Bgath[hf](h // 4, 0)], writes=[ktb])
                DMA("sp", vt_ap, vview(vst[hf][h // 4], 0)[:, (h % 4) * 128:(h % 4 + 1) * 128].rearrange("(t p) d -> p t d", p=128), vtb,
                    reads=[Bgath[hf](h // 4, 1)], writes=[vtb])
                for qt in range(2):
                    for tl in range(4 * qt):
                        tile_step(kt_ap, ktb, vt_ap, vtb, tl, qt, KBO[:, qt, tl, h:h + 1], [BLO], 0, False, False)
                    for j in range(4):
                        tl = 4 * qt + j
                        tile_step(kt_ap, ktb, vt_ap, vtb, tl, qt, KBO[:, qt, tl, h:h + 1], [BLO], 128 * j, True, j == 3)
                    flush()
                    o_, ob = OPS[qt]
                    l_, lb = LPS[qt]
                    rc, rcb = r32_ring.next()
                    OP("dve", "reciprocal", [lb], [rcb], out=rc[:, :], in_=l_[:, :])
                    OP("dve", "tensor_tensor", [ob, rcb], [BXN(h, qt * 512)], out=XN[:, h, qt * 512:(qt + 1) * 512], in0=o_[:, :],
                       in1=rc[:, :], op=ALU.mult)

            out_proj(1)
            p.barrier()
            stop_at(9, hf)
            mlp(1, PV_GMLP1)
            stop_at(10, hf)
            for (off, w) in TL:
                ss, ssb = ps_aux.next()
                for kc in range(16):
                    sq, sqb = sq_ring.next()
                    OP("act", "activation", [BY(kc, off)], [sqb], out=sq[:, :w], in_=Y[:, kc, off:off + w], func=AF.Square)
                    MM(ss[:, :w], ones_bf[:], sq[:, :w], kc == 0, kc == 15, [sqb, Bconst], [ssb])
                rs = MEAN_FIN[:, :w]
                OP("act", "activation", [ssb], [BFIN], out=rs, in_=ss[:, :w], func=AF.Sqrt, scale=1.0 / 2048.0, bias=EPS)
                OP("dve", "reciprocal", [BFIN], [BFIN], out=rs, in_=rs)
                for kc in range(16):
                    r, rb = r32_ring.next()
                    OP("dve", "scalar_tensor_tensor", [BY(kc, off), BFIN, Bconst], [rb], out=r[:, :w], in0=Y[:, kc, off:off + w],
                       scalar=pcol(PV_GFIN + kc), in1=rs, op0=ALU.mult, op1=ALU.mult)
                    if off < 1024:
                        DMA("sp", yp_o[hf, kc * 128:(kc + 1) * 128, off:off + w], r[:, :w], rb, reads=[rb], final=True)
                    else:
                        DMA("sp", ys_o[hf, kc * 128:(kc + 1) * 128, :], r[:, :w], rb, reads=[rb], final=True)

        def vview(ap2d, r):
            flat = ap2d[r * 512:(r + 1) * 512, :].rearrange("a b -> (a b)")
            return flat.rearrange("(t c) -> t c", c=512)

        MEAN_FIN = REGF[:, 0:512]
        BFIN = Buf("fin")

        p.dry = True
        try:
            emit()
        except StopEmit:
            pass
        p.dry = False
        wt_t = nc.dram_tensor("wt", [len(ws.uniq), 128, 4096], F32, kind="ExternalInput").ap()
        WT["ap"] = [wt_t[i] for i in range(len(ws.uniq))]
        try:
            emit()
        except StopEmit:
            pass
        p.build()
    nc._w_uniq = list(ws.uniq.keys())
    return nc


_NC = None


def kernel(x_prompt, x_sample, cache_mem_k, cache_mem_v, state_conv, cache_fox_k, cache_fox_v, cache_fox_logf,
           mem_prompt, g_mix, g_mem, w_mem_k, w_mem_v, w_in_conv, conv_w, conv_b, conv_ln_g, conv_ln_b,
           w_in_fox, b_fox_f, w_out, g_mlp, w_up, w_down, g_final):
    global _NC
    f32 = np.float32
    A = lambda a: np.ascontiguousarray(np.asarray(a, dtype=f32))
    x_prompt = A(x_prompt); x_sample = A(x_sample)
    if _NC is None:
        _NC = build_nc()
    nc = _NC

    def cols16(v):
        return np.asarray(v, f32).reshape(16, 128).T

    def cols12(v):
        return np.asarray(v, f32).reshape(12, 128).T

    pv = np.zeros((128, PV_N), f32)
    pv[:, PV_GMIX0:PV_GMIX0 + 16] = cols16(g_mix[0]); pv[:, PV_GMLP0:PV_GMLP0 + 16] = cols16(g_mlp[0])
    pv[:, PV_GMIX1:PV_GMIX1 + 16] = cols16(g_mix[1]); pv[:, PV_GMLP1:PV_GMLP1 + 16] = cols16(g_mlp[1])
    pv[:, PV_GFIN:PV_GFIN + 16] = cols16(g_final)
    pv[:, PV_GMEM0:PV_GMEM0 + 16] = cols16(g_mem[0]); pv[:, PV_GMEM1:PV_GMEM1 + 16] = cols16(g_mem[1])
    pv[:, PV_CB:PV_CB + 12] = cols12(conv_b[0]); pv[:, PV_LNG:PV_LNG + 12] = cols12(conv_ln_g[0]); pv[:, PV_LNB:PV_LNB + 12] = cols12(conv_ln_b[0])
    cw = np.asarray(conv_w[0], f32)
    for wi in range(31):
        pv[:, PV_CW + wi * 12:PV_CW + (wi + 1) * 12] = cols12(cw[wi])

    Wsrc = {"w_mem_k": A(w_mem_k), "w_mem_v": A(w_mem_v), "w_in_conv": A(w_in_conv), "w_in_fox": A(w_in_fox),
            "w_out": A(w_out), "w_up": A(w_up)}
    Wdown = A(w_down)
    uniq = nc._w_uniq
    wt = np.empty((len(uniq), 128, 4096), f32)
    for i, key in enumerate(uniq):
        if key[0] == "cols":
            W = Wsrc[key[1]][key[2]]
            off = 0
            for (c0, n) in key[3]:
                wt[i, :, off:off + 16 * n] = W[:, c0:c0 + n].reshape(16, 128, n).transpose(1, 0, 2).reshape(128, 16 * n)
                off += 16 * n
        else:
            _, layer, hg, dt_ = key
            wt[i] = Wdown[layer][hg * 512:(hg + 1) * 512, dt_ * 1024:(dt_ + 1) * 1024].reshape(4, 128, 1024).transpose(1, 0, 2).reshape(128, 4096)
    wf = Wsrc["w_in_fox"][0][:, 4608:4620].reshape(16, 128, 12).transpose(1, 0, 2).reshape(128, 192)
    shared = dict(pvec=pv, bfox=A(b_fox_f).reshape(1, 12), wt=wt, wf_d=np.ascontiguousarray(wf))
    in_maps = []
    for c in range(8):
        b, j = c // 4, c % 4
        xp = np.zeros((2, 2048, 1056), f32)
        xs = np.zeros((2, 2048, 16), f32)
        cst = np.zeros((2, 1536, 30), f32)
        cmk = np.zeros((2, 2, 512, 256), f32); cmv = np.zeros((2, 2, 256, 512), f32)
        cfk = np.zeros((2, 1536, 1024), f32); cfv = np.zeros((2, 1024, 1536), f32); cfl = np.zeros((2, 1024, 12), f32)
        for hf in range(2):
            ci = 4 * hf + j
            t0 = 1024 * ci
            xp[hf, :, 32:] = x_prompt[b, t0:t0 + 1024, :].T
            if ci > 0:
                xp[hf, :, 0:32] = x_prompt[b, t0 - 32:t0, :].T
            s = 2 * c + hf
            xs[hf] = x_sample[s].T
            cst[hf] = np.asarray(state_conv[0, s], f32).T
            for l in range(2):
                cmk[l, hf] = np.asarray(cache_mem_k[l, s], f32).transpose(1, 2, 0).reshape(512, 256)
                cmv[l, hf] = np.asarray(cache_mem_v[l, s], f32).reshape(256, 512)
            cfk[hf] = np.asarray(cache_fox_k[0, s], f32).transpose(1, 2, 0).reshape(1536, 1024)
            cfv[hf] = np.asarray(cache_fox_v[0, s], f32).reshape(1024, 1536)
            cfl[hf] = np.asarray(cache_fox_logf[0, s], f32)
        vm = np.zeros((1, 4, 12), f32); sm = np.zeros((1, 4, 12), f32)
        for i in range(4):
            vm[0, i, :] = 1.0 if i < j else 0.0
            sm[0, i, :] = 0.0 if i < j else -1.0e5
        m = dict(shared)
        m.update(xp=xp, xs=xs, memT=np.ascontiguousarray(np.asarray(mem_prompt[b], f32).T), vmask=vm.reshape(1, 48), smask=sm.reshape(1, 48),
                 convst=cst, cmk=cmk, cmv=cmv, cfk=cfk, cfv=cfv, cfl=cfl)
        if KSTOP == 1:
            for k in MINI_SKIP:
                m.pop(k)
        in_maps.append(m)

    res = run_bass_kernel_spmd(nc, in_maps, core_ids=list(range(8)))
    R = res.results

    y_prompt = np.zeros((2, 8192, 2048), f32); y_sample = np.zeros((16, 16, 2048), f32)
    new_mem_k = np.zeros((2, 2, 256, 4, 128), f32); new_mem_v = np.zeros((2, 2, 256, 4, 128), f32)
    conv_p = np.zeros((1, 2, 30, 1536), f32); conv_s = np.zeros((1, 16, 30, 1536), f32)
    fk_p = np.zeros((1, 2, 8192, 12, 128), f32); fv_p = np.zeros((1, 2, 8192, 12, 128), f32); fl_p = np.zeros((1, 2, 8192, 12), f32)
    fk_s = np.zeros((1, 16, 16, 12, 128), f32); fv_s = np.zeros((1, 16, 16, 12, 128), f32); fl_s = np.zeros((1, 16, 16, 12), f32)
    for c in range(8):
        b, j = c // 4, c % 4
        r = R[c]
        for hf in range(2):
            ci = 4 * hf + j
            t0 = 1024 * ci
            s = 2 * c + hf
            y_prompt[b, t0:t0 + 1024, :] = r["yp_o"][hf].T
            y_sample[s] = r["ys_o"][hf].T
            fk_p[0, b, t0:t0 + 1024] = r["fk_o"][hf].T.reshape(1024, 12, 128)
            fv_p[0, b, t0:t0 + 1024] = r["fv_o"][hf].reshape(1024, 12, 128)
            fl_p[0, b, t0:t0 + 1024] = r["fl_o"][hf]
            fk_s[0, s] = r["fks_o"][hf].T.reshape(16, 12, 128)
            fv_s[0, s] = r["fvs_o"][hf].reshape(16, 12, 128)
            fl_s[0, s] = r["fls_o"][hf]
            conv_s[0, s] = r["convs_o"][hf].T
            if ci == 7:
                conv_p[0, b] = r["convp_o"][hf].T
        if j == 0:
            for l in range(2):
                new_mem_k[l, b] = r["mk_o"][l].reshape(4, 128, 256).transpose(2, 0, 1)
                new_mem_v[l, b] = r["mv_o"][l].reshape(256, 4, 128)
    return (y_prompt, y_sample, new_mem_k, new_mem_v, conv_p, conv_s, fk_p, fv_p, fl_p, fk_s, fv_s, fl_s)
```

```python
import contextlib
import os
import numpy as np
import concourse.bass as bass
import concourse.mybir as mybir
from concourse.bass_utils import run_bass_kernel_spmd

F32 = mybir.dt.float32
BF16 = mybir.dt.bfloat16
AF = mybir.ActivationFunctionType
ALU = mybir.AluOpType
EPOCH = 20000
KSTOP = int(os.environ.get('KSTOP', '0'))
KSUB = int(os.environ.get('KSUB', '0'))
KNOCC = int(os.environ.get('KNOCC', '0'))


MINI_SKIP = ()


class StopEmit(Exception):
    pass
EPS = 1e-6
SCALE = 128 ** -0.5
NEG = -30000.0
TL = [(0, 512), (512, 512), (1024, 16)]
NT = 1040


class Rec:
    __slots__ = ("eng", "sem", "val", "flag")

    def __init__(self, eng):
        self.eng = eng
        self.sem = None
        self.val = None
        self.flag = False


class Buf:
    __slots__ = ("name", "w", "r", "dcount", "excl")

    def __init__(self, name="b", excl=False):
        self.name = name
        self.w = None
        self.r = {}
        self.dcount = 0
        self.excl = excl


class Prog:
    COMPUTE = ("pe", "act", "dve", "pool")
    ENGS = ("pe", "act", "dve", "pool", "sp")

    def __init__(self, nc):
        self.nc = nc
        self.dry = False
        self.stream = {e: [] for e in self.ENGS}
        self.dma_bufs = {}
        self.final_waits = []
        self.all_dma = {}
        self.nops = 0

    def _deps(self, eng, reads, writes, semkey=None):
        deps = []
        for b in reads:
            if b.w is not None:
                deps.append(b.w)
            if b.excl:
                deps.extend(r for k, r in b.r.items() if k != eng)
        for b in writes:
            if b.w is not None:
                if not (semkey is not None and b.w.sem == semkey):
                    deps.append(b.w)
            deps.extend(b.r.values())
        out = []
        seen = set()
        for d in deps:
            if id(d) in seen:
                continue
            seen.add(id(d))
            if d.eng == "pe" and eng == "pe":
                continue
            out.append(d)
        return out

    def op(self, eng, fn, reads=(), writes=()):
        if self.dry:
            return None
        deps = self._deps(eng, reads, writes)
        st = self.stream[eng]
        for d in deps:
            d.flag = True
            st.append(("wait", d))
        rec = Rec(eng)
        st.append(("op", fn, rec))
        for b in reads:
            b.r[eng] = rec
        for b in writes:
            b.w = rec
            b.r = {}
        self.nops += 1
        return rec

    def dma(self, queue, fn, owner, reads=(), writes=(), final=False):
        if self.dry:
            return None
        key = ("dma", id(owner))
        deps = self._deps("dma", reads, writes, semkey=key)
        st = self.stream[queue]
        for d in deps:
            d.flag = True
            st.append(("wait", d))
        rec = Rec("dma")
        owner.dcount += 1
        self.dma_bufs[id(owner)] = owner
        rec.sem = key
        rec.val = 16 * owner.dcount
        rec.flag = True
        st.append(("dma", fn, rec))
        for b in reads:
            b.r[key] = rec
        for b in writes:
            b.w = rec
            b.r = {}
        self.all_dma[key] = rec
        if final:
            self.final_waits.append(rec)
        return rec

    def barrier(self):
        if self.dry:
            return
        recs = []
        for e in self.COMPUTE:
            last = None
            for it in reversed(self.stream[e]):
                if it[0] == "op":
                    last = it[2]
                    break
            if last is not None:
                last.flag = True
                recs.append(last)
        recs.extend(self.all_dma.values())
        for e in self.ENGS:
            for r in recs:
                self.stream[e].append(("wait", r))

    def build(self):
        nc = self.nc
        nepoch = {}
        for e in self.COMPUTE:
            cnt = 0
            for it in self.stream[e]:
                if it[0] == "op" and it[2].flag:
                    rec = it[2]
                    ep = cnt // EPOCH
                    rec.sem = (e, ep)
                    rec.val = cnt - ep * EPOCH + 1
                    cnt += 1
            nepoch[e] = (cnt + EPOCH - 1) // EPOCH if cnt else 0
        for rec in self.final_waits:
            self.stream["sp"].append(("wait", rec))
        with contextlib.ExitStack() as es:
            sems = {}
            for e in self.COMPUTE:
                for ep in range(nepoch[e]):
                    sems[(e, ep)] = es.enter_context(nc.semaphore(f"c_{e}_{ep}"))
            for i, (k, b) in enumerate(self.dma_bufs.items()):
                sems[("dma", k)] = es.enter_context(nc.semaphore(f"d{i}"))
            print("PROG stats: sems", len(sems), "ops", {e: len(v) for e, v in self.stream.items()}, flush=True)
            block = es.enter_context(nc.Block())
            handles = {"pe": "tensor", "act": "scalar", "dve": "vector", "pool": "gpsimd", "sp": "sync"}

            def replay(engname):
                def run(eng):
                    waited = {}
                    for it in self.stream[engname]:
                        if it[0] == "wait":
                            d = it[1]
                            if waited.get(d.sem, 0) >= d.val:
                                continue
                            if d.eng != "dma" and d.sem[0] == engname and d.eng == "pe":
                                continue
                            waited[d.sem] = d.val
                            if d.eng != "dma":
                                for ep in range(d.sem[1]):
                                    waited[(d.sem[0], ep)] = EPOCH
                            eng.wait_ge(sems[d.sem], d.val)
                        else:
                            _, fn, rec = it
                            ins = fn(eng)
                            if rec.flag:
                                if rec.eng == "dma":
                                    ins.then_inc(sems[rec.sem], 16)
                                else:
                                    ins.then_inc(sems[rec.sem], 1)
                return run

            for engname in self.ENGS:
                getattr(block, handles[engname])(replay(engname))


class Ring:
    def __init__(self, items):
        self.items = items
        self.i = 0

    def next(self):
        it = self.items[self.i % len(self.items)]
        self.i += 1
        return it


class Grid:
    def __init__(self, name):
        self.name = name
        self.d = {}

    def __call__(self, *key):
        b = self.d.get(key)
        if b is None:
            b = Buf(self.name)
            self.d[key] = b
        return b

    def reset(self):
        self.d = {}


PV_GMIX0, PV_GMLP0, PV_GMIX1, PV_GMLP1, PV_GFIN, PV_GMEM0, PV_GMEM1 = 0, 16, 32, 48, 64, 80, 96
PV_CB, PV_LNG, PV_LNB, PV_CW = 112, 124, 136, 148
PV_N = 148 + 31 * 12


def build_nc():
    nc = bass.Bass("TRN2", target_bir_lowering=False)

    def din(name, shape):
        if KSTOP == 1 and name in MINI_SKIP:
            return nc.dram_tensor(name, list(shape), F32).ap()
        return nc.dram_tensor(name, list(shape), F32, kind="ExternalInput").ap()

    def dout(name, shape):
        return nc.dram_tensor(name, list(shape), F32, kind="ExternalOutput").ap()

    xp = din("xp", [2, 2048, 1056]); xs = din("xs", [2, 2048, 16]); memT = din("memT", [2048, 256])
    pvec_d = din("pvec", [128, PV_N]); bfox_d = din("bfox", [1, 12])
    vmask_d = din("vmask", [1, 48]); smask_d = din("smask", [1, 48])
    convst = din("convst", [2, 1536, 30])
    cmk = din("cmk", [2, 2, 512, 256]); cmv = din("cmv", [2, 2, 256, 512])
    cfk = din("cfk", [2, 1536, 1024]); cfv = din("cfv", [2, 1024, 1536]); cfl = din("cfl", [2, 1024, 12])
    wf_d = din("wf_d", [128, 192])
    w_mem_k = ("w_mem_k", 0), ("w_mem_k", 1)
    w_mem_v = ("w_mem_v", 0), ("w_mem_v", 1)
    w_in_conv = ("w_in_conv", 0)
    w_in_fox = ("w_in_fox", 0)
    w_out = ("w_out", 0), ("w_out", 1)
    w_up = ("w_up", 0), ("w_up", 1)
    WT = {}

    yp_o = dout("yp_o", [2, 2048, 1024]); ys_o = dout("ys_o", [2, 2048, 16])
    mk_o = dout("mk_o", [2, 512, 256]); mv_o = dout("mv_o", [2, 256, 512])
    convp_o = dout("convp_o", [2, 1536, 30]); convs_o = dout("convs_o", [2, 1536, 30])
    fk_o = dout("fk_o", [2, 1536, 1024]); fv_o = dout("fv_o", [2, 1024, 1536]); fl_o = dout("fl_o", [2, 1024, 12])
    fks_o = dout("fks_o", [2, 1536, 16]); fvs_o = dout("fvs_o", [2, 16, 1536]); fls_o = dout("fls_o", [2, 16, 12])

    KVN = 3072 * 1024
    kst = [[nc.dram_tensor(f"kst{h}_{q}", [512, 1024], BF16).ap() for q in range(3)] for h in range(2)]
    vst = [[nc.dram_tensor(f"vst{h}_{q}", [512, 1024], BF16).ap() for q in range(3)] for h in range(2)]
    kga = [[nc.dram_tensor(f"kga{h}_{q}", [2048, 1024], BF16).ap() for q in range(3)] for h in range(2)]
    vga = [[nc.dram_tensor(f"vga{h}_{q}", [2048, 1024], BF16).ap() for q in range(3)] for h in range(2)]
    lgst = [nc.dram_tensor(f"lgst{h}", [1024, 12], F32).ap() for h in range(2)]
    lgga = [nc.dram_tensor(f"lgga{h}", [4096, 12], F32).ap() for h in range(2)]

    p = Prog(nc)
    es = contextlib.ExitStack()
    with es:
        def sb(name, shape, dt):
            return es.enter_context(nc.sbuf_tensor("s_" + name, list(shape), dt))

        Y = sb("Y", [128, 16, NT], F32)
        XN = sb("XN", [128, 16, NT], BF16)
        QX = sb("QX", [128, 4, NT], BF16)
        NSLOT = 3
        WS = [sb(f"ws{i}", [128, 4096], BF16) for i in range(NSLOT)]
        MK = sb("MK", [128, 2, 4, 256], BF16)
        MV = sb("MV", [128, 2, 2, 512], BF16)
        ones_bf = sb("ones_bf", [128, 128], BF16)
        ones_f = sb("ones_f", [128, 128], F32)
        tri_f = sb("tri_f", [128, 128], F32)
        ident_f = sb("ident_f", [128, 128], F32)
        trimask = sb("trimask", [128, 128], F32)
        pvec = sb("pvec", [128, PV_N], F32)
        bfox = sb("bfox", [128, 12], F32)
        vmask = sb("vmask", [128, 4, 12], F32)
        smask = sb("smask", [128, 4, 12], F32)
        Wf = sb("Wf", [128, 16, 12], BF16)
        SQ = [sb(f"sq{i}", [128, 512], BF16) for i in range(3)]
        R32 = [sb(f"r32_{i}", [128, 512], F32) for i in range(3)]
        NRB = 19840
        NRF = 4224
        REGB = sb("REGB", [128, NRB], BF16)
        REGF = sb("REGF", [128, NRF], F32)
        PSB = [es.enter_context(nc.psum_tensor(f"ps{i}", [128, 512], F32)) for i in range(8)]

        Bconst = Buf("const")
        BY = Grid("Y"); BXN = Grid("XN"); BQX = Grid("QX")
        BWS = [Buf(f"ws{i}") for i in range(NSLOT)]
        BMK = Grid("MK"); BMV = Grid("MV"); SM = {}
        BSQ = [Buf("sq") for _ in SQ]; BR32 = [Buf("r32") for _ in R32]
        BPS = [Buf(f"ps{i}", excl=True) for i in range(8)]
        sq_ring = Ring(list(zip(SQ, BSQ)))
        r32_ring = Ring(list(zip(R32, BR32)))
        ps_main = Ring(list(zip(PSB[0:4], BPS[0:4])))
        ps_aux = Ring(list(zip(PSB[4:6], BPS[4:6])))
        Bstage = [[], []]
        Bgath = [Grid("kga0"), Grid("kga1")]
        Blgst = [Buf("lgst0"), Buf("lgst1")]
        Blgga = [Buf("lgga0"), Buf("lgga1")]
        Bout = Grid("out")

        def OP(eng, name, reads, writes, **kw):
            return p.op(eng, lambda e: getattr(e, name)(**kw), reads, writes)

        def DMA(q, out, in_, owner, reads=(), writes=(), final=False):
            return p.dma(q, lambda e: e.dma_start(out=out, in_=in_), owner, reads, writes, final)

        def MM(out, lhsT, rhs, start, stop, reads, writes):
            return p.op("pe", lambda e: e.matmul(out, lhsT=lhsT, rhs=rhs, start=start, stop=stop), reads, writes)

        def pcol(c):
            return pvec[:, c:c + 1]

        class WStream:
            def __init__(self):
                self.order = []
                self.uniq = {}
                self.pos = 0
                self.issued = 0

            def reset(self):
                self.pos = 0
                self.issued = 0

            def _issue(self, i):
                s_ = i % NSLOT
                tid = self.uniq[self.order[i]]
                DMA("pool", WS[s_][:, :], WT["ap"][tid], BWS[s_], writes=[BWS[s_]])

            def next(self, key):
                if p.dry:
                    self.order.append(key)
                    self.uniq.setdefault(key, len(self.uniq))
                    return WS[0], BWS[0]
                while self.issued < min(len(self.order), self.pos + NSLOT):
                    self._issue(self.issued)
                    self.issued += 1
                s_ = self.pos % NSLOT
                self.pos += 1
                return WS[s_], BWS[s_]

        ws = WStream()

        def wtile_cols(wref, pieces):
            return ("cols", wref[0], wref[1], tuple(pieces)), sum(n for _, n in pieces)

        def rmsnorm(src, dst, gcol0, tiles, nkc=16, denom=2048.0):
            for (off, w) in tiles:
                ss, ssb = ps_aux.next()
                for kc in range(nkc):
                    sq, sqb = sq_ring.next()
                    a, ab = src(kc, off, w)
                    OP("act", "activation", [ab], [sqb], out=sq[:, :w], in_=a, func=AF.Square)
                    MM(ss[:, :w], ones_bf[:], sq[:, :w], kc == 0, kc == nkc - 1, [sqb, Bconst], [ssb])
                rs, rsb = r32_ring.next()
                OP("act", "activation", [ssb], [rsb], out=rs[:, :w], in_=ss[:, :w], func=AF.Sqrt, scale=1.0 / denom, bias=EPS)
                OP("dve", "reciprocal", [rsb], [rsb], out=rs[:, :w], in_=rs[:, :w])
                for kc in range(nkc):
                    a, ab = src(kc, off, w)
                    d, db = dst(kc, off, w)
                    OP("dve", "scalar_tensor_tensor", [ab, rsb, Bconst], [db], out=d, in0=a, scalar=pcol(gcol0 + kc),
                       in1=rs[:, :w], op0=ALU.mult, op1=ALU.mult)

        def srcY(kc, off, w):
            return Y[:, kc, off:off + w], BY(kc, off)

        def dstXN(kc, off, w):
            return XN[:, kc, off:off + w], BXN(kc, off)

        def mem_attend(layer, prompt_tiles=True):
            for h in range(4):
                for (off, w) in TL:
                    smp = off >= 1024
                    pts = []
                    for blk in range(2):
                        sp_, spb = ps_main.next()
                        if smp:
                            lh, lb = SM["MKS"][:, h, blk * 128:(blk + 1) * 128], SM["BMKS"]
                        else:
                            lh, lb = MK[:, layer, h, blk * 128:(blk + 1) * 128], BMK(layer)
                        MM(sp_[:, :w], lh, QX[:, h, off:off + w], True, True, [lb, BQX(h, off)], [spb])
                        pt, ptb = sq_ring.next()
                        OP("act", "activation", [spb], [ptb], out=pt[:, :w], in_=sp_[:, :w], func=AF.Exp, scale=SCALE)
                        pts.append((pt, ptb))
                    o_, ob = ps_aux.next()
                    l_, lb2 = ps_aux.next()
                    for blk in range(2):
                        pt, ptb = pts[blk]
                        if smp:
                            vh, vb = SM["MVS"][:, blk, h * 128:(h + 1) * 128], SM["BMVS"]
                        else:
                            vh, vb = MV[:, layer, blk, h * 128:(h + 1) * 128], BMV(layer)
                        MM(o_[:, :w], vh, pt[:, :w], blk == 0, blk == 1, [vb, ptb], [ob])
                    for blk in range(2):
                        pt, ptb = pts[blk]
                        MM(l_[:, :w], ones_bf[:], pt[:, :w], blk == 0, blk == 1, [Bconst, ptb], [lb2])
                    rc, rcb = r32_ring.next()
                    OP("dve", "reciprocal", [lb2], [rcb], out=rc[:, :w], in_=l_[:, :w])
                    OP("dve", "tensor_tensor", [ob, rcb], [BXN(12 + h, off)], out=XN[:, 12 + h, off:off + w], in0=o_[:, :w],
                       in1=rc[:, :w], op=ALU.mult)

        def load_sample_mem(hf, layer):
            DMA("pool", SM["MKS"], cmk[layer, hf].rearrange("(h d) m -> d h m", d=128), SM["BMKS"], writes=[SM["BMKS"]])
            DMA("pool", SM["MVS"], cmv[layer, hf].rearrange("(b m) c -> m b c", m=128), SM["BMVS"], writes=[SM["BMVS"]])

        def out_proj(layer):
            for wt in range(8):
                desc, _ = wtile_cols(w_out[layer], [(wt * 256, 256)])
                slot, sbuf_ = ws.next(desc)
                sv = slot[:, :].rearrange("p (a b) -> p a b", a=16, b=256)
                for mc in range(2):
                    m = wt * 2 + mc
                    for (off, w) in TL:
                        pb, pbb = ps_main.next()
                        for kc in range(16):
                            MM(pb[:, :w], sv[:, kc, mc * 128:(mc + 1) * 128], XN[:, kc, off:off + w], kc == 0, kc == 15,
                               [sbuf_, BXN(kc, off)], [pbb])
                        OP("dve", "tensor_tensor", [pbb, BY(m, off)], [BY(m, off)], out=Y[:, m, off:off + w], in0=pb[:, :w],
                           in1=Y[:, m, off:off + w], op=ALU.add)

        def mlp(layer, gcol0):
            rmsnorm(srcY, dstXN, gcol0, TL)
            H = REGB[:, 0:4 * NT].rearrange("p (a b) -> p a b", a=4, b=NT)
            BH = Grid("H")
            for hg in range(16):
                for ut in range(2):
                    desc, _ = wtile_cols(w_up[layer], [(hg * 512 + ut * 256, 256)])
                    slot, sbuf_ = ws.next(desc)
                    sv = slot[:, :].rearrange("p (a b) -> p a b", a=16, b=256)
                    for mc in range(2):
                        hc = ut * 2 + mc
                        for (off, w) in TL:
                            pb, pbb = ps_main.next()
                            for kc in range(16):
                                MM(pb[:, :w], sv[:, kc, mc * 128:(mc + 1) * 128], XN[:, kc, off:off + w], kc == 0, kc == 15,
                                   [sbuf_, BXN(kc, off)], [pbb])
                            r, rb = r32_ring.next()
                            OP("act", "activation", [pbb], [rb], out=r[:, :w], in_=pb[:, :w], func=AF.Relu)
                            OP("dve", "tensor_tensor", [rb], [BH(hc, off)], out=H[:, hc, off:off + w], in0=r[:, :w], in1=r[:, :w],
                               op=ALU.mult)
                for dt_ in range(2):
                    slot, sbuf_ = ws.next(("down", layer, hg, dt_))
                    sv = slot[:, :].rearrange("p (a b) -> p a b", a=4, b=1024)
                    for mc in range(8):
                        m = dt_ * 8 + mc
                        for (off, w) in TL:
                            pb, pbb = ps_main.next()
                            for kc in range(4):
                                MM(pb[:, :w], sv[:, kc, mc * 128:(mc + 1) * 128], H[:, kc, off:off + w], kc == 0, kc == 3,
                                   [sbuf_, BH(kc, off)], [pbb])
                            OP("dve", "tensor_tensor", [pbb, BY(m, off)], [BY(m, off)], out=Y[:, m, off:off + w], in0=pb[:, :w],
                               in1=Y[:, m, off:off + w], op=ALU.add)

        def stop_at(n, hf=0, dump=True):
            if KSTOP == n and hf == 0:
                if dump:
                    Bd = Buf("dump")
                    DMA("sp", yp_o[hf].rearrange("(kc p) w -> p kc w", p=128), Y[:, :, 0:1024], Bd,
                        reads=[BY(kc, off) for kc in range(16) for off in (0, 512)], final=True)
                    DMA("sp", ys_o[hf].rearrange("(kc p) w -> p kc w", p=128), Y[:, :, 1024:1040], Bd,
                        reads=[BY(kc, 1024) for kc in range(16)], final=True)
                raise StopEmit()

        def emit():
            ws.reset()
            for g in (BY, BXN, BQX, BMK, BMV, Bout, Bgath[0], Bgath[1]):
                g.reset()
            Bstage[0] = []
            Bstage[1] = []
            OP("pool", "memset", [], [Bconst], ap=ones_bf[:], constant=1.0)
            OP("pool", "memset", [], [Bconst], ap=ones_f[:], constant=1.0)
            OP("pool", "memset", [], [Bconst], ap=tri_f[:], constant=1.0)
            OP("pool", "affine_select", [Bconst], [Bconst], out=tri_f[:], in_=tri_f[:], pattern=[[1, 128]], compare_op=ALU.is_ge,
               fill=0.0, base=0, channel_multiplier=-1)
            OP("pool", "memset", [], [Bconst], ap=ident_f[:], constant=1.0)
            OP("pool", "affine_select", [Bconst], [Bconst], out=ident_f[:], in_=ident_f[:], pattern=[[1, 128]], compare_op=ALU.is_equal,
               fill=0.0, base=0, channel_multiplier=-1)
            OP("pool", "memset", [], [Bconst], ap=trimask[:], constant=0.0)
            OP("pool", "affine_select", [Bconst], [Bconst], out=trimask[:], in_=trimask[:], pattern=[[1, 128]], compare_op=ALU.is_ge,
               fill=NEG, base=0, channel_multiplier=-1)
            Bpv = Buf("pv")
            DMA("sp", pvec[:], pvec_d, Bpv, writes=[Bconst])
            DMA("sp", bfox[:], bfox_d.partition_broadcast(128), Bpv, writes=[Bconst])
            DMA("sp", vmask[:].rearrange("p a b -> p (a b)"), vmask_d.partition_broadcast(128), Bpv, writes=[Bconst])
            DMA("sp", smask[:].rearrange("p a b -> p (a b)"), smask_d.partition_broadcast(128), Bpv, writes=[Bconst])
            DMA("pool", Wf[:].rearrange("p a b -> p (a b)"), wf_d, Bpv, writes=[Bconst])

            if KSUB == 1:
                raise StopEmit()
            DMA("sp", Y[:, :, 0:256], memT.rearrange("(kc p) m -> p kc m", p=128), BY("mem"), writes=[BY(kc, 0) for kc in range(16)])
            for layer in range(2):
                rmsnorm(srcY, dstXN, PV_GMEM0 + 16 * layer, [(0, 256)])
                if KSUB == 2:
                    raise StopEmit()
                for wt in range(2):
                    desc, _ = wtile_cols(w_mem_k[layer], [(wt * 256, 256)])
                    slot, sbuf_ = ws.next(desc)
                    sv = slot[:, :].rearrange("p (a b) -> p a b", a=16, b=256)
                    for mc in range(2):
                        h = wt * 2 + mc
                        pb, pbb = ps_main.next()
                        for kc in range(16):
                            MM(pb[:, :256], sv[:, kc, mc * 128:(mc + 1) * 128], XN[:, kc, 0:256], kc == 0, kc == 15,
                               [sbuf_, BXN(kc, 0)], [pbb])
                        OP("act", "copy", [pbb], [BMK(layer)], out=MK[:, layer, h, :], in_=pb[:, :256])
                        r, rb = r32_ring.next()
                        OP("dve", "tensor_copy", [pbb], [rb], out=r[:, :256], in_=pb[:, :256])
                        DMA("sp", mk_o[layer, h * 128:(h + 1) * 128, :], r[:, :256], rb, reads=[rb], final=True)
                for wt in range(2):
                    desc, _ = wtile_cols(w_mem_v[layer], [(wt * 256, 256)])
                    slot, sbuf_ = ws.next(desc)
                    sv = slot[:, :].rearrange("p (a b) -> p a b", a=16, b=256)
                    for blk in range(2):
                        pb, pbb = ps_main.next()
                        for kc in range(16):
                            MM(pb[:, :256], XN[:, kc, blk * 128:(blk + 1) * 128], sv[:, kc, :], kc == 0, kc == 15,
                               [sbuf_, BXN(kc, 0)], [pbb])
                        OP("act", "copy", [pbb], [BMV(layer)], out=MV[:, layer, blk, wt * 256:(wt + 1) * 256], in_=pb[:, :256])
                        r, rb = r32_ring.next()
                        OP("dve", "tensor_copy", [pbb], [rb], out=r[:, :256], in_=pb[:, :256])
                        DMA("sp", mv_o[layer, blk * 128:(blk + 1) * 128, wt * 256:(wt + 1) * 256], r[:, :256], rb, reads=[rb], final=True)

            stop_at(1, dump=False)
            for hf in range(2):
                emit_half(hf)

        def emit_half(hf):
            p.barrier()
            CB = REGB[:, 0:12 * NT].rearrange("p (a b) -> p a b", a=12, b=NT)
            XHN = REGB[:, 12 * NT:12 * NT + 512].rearrange("p (a b) -> p a b", a=16, b=32)
            UH = [REGF[:, i * 1056:(i + 1) * 1056] for i in range(2)]
            UHS = REGF[:, 2112:2112 + 552].rearrange("p (a b) -> p a b", a=12, b=46)
            ACC = REGF[:, 2664:2664 + 1024]
            ACCS = REGF[:, 3688:3688 + 16]
            XH = REGF[:, 3704:3704 + 512].rearrange("p (a b) -> p a b", a=16, b=32)
            BCB = Grid("CB"); BXHN = Grid("XHN"); BUH = [Buf("uh0"), Buf("uh1")]; BUHS = Grid("UHS")
            BACC_A = Buf("acca"); BACC_B = Buf("accb"); BACC2 = [BACC_A, BACC_B]; BACCS = Buf("accs"); BXH = Grid("XH")
            uh_ring = Ring(list(zip(UH, BUH)))
            SM["MKS"] = REGB[:, 12992:14016].rearrange("p (a b) -> p a b", a=4, b=256)
            SM["MVS"] = REGB[:, 14016:15040].rearrange("p (a b) -> p a b", a=2, b=512)
            SM["BMKS"] = Buf("mks"); SM["BMVS"] = Buf("mvs")
            UHB = [REGB[:, 15040 + i * 1056:15040 + (i + 1) * 1056] for i in range(2)]
            DGM = [REGB[:, 17152 + i * 128:17152 + (i + 1) * 128] for i in range(8)]
            assert 17152 + 8 * 128 <= NRB
            uhb_ring = Ring(list(zip(UHB, [Buf("uhb0"), Buf("uhb1")])))
            dgm_ring = Ring(list(zip(DGM, [Buf("dgm") for _ in DGM])))

            xv = xp[hf].rearrange("(kc p) w -> p kc w", p=128)
            DMA("sp", XH[:, :, :], xv[:, :, 0:32], BXH("ld"), writes=[BXH(kc) for kc in range(16)])
            for q4 in range(4):
                for (off, w) in TL[:2]:
                    DMA("sp", Y[:, q4 * 4:(q4 + 1) * 4, off:off + w], xv[:, q4 * 4:(q4 + 1) * 4, 32 + off:32 + off + w], BY("ld", q4, off),
                        writes=[BY(kc, off) for kc in range(q4 * 4, q4 * 4 + 4)])
            DMA("sp", Y[:, :, 1024:1040], xs[hf].rearrange("(kc p) w -> p kc w", p=128), BY("lds"),
                writes=[BY(kc, 1024) for kc in range(16)])
            DMA("sp", UHS[:, :, 0:30], convst[hf].rearrange("(m p) w -> p m w", p=128), BUHS("ld"), writes=[BUHS(m) for m in range(12)])
            load_sample_mem(hf, 0)

            rmsnorm(srcY, dstXN, PV_GMIX0, TL)
            rmsnorm(lambda kc, off, w: (XH[:, kc, off:off + w], BXH(kc)), lambda kc, off, w: (XHN[:, kc, off:off + w], BXHN(kc)),
                    PV_GMIX0, [(0, 32)])
            for m in range(12):
                desc, _ = wtile_cols(w_in_conv, [(m * 128, 128), (1536 + m * 128, 128)])
                slot, sbuf_ = ws.next(desc)
                sva = slot[:, 0:2048].rearrange("p (a b) -> p a b", a=16, b=128)
                svg = slot[:, 2048:4096].rearrange("p (a b) -> p a b", a=16, b=128)
                uh, uhb = uh_ring.next()
                for ti in range(4):
                    if ti == 0:
                        w = 32
                        rhs = lambda kc: (XHN[:, kc, 0:32], BXHN(kc))
                        dst, dstb = uh[:, 0:32], uhb
                    else:
                        off, w = TL[ti - 1]
                        rhs = (lambda off, w: (lambda kc: (XN[:, kc, off:off + w], BXN(kc, off))))(off, w)
                        if ti < 3:
                            dst, dstb = uh[:, 32 + off:32 + off + w], uhb
                        else:
                            dst, dstb = UHS[:, m, 30:46], BUHS(m)
                    pa, pab = ps_main.next()
                    pg, pgb = ps_main.next()
                    for kc in range(16):
                        r_, rb_ = rhs(kc)
                        MM(pa[:, :w], sva[:, kc, :], r_, kc == 0, kc == 15, [sbuf_, rb_], [pab])
                    for kc in range(16):
                        r_, rb_ = rhs(kc)
                        MM(pg[:, :w], svg[:, kc, :], r_, kc == 0, kc == 15, [sbuf_, rb_], [pgb])
                    sg, sgb = r32_ring.next()
                    OP("act", "activation", [pgb], [sgb], out=sg[:, :w], in_=pg[:, :w], func=AF.Sigmoid)
                    OP("dve", "tensor_tensor", [pab, sgb], [dstb], out=dst, in0=pa[:, :w], in1=sg[:, :w], op=ALU.mult)
                cwc = lambda wi: pcol(PV_CW + wi * 12 + m)
                uhbf, uhbfb = uhb_ring.next()
                OP("act", "copy", [uhb], [uhbfb], out=uhbf, in_=uh[:, 0:1056])
                for (off, w) in TL[:2]:
                    cps, cpsb = ps_main.next()
                    for wi in range(31):
                        dgm, dgmb = dgm_ring.next()
                        OP("pool", "tensor_scalar", [Bconst], [dgmb], out=dgm, in0=ident_f[:], scalar1=cwc(wi), scalar2=0.0,
                           op0=ALU.mult, op1=ALU.add)
                        MM(cps[:, :512], dgm, uhbf[:, 2 + off + wi:2 + off + wi + 512], wi == 0, wi == 30, [dgmb, uhbfb], [cpsb])
                    OP("act", "activation", [cpsb, Bconst], [BCB(m, 0)], out=CB[:, m, off:off + 512], in_=cps[:, :512], func=AF.Identity,
                       bias=pcol(PV_CB + m), scale=1.0)
                for wi in range(31):
                    if wi == 0:
                        OP("dve", "tensor_scalar", [BUHS(m), Bconst], [BACCS], out=ACCS, in0=UHS[:, m, 0:16], scalar1=cwc(0), scalar2=pcol(PV_CB + m),
                           op0=ALU.mult, op1=ALU.add)
                    else:
                        OP("dve", "scalar_tensor_tensor", [BUHS(m), BACCS, Bconst], [BACCS], out=ACCS, in0=UHS[:, m, wi:wi + 16], scalar=cwc(wi),
                           in1=ACCS, op0=ALU.mult, op1=ALU.add)
                DMA("sp", convp_o[hf, m * 128:(m + 1) * 128, :], uh[:, 1026:1056], uhb, reads=[uhb], final=True)
                OP("act", "copy", [BACCS], [BCB(m, 1024)], out=CB[:, m, 1024:1040], in_=ACCS)
            DMA("sp", convs_o[hf].rearrange("(m p) w -> p m w", p=128), UHS[:, :, 16:46], BUHS("st"),
                reads=[BUHS(m) for m in range(12)], final=True)
            for wt in range(2):
                desc, _ = wtile_cols(w_in_conv, [(3072 + wt * 256, 256)])
                slot, sbuf_ = ws.next(desc)
                sv = slot[:, :].rearrange("p (a b) -> p a b", a=16, b=256)
                for mc in range(2):
                    h = wt * 2 + mc
                    for (off, w) in TL:
                        pb, pbb = ps_main.next()
                        for kc in range(16):
                            MM(pb[:, :w], sv[:, kc, mc * 128:(mc + 1) * 128], XN[:, kc, off:off + w], kc == 0, kc == 15,
                               [sbuf_, BXN(kc, off)], [pbb])
                        OP("act", "copy", [pbb], [BQX(h, off)], out=QX[:, h, off:off + w], in_=pb[:, :w])
            stop_at(2, hf)
            MEANB = ACC[:, 0:512]
            RSTDB = ACC[:, 512:1024]
            for (off, w) in TL:
                cboff = 0 if off < 1024 else 1024
                s_, sb_ = ps_aux.next()
                q_, qb_ = ps_aux.next()
                for m in range(12):
                    MM(s_[:, :w], ones_bf[:], CB[:, m, off:off + w], m == 0, m == 11, [Bconst, BCB(m, cboff)], [sb_])
                for m in range(12):
                    sq, sqb = sq_ring.next()
                    OP("act", "activation", [BCB(m, cboff)], [sqb], out=sq[:, :w], in_=CB[:, m, off:off + w], func=AF.Square)
                    MM(q_[:, :w], ones_bf[:], sq[:, :w], m == 0, m == 11, [Bconst, sqb], [qb_])
                OP("dve", "tensor_scalar", [sb_, *BACC2], [*BACC2], out=MEANB[:, :w], in0=s_[:, :w], scalar1=1.0 / 1536, scalar2=0.0,
                   op0=ALU.mult, op1=ALU.add)
                t_, tb_ = r32_ring.next()
                OP("dve", "tensor_tensor", [*BACC2], [tb_], out=t_[:, :w], in0=MEANB[:, :w], in1=MEANB[:, :w], op=ALU.mult)
                OP("dve", "scalar_tensor_tensor", [qb_, tb_, *BACC2], [*BACC2], out=RSTDB[:, :w], in0=q_[:, :w], scalar=1.0 / 1536, in1=t_[:, :w],
                   op0=ALU.mult, op1=ALU.subtract)
                OP("act", "activation", [*BACC2], [*BACC2], out=RSTDB[:, :w], in_=RSTDB[:, :w], func=AF.Sqrt, bias=EPS, scale=1.0)
                OP("dve", "reciprocal", [*BACC2], [*BACC2], out=RSTDB[:, :w], in_=RSTDB[:, :w])
                for m in range(12):
                    t_, tb_ = r32_ring.next()
                    OP("dve", "tensor_tensor", [BCB(m, cboff), *BACC2], [tb_], out=t_[:, :w], in0=CB[:, m, off:off + w], in1=MEANB[:, :w],
                       op=ALU.subtract)
                    OP("dve", "tensor_tensor", [tb_, *BACC2], [tb_], out=t_[:, :w], in0=t_[:, :w], in1=RSTDB[:, :w], op=ALU.mult)
                    OP("act", "activation", [tb_, Bconst], [BXN(m, off)], out=XN[:, m, off:off + w], in_=t_[:, :w], func=AF.Silu,
                       scale=pcol(PV_LNG + m), bias=pcol(PV_LNB + m))
            mem_attend(0)
            out_proj(0)
            stop_at(3, hf)
            p.barrier()
            mlp(0, PV_GMLP0)
            stop_at(4, hf)

            p.barrier()
            QT = REGB[:, 0:12 * NT].rearrange("p (a b) -> p a b", a=12, b=NT)
            o_ = 12 * NT
            KT = [REGB[:, o_ + i * 1024:o_ + (i + 1) * 1024] for i in range(2)]; o_ += 2048
            VT = [REGB[:, o_ + i * 1024:o_ + (i + 1) * 1024].rearrange("p (a b) -> p a b", a=8, b=128) for i in range(2)]; o_ += 2048
            PT = [REGB[:, o_ + i * 512:o_ + (i + 1) * 512] for i in range(3)]; o_ += 1536
            KSN = REGB[:, o_:o_ + 192].rearrange("p (a b) -> p a b", a=12, b=16); o_ += 192
            VSN = REGB[:, o_:o_ + 1536]; o_ += 1536
            assert o_ <= NRB, o_
            f_ = 0
            CQ = [REGF[:, f_ + i * 512:f_ + (i + 1) * 512] for i in range(2)]; f_ += 1024
            NCH = 4 if hf == 0 else 8
            LCS = REGF[:, f_:f_ + NCH * 96].rearrange("p (c t h) -> p c t h", c=NCH, t=8, h=12); f_ += 768
            OFF = REGF[:, f_:f_ + NCH * 96].rearrange("p (c t h) -> p c t h", c=NCH, t=8, h=12); f_ += 768
            TOT = REGF[:, f_:f_ + NCH * 12].rearrange("p (c h) -> p c h", c=NCH, h=12); f_ += 96
            GS = REGF[:, f_:f_ + 96].rearrange("p (c h) -> p c h", c=8, h=12); f_ += 96
            BASE = REGF[:, f_:f_ + 192].rearrange("p (c q h) -> p c q h", c=8, q=2, h=12); f_ += 192
            LGO = REGF[:, f_:f_ + 96].rearrange("p (t h) -> p t h", t=8, h=12); f_ += 96
            OFFO = REGF[:, f_:f_ + 108].rearrange("p (t h) -> p t h", t=9, h=12); f_ += 108
            CQREL = REGF[:, f_:f_ + 96].rearrange("p (t h) -> p t h", t=8, h=12); f_ += 96
            KBO = REGF[:, f_:f_ + 192].rearrange("p (q t h) -> p q t h", q=2, t=8, h=12); f_ += 192
            LGS = REGF[:, f_:f_ + 12]; f_ += 12
            LCSS = REGF[:, f_:f_ + 12]; f_ += 12
            NLCSS = REGF[:, f_:f_ + 12]; f_ += 12
            FB = REGF[:, f_:f_ + 12]; f_ += 12
            KBT = [REGF[:, f_ + i * 16:f_ + (i + 1) * 16].rearrange("p (q t) -> p q t", q=2, t=8) for i in range(2)]; f_ += 32
            DG = [REGF[:, f_ + i * 128:f_ + (i + 1) * 128] for i in range(2)]; f_ += 256
            VTMP = REGF[:, f_:f_ + 48].rearrange("p (c h) -> p c h", c=4, h=12); f_ += 48
            assert f_ <= NRF, f_
            BQT = Grid("QT"); BKT = [Buf("kt0"), Buf("kt1")]; BVT = [Buf("vt0"), Buf("vt1")]
            BPT = [Buf("pt") for _ in PT]; BCQ = [Buf("cq0"), Buf("cq1")]
            BL = Buf("logf"); BKSN = Buf("ksn"); BVSN = Buf("vsn"); BKBT = [Buf("kbt0"), Buf("kbt1")]; BDG = [Buf("dg0"), Buf("dg1")]
            kt_ring = Ring(list(zip(KT, BKT, VT, BVT)))
            pt_ring = Ring(list(zip(PT, BPT)))
            kbt_ring = Ring(list(zip(KBT, BKBT)))
            dg_ring = Ring(list(zip(DG, BDG)))
            ps_s = Ring(list(zip(PSB[0:3], BPS[0:3])))

            rmsnorm(srcY, dstXN, PV_GMIX1, TL)
            BLO = Buf("lo"); BLS = Buf("ls"); BLG = Buf("lg"); BFB = Buf("fb"); BstageK = []; BstageV = []
            for wt in range(6):
                desc, _ = wtile_cols(w_in_fox, [(1536 + wt * 256, 256)])
                slot, sbuf_ = ws.next(desc)
                sv = slot[:, :].rearrange("p (a b) -> p a b", a=16, b=256)
                for mc in range(2):
                    h = wt * 2 + mc
                    for (off, w) in TL:
                        pb, pbb = ps_main.next()
                        for kc in range(16):
                            MM(pb[:, :w], sv[:, kc, mc * 128:(mc + 1) * 128], XN[:, kc, off:off + w], kc == 0, kc == 15,
                               [sbuf_, BXN(kc, off)], [pbb])
                        r, rb = r32_ring.next()
                        OP("dve", "tensor_copy", [pbb], [rb], out=r[:, :w], in_=pb[:, :w])
                        if off < 1024:
                            DMA("sp", fk_o[hf, h * 128:(h + 1) * 128, off:off + w], r[:, :w], rb, reads=[rb], final=True)
                            kb_, kbb = sq_ring.next()
                            OP("act", "copy", [pbb], [kbb], out=kb_[:, :w], in_=pb[:, :w])
                            BstageK.append(Buf("stw"))
                            DMA("sp", kst[hf][h // 4][(h % 4) * 128:(h % 4 + 1) * 128, off:off + w], kb_[:, :w], kbb, reads=[kbb], writes=[BstageK[-1]])
                        else:
                            DMA("sp", fks_o[hf, h * 128:(h + 1) * 128, :], r[:, :w], rb, reads=[rb], final=True)
                            OP("act", "copy", [pbb], [BKSN], out=KSN[:, h, :], in_=pb[:, :w])
            for q in range(3):
                rec_ = p.op("pool", (lambda a_, b_: lambda e: e.collective_compute("AllGather", ALU.bypass, replica_groups=[[0, 1, 2, 3], [4, 5, 6, 7]],
                                                                                    ins=[a_.opt()], outs=[b_.opt()]))(kst[hf][q], kga[hf][q]), list(BstageK), [Bgath[hf](q, 0)])
                if rec_ is not None:
                    rec_.flag = True
            for wt in range(6):
                desc, _ = wtile_cols(w_in_fox, [(3072 + wt * 256, 256)])
                slot, sbuf_ = ws.next(desc)
                sv = slot[:, :].rearrange("p (a b) -> p a b", a=16, b=256)
                for tb in range(9):
                    off, mm = (tb * 128, 128) if tb < 8 else (1024, 16)
                    toff = 0 if off < 512 else (512 if off < 1024 else 1024)
                    pb, pbb = ps_main.next()
                    for kc in range(16):
                        MM(pb[0:mm, 0:256], XN[:, kc, off:off + mm], sv[:, kc, :], kc == 0, kc == 15, [sbuf_, BXN(kc, toff)], [pbb])
                    r, rb = r32_ring.next()
                    OP("dve", "tensor_copy", [pbb], [rb], out=r[0:mm, 0:256], in_=pb[0:mm, 0:256])
                    if tb < 8:
                        DMA("sp", fv_o[hf, off:off + 128, wt * 256:(wt + 1) * 256], r[:, 0:256], rb, reads=[rb], final=True)
                        vb_, vbb = sq_ring.next()
                        OP("act", "copy", [pbb], [vbb], out=vb_[:, 0:256], in_=pb[:, 0:256])
                        BstageV.append(Buf("stw"))
                        DMA("sp", vview(vst[hf][wt // 2], 0)[off:off + 128, (wt % 2) * 256:(wt % 2 + 1) * 256], vb_[:, 0:256], vbb, reads=[vbb],
                            writes=[BstageV[-1]])
                    else:
                        DMA("sp", fvs_o[hf, :, wt * 256:(wt + 1) * 256], r[0:16, 0:256], rb, reads=[rb], final=True)
                        OP("act", "copy", [pbb], [BVSN], out=VSN[0:16, wt * 256:(wt + 1) * 256], in_=pb[0:16, 0:256])
            for q in range(3):
                rec_ = p.op("pool", (lambda a_, b_: lambda e: e.collective_compute("AllGather", ALU.bypass, replica_groups=[[0, 1, 2, 3], [4, 5, 6, 7]],
                                                                                    ins=[a_.opt()], outs=[b_.opt()]))(vst[hf][q], vga[hf][q]), list(BstageV), [Bgath[hf](q, 1)])
                if rec_ is not None:
                    rec_.flag = True
            for tb in range(9):
                off, mm = (tb * 128, 128) if tb < 8 else (1024, 16)
                toff = 0 if off < 512 else (512 if off < 1024 else 1024)
                pb, pbb = ps_aux.next()
                for kc in range(16):
                    MM(pb[0:mm, 0:12], XN[:, kc, off:off + mm], Wf[:, kc, :], kc == 0, kc == 15, [Bconst, BXN(kc, toff)], [pbb])
                dst = LGO[:, tb, :] if tb < 8 else LGS[0:16, :]
                OP("dve", "tensor_tensor", [pbb, Bconst], [BFB], out=FB[0:mm, :], in0=pb[0:mm, 0:12], in1=bfox[0:mm, :], op=ALU.add)
                OP("act", "activation", [BFB], [BFB], out=FB[0:mm, :], in_=FB[0:mm, :], func=AF.Exp, scale=-1.0)
                OP("act", "activation", [BFB], [BFB], out=FB[0:mm, :], in_=FB[0:mm, :], func=AF.Ln, bias=1.0, scale=1.0)
                OP("dve", "tensor_scalar", [BFB], [BLO if tb < 8 else BLS], out=dst if tb < 8 else LGS[0:16, :], in0=FB[0:mm, :], scalar1=-1.0, scalar2=0.0,
                   op0=ALU.mult, op1=ALU.add)
            Bl2 = Buf("lgo_st")
            DMA("sp", fl_o[hf].rearrange("(t p) h -> p t h", p=128), LGO, Bl2, reads=[BLO], final=True)
            DMA("sp", fls_o[hf], LGS[0:16, :], Bl2, reads=[BLS], final=True)
            DMA("sp", lgst[hf].rearrange("(t p) h -> p t h", p=128), LGO, Bl2, reads=[BLO], writes=[Blgst[hf]])
            rec_ = p.op("pool", lambda e: e.collective_compute("AllGather", ALU.bypass, replica_groups=[[0, 1, 2, 3], [4, 5, 6, 7]],
                                                               ins=[lgst[hf].opt()], outs=[lgga[hf].opt()]), [Blgst[hf]], [Blgga[hf]])
            if rec_ is not None:
                rec_.flag = True
            for wt in range(6):
                desc, _ = wtile_cols(w_in_fox, [(wt * 256, 256)])
                slot, sbuf_ = ws.next(desc)
                sv = slot[:, :].rearrange("p (a b) -> p a b", a=16, b=256)
                for mc in range(2):
                    h = wt * 2 + mc
                    for (off, w) in TL:
                        pb, pbb = ps_main.next()
                        for kc in range(16):
                            MM(pb[:, :w], sv[:, kc, mc * 128:(mc + 1) * 128], XN[:, kc, off:off + w], kc == 0, kc == 15,
                               [sbuf_, BXN(kc, off)], [pbb])
                        OP("act", "copy", [pbb], [BQT(h, off)], out=QT[:, h, off:off + w], in_=pb[:, :w])
            for wt in range(2):
                desc, _ = wtile_cols(w_in_fox, [(4620 + wt * 256, 256)])
                slot, sbuf_ = ws.next(desc)
                sv = slot[:, :].rearrange("p (a b) -> p a b", a=16, b=256)
                for mc in range(2):
                    h = wt * 2 + mc
                    for (off, w) in TL:
                        pb, pbb = ps_main.next()
                        for kc in range(16):
                            MM(pb[:, :w], sv[:, kc, mc * 128:(mc + 1) * 128], XN[:, kc, off:off + w], kc == 0, kc == 15,
                               [sbuf_, BXN(kc, off)], [pbb])
                        OP("act", "copy", [pbb], [BQX(h, off)], out=QX[:, h, off:off + w], in_=pb[:, :w])
            stop_at(5, hf)
            if hf == 0:
                srcs = [(0, r) for r in range(3)]
            else:
                srcs = [(0, r) for r in range(4)] + [(1, r) for r in range(3)]
            nsrc = len(srcs)
            LG = LCS

            def chunk_cumsum(c0, c1, B_):
                ncol = (c1 - c0) * 96
                w_, wb_ = ps_aux.next()
                t_, tb_ = ps_aux.next()
                lgv = LG[:, c0:c1, :, :].rearrange("p c t h -> p (c t h)")
                MM(w_[:, :ncol], tri_f[:], lgv, True, True, [Bconst, B_], [wb_])
                MM(t_[:, :ncol], ones_f[:], lgv, True, True, [Bconst, B_], [tb_])
                tv = t_[:, :ncol].rearrange("p (c t h) -> p c t h", c=c1 - c0, t=8, h=12)
                OP("dve", "memset", [], [B_], ap=OFF[:, c0:c1, 0, :], constant=0.0)
                for tl in range(7):
                    OP("dve", "tensor_tensor", [tb_, B_], [B_], out=OFF[:, c0:c1, tl + 1, :], in0=tv[:, :, tl, :], in1=OFF[:, c0:c1, tl, :],
                       op=ALU.add)
                OP("dve", "tensor_tensor", [tb_, B_], [B_], out=TOT[:, c0:c1, :], in0=tv[:, :, 7, :], in1=OFF[:, c0:c1, 7, :], op=ALU.add)
                OP("dve", "tensor_tensor", [wb_, B_], [B_], out=lgv, in0=w_[:, :ncol],
                   in1=OFF[:, c0:c1, :, :].rearrange("p c t h -> p (c t h)"), op=ALU.add)

            w_, wb_ = ps_aux.next()
            t_, tb_ = ps_aux.next()
            lgov = LGO.rearrange("p t h -> p (t h)")
            MM(w_[:, :96], tri_f[:], lgov, True, True, [Bconst, BLO], [wb_])
            MM(t_[:, :96], ones_f[:], lgov, True, True, [Bconst, BLO], [tb_])
            tv = t_[:, :96].rearrange("p (t h) -> p t h", t=8, h=12)
            OP("dve", "memset", [], [BLO], ap=OFFO[:, 0, :], constant=0.0)
            for tl in range(8):
                OP("dve", "tensor_tensor", [tb_, BLO], [BLO], out=OFFO[:, tl + 1, :], in0=tv[:, tl, :], in1=OFFO[:, tl, :], op=ALU.add)
            OP("dve", "tensor_tensor", [wb_, BLO], [BLO], out=lgov, in0=w_[:, :96], in1=OFFO[:, 0:8, :].rearrange("p t h -> p (t h)"),
               op=ALU.add)
            LCO = LGO
            for qt in range(2):
                OP("dve", "tensor_tensor", [BLO], [BLO], out=KBO[:, qt, :, :], in0=OFFO[:, 4 * qt, :].unsqueeze(1).broadcast_to([128, 8, 12]),
                   in1=LCO, op=ALU.subtract)
                OP("dve", "tensor_tensor", [BLO], [BLO], out=CQREL[:, 4 * qt:4 * qt + 4, :], in0=LCO[:, 4 * qt:4 * qt + 4, :],
                   in1=OFFO[:, 4 * qt, :].unsqueeze(1).broadcast_to([128, 4, 12]), op=ALU.subtract)
            Bld = Buf("lgld")
            DMA("sp", LG[:, nsrc, :, :], cfl[hf].rearrange("(t p) h -> p t h", p=128), Bld, writes=[BLS])
            chunk_cumsum(nsrc, nsrc + 1, BLS)
            w_, wb_ = ps_aux.next()
            MM(w_[0:16, 0:12], tri_f[0:16, 0:16], LGS[0:16, :], True, True, [Bconst, BLS], [wb_])
            OP("dve", "tensor_copy", [wb_], [BLS], out=LCSS[0:16, :], in_=w_[0:16, 0:12])
            OP("dve", "tensor_scalar", [BLS], [BLS], out=NLCSS[0:16, :], in0=LCSS[0:16, :], scalar1=-1.0, scalar2=0.0, op0=ALU.mult, op1=ALU.add)
            stop_at(6, hf)
            OPS = [(PSB[4], BPS[4]), (PSB[5], BPS[5])]
            LPS = [(PSB[6], BPS[6]), (PSB[7], BPS[7])]
            AUXP = (PSB[3], BPS[3])

            for h in range(12):
                cqp, cqb = AUXP
                dg, dgb = dg_ring.next()
                OP("dve", "tensor_scalar", [Bconst, BLS], [dgb], out=dg[0:16, 0:16], in0=ident_f[0:16, 0:16], scalar1=LCSS[0:16, h:h + 1],
                   scalar2=0.0, op0=ALU.mult, op1=ALU.add)
                MM(cqp[:, 0:16], ones_f[0:16, :], dg[0:16, 0:16], True, True, [Bconst, dgb], [cqb])
                OP("act", "activation", [cqb], [BCQ[0]], out=CQ[0][:, 0:16], in_=cqp[:, 0:16], func=AF.Copy)
                kt_ap, ktb, vt_ap, vtb = kt_ring.next()
                DMA("pool", kt_ap, cfk[hf, h * 128:(h + 1) * 128, :], ktb, writes=[ktb])
                DMA("pool", vt_ap, cfv[hf][:, h * 128:(h + 1) * 128].rearrange("(t p) d -> p t d", p=128), vtb, writes=[vtb])
                kbt, kbtb = kbt_ring.next()
                OP("dve", "tensor_scalar", [BLS], [kbtb], out=kbt[:, 0, :], in0=LCS[:, nsrc, :, h], scalar1=-1.0,
                   scalar2=TOT[:, nsrc, h:h + 1], op0=ALU.mult, op1=ALU.add)
                o_, ob = OPS[0]
                l_, lb = LPS[0]
                pend = []
                for tl in range(9):
                    kk = 128 if tl < 8 else 16
                    sp_, spb = ps_s.next()
                    if tl < 8:
                        MM(sp_[:, 0:16], kt_ap[:, tl * 128:(tl + 1) * 128], QT[:, h, 1024:1040], True, True, [ktb, BQT(h, 1024)], [spb])
                    else:
                        MM(sp_[0:16, 0:16], KSN[:, h, :], QT[:, h, 1024:1040], True, True, [BKSN, BQT(h, 1024)], [spb])
                    OP("dve", "scalar_tensor_tensor", [spb, BCQ[0]], [spb], out=sp_[0:kk, 0:16], in0=sp_[0:kk, 0:16], scalar=SCALE,
                       in1=CQ[0][0:kk, 0:16], op0=ALU.mult, op1=ALU.add)
                    if tl == 8:
                        OP("dve", "tensor_tensor", [spb, Bconst], [spb], out=sp_[0:16, 0:16], in0=sp_[0:16, 0:16], in1=trimask[0:16, 0:16],
                           op=ALU.add)
                    pt, ptb = pt_ring.next()
                    bias_ap = kbt[:, 0, tl:tl + 1] if tl < 8 else NLCSS[0:16, h:h + 1]
                    OP("act", "activation", [spb, kbtb, BLS], [ptb], out=pt[0:kk, 0:16], in_=sp_[0:kk, 0:16], func=AF.Exp, bias=bias_ap,
                       scale=1.0)

                    def back(tl=tl, pt=pt, ptb=ptb):
                        if tl < 8:
                            MM(o_[:, 0:16], vt_ap[:, tl, :], pt[:, 0:16], tl == 0, False, [vtb, ptb], [ob])
                            MM(l_[:, 0:16], ones_bf[:], pt[:, 0:16], tl == 0, False, [Bconst, ptb], [lb])
                        else:
                            MM(o_[:, 0:16], VSN[0:16, h * 128:(h + 1) * 128], pt[0:16, 0:16], False, True, [BVSN, ptb], [ob])
                            MM(l_[:, 0:16], ones_bf[0:16, :], pt[0:16, 0:16], False, True, [Bconst, ptb], [lb])
                    pend.append(back)
                    while len(pend) > 2:
                        pend.pop(0)()
                while pend:
                    pend.pop(0)()
                rc, rcb = r32_ring.next()
                OP("dve", "reciprocal", [lb], [rcb], out=rc[:, 0:16], in_=l_[:, 0:16])
                OP("dve", "tensor_tensor", [ob, rcb], [BXN(h, 1024)], out=XN[:, h, 1024:1040], in0=o_[:, 0:16], in1=rc[:, 0:16], op=ALU.mult)

            SM["MKS"] = KT[0].rearrange("p (a b) -> p a b", a=4, b=256)
            SM["MVS"] = KT[1].rearrange("p (a b) -> p a b", a=2, b=512)
            SM["BMKS"] = BKT[0]; SM["BMVS"] = BKT[1]
            load_sample_mem(hf, 1)
            mem_attend(1)
            Bld2 = Buf("lgld2")
            for ci, (g, r) in enumerate(srcs):
                DMA("sp", LG[:, ci, :, :], lgga[g][r * 1024:(r + 1) * 1024, :].rearrange("(t p) h -> p t h", p=128), Bld2,
                    reads=[Blgga[g]], writes=[BLG])
            for c0 in range(0, nsrc, 4):
                chunk_cumsum(c0, min(nsrc, c0 + 4), BLG)

            def vt(ci, r):
                OP("dve", "tensor_tensor", [BLG, Bconst], [BLG], out=VTMP[:, r, :], in0=TOT[:, ci, :], in1=vmask[:, r, :], op=ALU.mult)
                return VTMP[:, r, :]
            last = None
            for ci in range(nsrc - 1, -1, -1):
                g, r = srcs[ci]
                masked = (hf == 0) or (g == 1)
                term = vt(ci, r) if masked else TOT[:, ci, :]
                if last is None:
                    OP("dve", "tensor_copy", [BLG], [BLG], out=GS[:, ci, :], in_=term)
                else:
                    OP("dve", "tensor_tensor", [BLG], [BLG], out=GS[:, ci, :], in0=term, in1=GS[:, last, :], op=ALU.add)
                last = ci
            for ci in range(nsrc):
                g, r = srcs[ci]
                masked = (hf == 0) or (g == 1)
                for qt in range(2):
                    OP("dve", "tensor_tensor", [BLG, BLO], [BLG], out=BASE[:, ci, qt, :], in0=GS[:, ci, :], in1=OFFO[:, 4 * qt, :], op=ALU.add)
                    if masked:
                        OP("dve", "tensor_tensor", [BLG, Bconst], [BLG], out=BASE[:, ci, qt, :], in0=BASE[:, ci, qt, :], in1=smask[:, r, :],
                           op=ALU.add)
            stop_at(6, hf)
            for h in range(12):
                for qt in range(2):
                    cqp, cqb = AUXP
                    for tt in range(4):
                        dg, dgb = dg_ring.next()
                        OP("dve", "tensor_scalar", [Bconst, BLO], [dgb], out=dg, in0=ident_f[:], scalar1=CQREL[:, 4 * qt + tt, h:h + 1],
                           scalar2=0.0, op0=ALU.mult, op1=ALU.add)
                        MM(cqp[:, tt * 128:(tt + 1) * 128], ones_f[:], dg, True, True, [Bconst, dgb], [cqb])
                    OP("act", "activation", [cqb], [BCQ[qt]], out=CQ[qt], in_=cqp[:, :], func=AF.Copy)
                first = [True, True]

                pending = []

                def flush(keep=0):
                    while len(pending) > keep:
                        pending.pop(0)()

                def tile_step(kt_ap, ktb, vt_ap, vtb, tl, qt, bias_ap, bias_bufs, c0, diag, lastflag):
                    sp_, spb = ps_s.next()
                    MM(sp_[:, c0:512], kt_ap[:, tl * 128:(tl + 1) * 128], QT[:, h, qt * 512 + c0:(qt + 1) * 512], True, True,
                       [ktb, BQT(h, qt * 512)], [spb])
                    OP("dve", "scalar_tensor_tensor", [spb, BCQ[qt]], [spb], out=sp_[:, c0:512], in0=sp_[:, c0:512], scalar=SCALE,
                       in1=CQ[qt][:, c0:512], op0=ALU.mult, op1=ALU.add)
                    if diag:
                        OP("dve", "tensor_tensor", [spb, Bconst], [spb], out=sp_[:, c0:c0 + 128], in0=sp_[:, c0:c0 + 128], in1=trimask[:],
                           op=ALU.add)
                    pt, ptb = pt_ring.next()
                    OP("act", "activation", [spb] + bias_bufs, [ptb], out=pt[:, c0:512], in_=sp_[:, c0:512], func=AF.Exp, bias=bias_ap,
                       scale=1.0)

                    def back():
                        o_, ob = OPS[qt]
                        l_, lb = LPS[qt]
                        MM(o_[:, c0:512], vt_ap[:, tl, :], pt[:, c0:512], first[qt], lastflag, [vtb, ptb], [ob])
                        MM(l_[:, c0:512], ones_bf[:], pt[:, c0:512], first[qt], lastflag, [Bconst, ptb], [lb])
                        first[qt] = False
                    pending.append(back)
                    flush(keep=2)

                for ci, (g, r) in enumerate(srcs):
                    kt_ap, ktb, vt_ap, vtb = kt_ring.next()
                    DMA("sp", kt_ap, kga[g][h // 4][r * 512 + (h % 4) * 128:r * 512 + (h % 4 + 1) * 128, :], ktb, reads=[Bgath[g](h // 4, 0)], writes=[ktb])
                    DMA("sp", vt_ap, vview(vga[g][h // 4], r)[:, (h % 4) * 128:(h % 4 + 1) * 128].rearrange("(t p) d -> p t d", p=128), vtb,
                        reads=[Bgath[g](h // 4, 1)], writes=[vtb])
                    kbt, kbtb = kbt_ring.next()
                    for qt in range(2):
                        OP("dve", "tensor_scalar", [BLG], [kbtb], out=kbt[:, qt, :], in0=LCS[:, ci, :, h], scalar1=-1.0,
                           scalar2=BASE[:, ci, qt, h:h + 1], op0=ALU.mult, op1=ALU.add)
                    for tl in range(8):
                        for qt in range(2):
                            tile_step(kt_ap, ktb, vt_ap, vtb, tl, qt, kbt[:, qt, tl:tl + 1], [kbtb], 0, False, False)
                kt_ap, ktb, vt_ap, vtb = kt_ring.next()
                DMA("sp", kt_ap, kst[hf][h // 4][(h % 4) * 128:(h % 4 + 1) * 128, :], ktb, reads=[Bgath[hf](h // 4, 0)], writes=[ktb])
                DMA("sp", vt_ap, vview(vst[hf][h // 4], 0)[:, (h % 4) * 128:(h % 4 + 1) * 128].rearrange("(t p) d -> p t d", p=128), vtb,
                    reads=[Bgath[hf](h // 4, 1)], writes=[vtb])
                for qt in range(2):
                    for tl in range(4 * qt):
                        tile_step(kt_ap, ktb, vt_ap, vtb, tl, qt, KBO[:, qt, tl, h:h + 1], [BLO], 0, False, False)
                    for j in range(4):
                        tl = 4 * qt + j
                        tile_step(kt_ap, ktb, vt_ap, vtb, tl, qt, KBO[:, qt, tl, h:h + 1], [BLO], 128 * j, True, j == 3)
                    flush()
                    o_, ob = OPS[qt]
                    l_, lb = LPS[qt]
                    rc, rcb = r32_ring.next()
                    OP("dve", "reciprocal", [lb], [rcb], out=rc[:, :], in_=l_[:, :])
                    OP("dve", "tensor_tensor", [ob, rcb], [BXN(h, qt * 512)], out=XN[:, h, qt * 512:(qt + 1) * 512], in0=o_[:, :],
                       in1=rc[:, :], op=ALU.mult)

            out_proj(1)
            p.barrier()
            stop_at(9, hf)
            mlp(1, PV_GMLP1)
            stop_at(10, hf)
            for (off, w) in TL:
                ss, ssb = ps_aux.next()
                for kc in range(16):
                    sq, sqb = sq_ring.next()
                    OP("act", "activation", [BY(kc, off)], [sqb], out=sq[:, :w], in_=Y[:, kc, off:off + w], func=AF.Square)
                    MM(ss[:, :w], ones_bf[:], sq[:, :w], kc == 0, kc == 15, [sqb, Bconst], [ssb])
                rs = MEAN_FIN[:, :w]
                OP("act", "activation", [ssb], [BFIN], out=rs, in_=ss[:, :w], func=AF.Sqrt, scale=1.0 / 2048.0, bias=EPS)
                OP("dve", "reciprocal", [BFIN], [BFIN], out=rs, in_=rs)
                for kc in range(16):
                    r, rb = r32_ring.next()
                    OP("dve", "scalar_tensor_tensor", [BY(kc, off), BFIN, Bconst], [rb], out=r[:, :w], in0=Y[:, kc, off:off + w],
                       scalar=pcol(PV_GFIN + kc), in1=rs, op0=ALU.mult, op1=ALU.mult)
                    if off < 1024:
                        DMA("sp", yp_o[hf, kc * 128:(kc + 1) * 128, off:off + w], r[:, :w], rb, reads=[rb], final=True)
                    else:
                        DMA("sp", ys_o[hf, kc * 128:(kc + 1) * 128, :], r[:, :w], rb, reads=[rb], final=True)

        def vview(ap2d, r):
            flat = ap2d[r * 512:(r + 1) * 512, :].rearrange("a b -> (a b)")
            return flat.rearrange("(t c) -> t c", c=512)

        MEAN_FIN = REGF[:, 0:512]
        BFIN = Buf("fin")

        p.dry = True
        try:
            emit()
        except StopEmit:
            pass
        p.dry = False
        wt_t = nc.dram_tensor("wt", [len(ws.uniq), 128, 4096], F32, kind="ExternalInput").ap()
        WT["ap"] = [wt_t[i] for i in range(len(ws.uniq))]
        try:
            emit()
        except StopEmit:
            pass
        p.build()
    nc._w_uniq = list(ws.uniq.keys())
    return nc


_NC = None


def kernel(x_prompt, x_sample, cache_mem_k, cache_mem_v, state_conv, cache_fox_k, cache_fox_v, cache_fox_logf,
           mem_prompt, g_mix, g_mem, w_mem_k, w_mem_v, w_in_conv, conv_w, conv_b, conv_ln_g, conv_ln_b,
           w_in_fox, b_fox_f, w_out, g_mlp, w_up, w_down, g_final):
    global _NC
    f32 = np.float32
    A = lambda a: np.ascontiguousarray(np.asarray(a, dtype=f32))
    x_prompt = A(x_prompt); x_sample = A(x_sample)
    if _NC is None:
        _NC = build_nc()
    nc = _NC

    def cols16(v):
        return np.asarray(v, f32).reshape(16, 128).T

    def cols12(v):
        return np.asarray(v, f32).reshape(12, 128).T

    pv = np.zeros((128, PV_N), f32)
    pv[:, PV_GMIX0:PV_GMIX0 + 16] = cols16(g_mix[0]); pv[:, PV_GMLP0:PV_GMLP0 + 16] = cols16(g_mlp[0])
    pv[:, PV_GMIX1:PV_GMIX1 + 16] = cols16(g_mix[1]); pv[:, PV_GMLP1:PV_GMLP1 + 16] = cols16(g_mlp[1])
    pv[:, PV_GFIN:PV_GFIN + 16] = cols16(g_final)
    pv[:, PV_GMEM0:PV_GMEM0 + 16] = cols16(g_mem[0]); pv[:, PV_GMEM1:PV_GMEM1 + 16] = cols16(g_mem[1])
    pv[:, PV_CB:PV_CB + 12] = cols12(conv_b[0]); pv[:, PV_LNG:PV_LNG + 12] = cols12(conv_ln_g[0]); pv[:, PV_LNB:PV_LNB + 12] = cols12(conv_ln_b[0])
    cw = np.asarray(conv_w[0], f32)
    for wi in range(31):
        pv[:, PV_CW + wi * 12:PV_CW + (wi + 1) * 12] = cols12(cw[wi])

    Wsrc = {"w_mem_k": A(w_mem_k), "w_mem_v": A(w_mem_v), "w_in_conv": A(w_in_conv), "w_in_fox": A(w_in_fox),
            "w_out": A(w_out), "w_up": A(w_up)}
    Wdown = A(w_down)
    uniq = nc._w_uniq
    wt = np.empty((len(uniq), 128, 4096), f32)
    for i, key in enumerate(uniq):
        if key[0] == "cols":
            W = Wsrc[key[1]][key[2]]
            off = 0
            for (c0, n) in key[3]:
                wt[i, :, off:off + 16 * n] = W[:, c0:c0 + n].reshape(16, 128, n).transpose(1, 0, 2).reshape(128, 16 * n)
                off += 16 * n
        else:
            _, layer, hg, dt_ = key
            wt[i] = Wdown[layer][hg * 512:(hg + 1) * 512, dt_ * 1024:(dt_ + 1) * 1024].reshape(4, 128, 1024).transpose(1, 0, 2).reshape(128, 4096)
    wf = Wsrc["w_in_fox"][0][:, 4608:4620].reshape(16, 128, 12).transpose(1, 0, 2).reshape(128, 192)
    shared = dict(pvec=pv, bfox=A(b_fox_f).reshape(1, 12), wt=wt, wf_d=np.ascontiguousarray(wf))
    in_maps = []
    for c in range(8):
        b, j = c // 4, c % 4
        xp = np.zeros((2, 2048, 1056), f32)
        xs = np.zeros((2, 2048, 16), f32)
        cst = np.zeros((2, 1536, 30), f32)
        cmk = np.zeros((2, 2, 512, 256), f32); cmv = np.zeros((2, 2, 256, 512), f32)
        cfk = np.zeros((2, 1536, 1024), f32); cfv = np.zeros((2, 1024, 1536), f32); cfl = np.zeros((2, 1024, 12), f32)
        for hf in range(2):
            ci = 4 * hf + j
            t0 = 1024 * ci
            xp[hf, :, 32:] = x_prompt[b, t0:t0 + 1024, :].T
            if ci > 0:
                xp[hf, :, 0:32] = x_prompt[b, t0 - 32:t0, :].T
            s = 2 * c + hf
            xs[hf] = x_sample[s].T
            cst[hf] = np.asarray(state_conv[0, s], f32).T
            for l in range(2):
                cmk[l, hf] = np.asarray(cache_mem_k[l, s], f32).transpose(1, 2, 0).reshape(512, 256)
                cmv[l, hf] = np.asarray(cache_mem_v[l, s], f32).reshape(256, 512)
            cfk[hf] = np.asarray(cache_fox_k[0, s], f32).transpose(1, 2, 0).reshape(1536, 1024)
            cfv[hf] = np.asarray(cache_fox_v[0, s], f32).reshape(1024, 1536)
            cfl[hf] = np.asarray(cache_fox_logf[0, s], f32)
        vm = np.zeros((1, 4, 12), f32); sm = np.zeros((1, 4, 12), f32)
        for i in range(4):
            vm[0, i, :] = 1.0 if i < j else 0.0
            sm[0, i, :] = 0.0 if i < j else -1.0e5
        m = dict(shared)
        m.update(xp=xp, xs=xs, memT=np.ascontiguousarray(np.asarray(mem_prompt[b], f32).T), vmask=vm.reshape(1, 48), smask=sm.reshape(1, 48),
                 convst=cst, cmk=cmk, cmv=cmv, cfk=cfk, cfv=cfv, cfl=cfl)
        if KSTOP == 1:
            for k in MINI_SKIP:
                m.pop(k)
        in_maps.append(m)

    res = run_bass_kernel_spmd(nc, in_maps, core_ids=list(range(8)))
    R = res.results

    y_prompt = np.zeros((2, 8192, 2048), f32); y_sample = np.zeros((16, 16, 2048), f32)
    new_mem_k = np.zeros((2, 2, 256, 4, 128), f32); new_mem_v = np.zeros((2, 2, 256, 4, 128), f32)
    conv_p = np.zeros((1, 2, 30, 1536), f32); conv_s = np.zeros((1, 16, 30, 1536), f32)
    fk_p = np.zeros((1, 2, 8192, 12, 128), f32); fv_p = np.zeros((1, 2, 8192, 12, 128), f32); fl_p = np.zeros((1, 2, 8192, 12), f32)
    fk_s = np.zeros((1, 16, 16, 12, 128), f32); fv_s = np.zeros((1, 16, 16, 12, 128), f32); fl_s = np.zeros((1, 16, 16, 12), f32)
    for c in range(8):
        b, j = c // 4, c % 4
        r = R[c]
        for hf in range(2):
            ci = 4 * hf + j
            t0 = 1024 * ci
            s = 2 * c + hf
            y_prompt[b, t0:t0 + 1024, :] = r["yp_o"][hf].T
            y_sample[s] = r["ys_o"][hf].T
            fk_p[0, b, t0:t0 + 1024] = r["fk_o"][hf].T.reshape(1024, 12, 128)
            fv_p[0, b, t0:t0 + 1024] = r["fv_o"][hf].reshape(1024, 12, 128)
            fl_p[0, b, t0:t0 + 1024] = r["fl_o"][hf]
            fk_s[0, s] = r["fks_o"][hf].T.reshape(16, 12, 128)
            fv_s[0, s] = r["fvs_o"][hf].reshape(16, 12, 128)
            fl_s[0, s] = r["fls_o"][hf]
            conv_s[0, s] = r["convs_o"][hf].T
            if ci == 7:
                conv_p[0, b] = r["convp_o"][hf].T
        if j == 0:
            for l in range(2):
                new_mem_k[l, b] = r["mk_o"][l].reshape(4, 128, 256).transpose(2, 0, 1)
                new_mem_v[l, b] = r["mv_o"][l].reshape(256, 4, 128)
    return (y_prompt, y_sample, new_mem_k, new_mem_v, conv_p, conv_s, fk_p, fv_p, fl_p, fk_s, fv_s, fl_s)
```

```python
import contextlib
import os
import numpy as np
import concourse.bass as bass
import concourse.mybir as mybir
from concourse.bass_utils import run_bass_kernel_spmd

F32 = mybir.dt.float32
BF16 = mybir.dt.bfloat16
AF = mybir.ActivationFunctionType
ALU = mybir.AluOpType
EPOCH = 20000
KSTOP = int(os.environ.get('KSTOP', '0'))
KSUB = int(os.environ.get('KSUB', '0'))
KNOCC = int(os.environ.get('KNOCC', '0'))


MINI_SKIP = ()


class StopEmit(Exception):
    pass
EPS = 1e-6
SCALE = 128 ** -0.5
NEG = -30000.0
TL = [(0, 512), (512, 512), (1024, 16)]
NT = 1040


class Rec:
    __slots__ = ("eng", "sem", "val", "flag")

    def __init__(self, eng):
        self.eng = eng
        self.sem = None
        self.val = None
        self.flag = False


class Buf:
    __slots__ = ("name", "w", "r", "dcount", "excl")

    def __init__(self, name="b", excl=False):
        self.name = name
        self.w = None
        self.r = {}
        self.dcount = 0
        self.excl = excl


class Prog:
    COMPUTE = ("pe", "act", "dve", "pool")
    ENGS = ("pe", "act", "dve", "pool", "sp")

    def __init__(self, nc):
        self.nc = nc
        self.dry = False
        self.stream = {e: [] for e in self.ENGS}
        self.dma_bufs = {}
        self.final_waits = []
        self.all_dma = {}
        self.nops = 0

    def _deps(self, eng, reads, writes, semkey=None):
        deps = []
        for b in reads:
            if b.w is not None:
                deps.append(b.w)
            if b.excl:
                deps.extend(r for k, r in b.r.items() if k != eng)
        for b in writes:
            if b.w is not None:
                if not (semkey is not None and b.w.sem == semkey):
                    deps.append(b.w)
            deps.extend(b.r.values())
        out = []
        seen = set()
        for d in deps:
            if id(d) in seen:
                continue
            seen.add(id(d))
            if d.eng == "pe" and eng == "pe":
                continue
            out.append(d)
        return out

    def op(self, eng, fn, reads=(), writes=()):
        if self.dry:
            return None
        deps = self._deps(eng, reads, writes)
        st = self.stream[eng]
        for d in deps:
            d.flag = True
            st.append(("wait", d))
        rec = Rec(eng)
        st.append(("op", fn, rec))
        for b in reads:
            b.r[eng] = rec
        for b in writes:
            b.w = rec
            b.r = {}
        self.nops += 1
        return rec

    def dma(self, queue, fn, owner, reads=(), writes=(), final=False):
        if self.dry:
            return None
        key = ("dma", id(owner))
        deps = self._deps("dma", reads, writes, semkey=key)
        st = self.stream[queue]
        for d in deps:
            d.flag = True
            st.append(("wait", d))
        rec = Rec("dma")
        owner.dcount += 1
        self.dma_bufs[id(owner)] = owner
        rec.sem = key
        rec.val = 16 * owner.dcount
        rec.flag = True
        st.append(("dma", fn, rec))
        for b in reads:
            b.r[key] = rec
        for b in writes:
            b.w = rec
            b.r = {}
        self.all_dma[key] = rec
        if final:
            self.final_waits.append(rec)
        return rec

    def barrier(self):
        if self.dry:
            return
        recs = []
        for e in self.COMPUTE:
            last = None
            for it in reversed(self.stream[e]):
                if it[0] == "op":
                    last = it[2]
                    break
            if last is not None:
                last.flag = True
                recs.append(last)
        recs.extend(self.all_dma.values())
        for e in self.ENGS:
            for r in recs:
                self.stream[e].append(("wait", r))

    def build(self):
        nc = self.nc
        nepoch = {}
        for e in self.COMPUTE:
            cnt = 0
            for it in self.stream[e]:
                if it[0] == "op" and it[2].flag:
                    rec = it[2]
                    ep = cnt // EPOCH
                    rec.sem = (e, ep)
                    rec.val = cnt - ep * EPOCH + 1
                    cnt += 1
            nepoch[e] = (cnt + EPOCH - 1) // EPOCH if cnt else 0
        for rec in self.final_waits:
            self.stream["sp"].append(("wait", rec))
        with contextlib.ExitStack() as es:
            sems = {}
            for e in self.COMPUTE:
                for ep in range(nepoch[e]):
                    sems[(e, ep)] = es.enter_context(nc.semaphore(f"c_{e}_{ep}"))
            for i, (k, b) in enumerate(self.dma_bufs.items()):
                sems[("dma", k)] = es.enter_context(nc.semaphore(f"d{i}"))
            print("PROG stats: sems", len(sems), "ops", {e: len(v) for e, v in self.stream.items()}, flush=True)
            block = es.enter_context(nc.Block())
            handles = {"pe": "tensor", "act": "scalar", "dve": "vector", "pool": "gpsimd", "sp": "sync"}

            def replay(engname):
                def run(eng):
                    waited = {}
                    for it in self.stream[engname]:
                        if it[0] == "wait":
                            d = it[1]
                            if waited.get(d.sem, 0) >= d.val:
                                continue
                            if d.eng != "dma" and d.sem[0] == engname and d.eng == "pe":
                                continue
                            waited[d.sem] = d.val
                            if d.eng != "dma":
                                for ep in range(d.sem[1]):
                                    waited[(d.sem[0], ep)] = EPOCH
                            eng.wait_ge(sems[d.sem], d.val)
                        else:
                            _, fn, rec = it
                            ins = fn(eng)
                            if rec.flag:
                                if rec.eng == "dma":
                                    ins.then_inc(sems[rec.sem], 16)
                                else:
                                    ins.then_inc(sems[rec.sem], 1)
                return run

            for engname in self.ENGS:
                getattr(block, handles[engname])(replay(engname))


class Ring:
    def __init__(self, items):
        self.items = items
        self.i = 0

    def next(self):
        it = self.items[self.i % len(self.items)]
        self.i += 1
        return it


class Grid:
    def __init__(self, name):
        self.name = name
        self.d = {}

    def __call__(self, *key):
        b = self.d.get(key)
        if b is None:
            b = Buf(self.name)
            self.d[key] = b
        return b

    def reset(self):
        self.d = {}


PV_GMIX0, PV_GMLP0, PV_GMIX1, PV_GMLP1, PV_GFIN, PV_GMEM0, PV_GMEM1 = 0, 16, 32, 48, 64, 80, 96
PV_CB, PV_LNG, PV_LNB, PV_CW = 112, 124, 136, 148
PV_N = 148 + 31 * 12


def build_nc():
    nc = bass.Bass("TRN2", target_bir_lowering=False)

    def din(name, shape):
        if KSTOP == 1 and name in MINI_SKIP:
            return nc.dram_tensor(name, list(shape), F32).ap()
        return nc.dram_tensor(name, list(shape), F32, kind="ExternalInput").ap()

    def dout(name, shape):
        return nc.dram_tensor(name, list(shape), F32, kind="ExternalOutput").ap()

    xp = din("xp", [2, 2048, 1056]); xs = din("xs", [2, 2048, 16]); memT = din("memT", [2048, 256])
    pvec_d = din("pvec", [128, PV_N]); bfox_d = din("bfox", [1, 12])
    vmask_d = din("vmask", [1, 48]); smask_d = din("smask", [1, 48])
    convst = din("convst", [2, 1536, 30])
    cmk = din("cmk", [2, 2, 512, 256]); cmv = din("cmv", [2, 2, 256, 512])
    cfk = din("cfk", [2, 1536, 1024]); cfv = din("cfv", [2, 1024, 1536]); cfl = din("cfl", [2, 1024, 12])
    wf_d = din("wf_d", [128, 192])
    w_mem_k = ("w_mem_k", 0), ("w_mem_k", 1)
    w_mem_v = ("w_mem_v", 0), ("w_mem_v", 1)
    w_in_conv = ("w_in_conv", 0)
    w_in_fox = ("w_in_fox", 0)
    w_out = ("w_out", 0), ("w_out", 1)
    w_up = ("w_up", 0), ("w_up", 1)
    WT = {}

    yp_o = dout("yp_o", [2, 2048, 1024]); ys_o = dout("ys_o", [2, 2048, 16])
    mk_o = dout("mk_o", [2, 512, 256]); mv_o = dout("mv_o", [2, 256, 512])
    convp_o = dout("convp_o", [2, 1536, 30]); convs_o = dout("convs_o", [2, 1536, 30])
    fk_o = dout("fk_o", [2, 1536, 1024]); fv_o = dout("fv_o", [2, 1024, 1536]); fl_o = dout("fl_o", [2, 1024, 12])
    fks_o = dout("fks_o", [2, 1536, 16]); fvs_o = dout("fvs_o", [2, 16, 1536]); fls_o = dout("fls_o", [2, 16, 12])

    KVN = 3072 * 1024
    kst = [[nc.dram_tensor(f"kst{h}_{q}", [512, 1024], BF16).ap() for q in range(3)] for h in range(2)]
    vst = [[nc.dram_tensor(f"vst{h}_{q}", [512, 1024], BF16).ap() for q in range(3)] for h in range(2)]
    kga = [[nc.dram_tensor(f"kga{h}_{q}", [2048, 1024], BF16).ap() for q in range(3)] for h in range(2)]
    vga = [[nc.dram_tensor(f"vga{h}_{q}", [2048, 1024], BF16).ap() for q in range(3)] for h in range(2)]
    lgst = [nc.dram_tensor(f"lgst{h}", [1024, 12], F32).ap() for h in range(2)]
    lgga = [nc.dram_tensor(f"lgga{h}", [4096, 12], F32).ap() for h in range(2)]

    p = Prog(nc)
    es = contextlib.ExitStack()
    with es:
        def sb(name, shape, dt):
            return es.enter_context(nc.sbuf_tensor("s_" + name, list(shape), dt))

        Y = sb("Y", [128, 16, NT], F32)
        XN = sb("XN", [128, 16, NT], BF16)
        QX = sb("QX", [128, 4, NT], BF16)
        NSLOT = 3
        WS = [sb(f"ws{i}", [128, 4096], BF16) for i in range(NSLOT)]
        MK = sb("MK", [128, 2, 4, 256], BF16)
        MV = sb("MV", [128, 2, 2, 512], BF16)
        ones_bf = sb("ones_bf", [128, 128], BF16)
        ones_f = sb("ones_f", [128, 128], F32)
        tri_f = sb("tri_f", [128, 128], F32)
        ident_f = sb("ident_f", [128, 128], F32)
        trimask = sb("trimask", [128, 128], F32)
        pvec = sb("pvec", [128, PV_N], F32)
        bfox = sb("bfox", [128, 12], F32)
        vmask = sb("vmask", [128, 4, 12], F32)
        smask = sb("smask", [128, 4, 12], F32)
        Wf = sb("Wf", [128, 16, 12], BF16)
        SQ = [sb(f"sq{i}", [128, 512], BF16) for i in range(3)]
        R32 = [sb(f"r32_{i}", [128, 512], F32) for i in range(3)]
        NRB = 20352
        NRF = 3840
        REGB = sb("REGB", [128, NRB], BF16)
        REGF = sb("REGF", [128, NRF], F32)
        PSB = [es.enter_context(nc.psum_tensor(f"ps{i}", [128, 512], F32)) for i in range(8)]

        Bconst = Buf("const")
        BY = Grid("Y"); BXN = Grid("XN"); BQX = Grid("QX")
        BWS = [Buf(f"ws{i}") for i in range(NSLOT)]
        BMK = Grid("MK"); BMV = Grid("MV"); SM = {}
        BSQ = [Buf("sq") for _ in SQ]; BR32 = [Buf("r32") for _ in R32]
        BPS = [Buf(f"ps{i}", excl=True) for i in range(8)]
        sq_ring = Ring(list(zip(SQ, BSQ)))
        r32_ring = Ring(list(zip(R32, BR32)))
        ps_main = Ring(list(zip(PSB[0:4], BPS[0:4])))
        ps_aux = Ring(list(zip(PSB[4:6], BPS[4:6])))
        Bstage = [[], []]
        Bgath = [Grid("kga0"), Grid("kga1")]
        Blgst = [Buf("lgst0"), Buf("lgst1")]
        Blgga = [Buf("lgga0"), Buf("lgga1")]
        Bout = Grid("out")

        def OP(eng, name, reads, writes, **kw):
            return p.op(eng, lambda e: getattr(e, name)(**kw), reads, writes)

        def DMA(q, out, in_, owner, reads=(), writes=(), final=False):
            return p.dma(q, lambda e: e.dma_start(out=out, in_=in_), owner, reads, writes, final)

        def MM(out, lhsT, rhs, start, stop, reads, writes):
            return p.op("pe", lambda e: e.matmul(out, lhsT=lhsT, rhs=rhs, start=start, stop=stop), reads, writes)

        def pcol(c):
            return pvec[:, c:c + 1]

        class WStream:
            def __init__(self):
                self.order = []
                self.uniq = {}
                self.pos = 0
                self.issued = 0

            def reset(self):
                self.pos = 0
                self.issued = 0

            def _issue(self, i):
                s_ = i % NSLOT
                tid = self.uniq[self.order[i]]
                DMA("pool", WS[s_][:, :], WT["ap"][tid], BWS[s_], writes=[BWS[s_]])

            def next(self, key):
                if p.dry:
                    self.order.append(key)
                    self.uniq.setdefault(key, len(self.uniq))
                    return WS[0], BWS[0]
                while self.issued < min(len(self.order), self.pos + NSLOT):
                    self._issue(self.issued)
                    self.issued += 1
                s_ = self.pos % NSLOT
                self.pos += 1
                return WS[s_], BWS[s_]

        ws = WStream()

        def wtile_cols(wref, pieces):
            return ("cols", wref[0], wref[1], tuple(pieces)), sum(n for _, n in pieces)

        def rmsnorm(src, dst, gcol0, tiles, nkc=16, denom=2048.0):
            for (off, w) in tiles:
                ss, ssb = ps_aux.next()
                for kc in range(nkc):
                    sq, sqb = sq_ring.next()
                    a, ab = src(kc, off, w)
                    OP("act", "activation", [ab], [sqb], out=sq[:, :w], in_=a, func=AF.Square)
                    MM(ss[:, :w], ones_bf[:], sq[:, :w], kc == 0, kc == nkc - 1, [sqb, Bconst], [ssb])
                rs, rsb = r32_ring.next()
                OP("act", "activation", [ssb], [rsb], out=rs[:, :w], in_=ss[:, :w], func=AF.Sqrt, scale=1.0 / denom, bias=EPS)
                OP("dve", "reciprocal", [rsb], [rsb], out=rs[:, :w], in_=rs[:, :w])
                for kc in range(nkc):
                    a, ab = src(kc, off, w)
                    d, db = dst(kc, off, w)
                    OP("dve", "scalar_tensor_tensor", [ab, rsb, Bconst], [db], out=d, in0=a, scalar=pcol(gcol0 + kc),
                       in1=rs[:, :w], op0=ALU.mult, op1=ALU.mult)

        def srcY(kc, off, w):
            return Y[:, kc, off:off + w], BY(kc, off)

        def dstXN(kc, off, w):
            return XN[:, kc, off:off + w], BXN(kc, off)

        def mem_attend(layer, prompt_tiles=True):
            for h in range(4):
                for (off, w) in TL:
                    smp = off >= 1024
                    pts = []
                    for blk in range(2):
                        sp_, spb = ps_main.next()
                        if smp:
                            lh, lb = SM["MKS"][:, h, blk * 128:(blk + 1) * 128], SM["BMKS"]
                        else:
                            lh, lb = MK[:, layer, h, blk * 128:(blk + 1) * 128], BMK(layer)
                        MM(sp_[:, :w], lh, QX[:, h, off:off + w], True, True, [lb, BQX(h, off)], [spb])
                        pt, ptb = sq_ring.next()
                        OP("act", "activation", [spb], [ptb], out=pt[:, :w], in_=sp_[:, :w], func=AF.Exp, scale=SCALE)
                        pts.append((pt, ptb))
                    o_, ob = ps_aux.next()
                    l_, lb2 = ps_aux.next()
                    for blk in range(2):
                        pt, ptb = pts[blk]
                        if smp:
                            vh, vb = SM["MVS"][:, blk, h * 128:(h + 1) * 128], SM["BMVS"]
                        else:
                            vh, vb = MV[:, layer, blk, h * 128:(h + 1) * 128], BMV(layer)
                        MM(o_[:, :w], vh, pt[:, :w], blk == 0, blk == 1, [vb, ptb], [ob])
                    for blk in range(2):
                        pt, ptb = pts[blk]
                        MM(l_[:, :w], ones_bf[:], pt[:, :w], blk == 0, blk == 1, [Bconst, ptb], [lb2])
                    rc, rcb = r32_ring.next()
                    OP("dve", "reciprocal", [lb2], [rcb], out=rc[:, :w], in_=l_[:, :w])
                    OP("dve", "tensor_tensor", [ob, rcb], [BXN(12 + h, off)], out=XN[:, 12 + h, off:off + w], in0=o_[:, :w],
                       in1=rc[:, :w], op=ALU.mult)

        def load_sample_mem(hf, layer):
            DMA("pool", SM["MKS"], cmk[layer, hf].rearrange("(h d) m -> d h m", d=128), SM["BMKS"], writes=[SM["BMKS"]])
            DMA("pool", SM["MVS"], cmv[layer, hf].rearrange("(b m) c -> m b c", m=128), SM["BMVS"], writes=[SM["BMVS"]])

        def out_proj(layer):
            for wt in range(8):
                desc, _ = wtile_cols(w_out[layer], [(wt * 256, 256)])
                slot, sbuf_ = ws.next(desc)
                sv = slot[:, :].rearrange("p (a b) -> p a b", a=16, b=256)
                for mc in range(2):
                    m = wt * 2 + mc
                    for (off, w) in TL:
                        pb, pbb = ps_main.next()
                        for kc in range(16):
                            MM(pb[:, :w], sv[:, kc, mc * 128:(mc + 1) * 128], XN[:, kc, off:off + w], kc == 0, kc == 15,
                               [sbuf_, BXN(kc, off)], [pbb])
                        OP("dve", "tensor_tensor", [pbb, BY(m, off)], [BY(m, off)], out=Y[:, m, off:off + w], in0=pb[:, :w],
                           in1=Y[:, m, off:off + w], op=ALU.add)

        def mlp(layer, gcol0):
            rmsnorm(srcY, dstXN, gcol0, TL)
            H = REGB[:, 0:4 * NT].rearrange("p (a b) -> p a b", a=4, b=NT)
            BH = Grid("H")
            for hg in range(16):
                for ut in range(2):
                    desc, _ = wtile_cols(w_up[layer], [(hg * 512 + ut * 256, 256)])
                    slot, sbuf_ = ws.next(desc)
                    sv = slot[:, :].rearrange("p (a b) -> p a b", a=16, b=256)
                    for mc in range(2):
                        hc = ut * 2 + mc
                        for (off, w) in TL:
                            pb, pbb = ps_main.next()
                            for kc in range(16):
                                MM(pb[:, :w], sv[:, kc, mc * 128:(mc + 1) * 128], XN[:, kc, off:off + w], kc == 0, kc == 15,
                                   [sbuf_, BXN(kc, off)], [pbb])
                            r, rb = r32_ring.next()
                            OP("act", "activation", [pbb], [rb], out=r[:, :w], in_=pb[:, :w], func=AF.Relu)
                            OP("dve", "tensor_tensor", [rb], [BH(hc, off)], out=H[:, hc, off:off + w], in0=r[:, :w], in1=r[:, :w],
                               op=ALU.mult)
                for dt_ in range(2):
                    slot, sbuf_ = ws.next(("down", layer, hg, dt_))
                    sv = slot[:, :].rearrange("p (a b) -> p a b", a=4, b=1024)
                    for mc in range(8):
                        m = dt_ * 8 + mc
                        for (off, w) in TL:
                            pb, pbb = ps_main.next()
                            for kc in range(4):
                                MM(pb[:, :w], sv[:, kc, mc * 128:(mc + 1) * 128], H[:, kc, off:off + w], kc == 0, kc == 3,
                                   [sbuf_, BH(kc, off)], [pbb])
                            OP("dve", "tensor_tensor", [pbb, BY(m, off)], [BY(m, off)], out=Y[:, m, off:off + w], in0=pb[:, :w],
                               in1=Y[:, m, off:off + w], op=ALU.add)

        def stop_at(n, hf=0, dump=True):
            if KSTOP == n and hf == 0:
                if dump:
                    Bd = Buf("dump")
                    DMA("sp", yp_o[hf].rearrange("(kc p) w -> p kc w", p=128), Y[:, :, 0:1024], Bd,
                        reads=[BY(kc, off) for kc in range(16) for off in (0, 512)], final=True)
                    DMA("sp", ys_o[hf].rearrange("(kc p) w -> p kc w", p=128), Y[:, :, 1024:1040], Bd,
                        reads=[BY(kc, 1024) for kc in range(16)], final=True)
                raise StopEmit()

        def emit():
            ws.reset()
            for g in (BY, BXN, BQX, BMK, BMV, Bout, Bgath[0], Bgath[1]):
                g.reset()
            Bstage[0] = []
            Bstage[1] = []
            OP("pool", "memset", [], [Bconst], ap=ones_bf[:], constant=1.0)
            OP("pool", "memset", [], [Bconst], ap=ones_f[:], constant=1.0)
            OP("pool", "memset", [], [Bconst], ap=tri_f[:], constant=1.0)
            OP("pool", "affine_select", [Bconst], [Bconst], out=tri_f[:], in_=tri_f[:], pattern=[[1, 128]], compare_op=ALU.is_ge,
               fill=0.0, base=0, channel_multiplier=-1)
            OP("pool", "memset", [], [Bconst], ap=ident_f[:], constant=1.0)
            OP("pool", "affine_select", [Bconst], [Bconst], out=ident_f[:], in_=ident_f[:], pattern=[[1, 128]], compare_op=ALU.is_equal,
               fill=0.0, base=0, channel_multiplier=-1)
            OP("pool", "memset", [], [Bconst], ap=trimask[:], constant=0.0)
            OP("pool", "affine_select", [Bconst], [Bconst], out=trimask[:], in_=trimask[:], pattern=[[1, 128]], compare_op=ALU.is_ge,
               fill=NEG, base=0, channel_multiplier=-1)
            Bpv = Buf("pv")
            DMA("sp", pvec[:], pvec_d, Bpv, writes=[Bconst])
            DMA("sp", bfox[:], bfox_d.partition_broadcast(128), Bpv, writes=[Bconst])
            DMA("sp", vmask[:].rearrange("p a b -> p (a b)"), vmask_d.partition_broadcast(128), Bpv, writes=[Bconst])
            DMA("sp", smask[:].rearrange("p a b -> p (a b)"), smask_d.partition_broadcast(128), Bpv, writes=[Bconst])
            DMA("pool", Wf[:].rearrange("p a b -> p (a b)"), wf_d, Bpv, writes=[Bconst])

            if KSUB == 1:
                raise StopEmit()
            DMA("sp", Y[:, :, 0:256], memT.rearrange("(kc p) m -> p kc m", p=128), BY("mem"), writes=[BY(kc, 0) for kc in range(16)])
            for layer in range(2):
                rmsnorm(srcY, dstXN, PV_GMEM0 + 16 * layer, [(0, 256)])
                if KSUB == 2:
                    raise StopEmit()
                for wt in range(2):
                    desc, _ = wtile_cols(w_mem_k[layer], [(wt * 256, 256)])
                    slot, sbuf_ = ws.next(desc)
                    sv = slot[:, :].rearrange("p (a b) -> p a b", a=16, b=256)
                    for mc in range(2):
                        h = wt * 2 + mc
                        pb, pbb = ps_main.next()
                        for kc in range(16):
                            MM(pb[:, :256], sv[:, kc, mc * 128:(mc + 1) * 128], XN[:, kc, 0:256], kc == 0, kc == 15,
                               [sbuf_, BXN(kc, 0)], [pbb])
                        OP("act", "copy", [pbb], [BMK(layer)], out=MK[:, layer, h, :], in_=pb[:, :256])
                        r, rb = r32_ring.next()
                        OP("dve", "tensor_copy", [pbb], [rb], out=r[:, :256], in_=pb[:, :256])
                        DMA("sp", mk_o[layer, h * 128:(h + 1) * 128, :], r[:, :256], rb, reads=[rb], final=True)
                for wt in range(2):
                    desc, _ = wtile_cols(w_mem_v[layer], [(wt * 256, 256)])
                    slot, sbuf_ = ws.next(desc)
                    sv = slot[:, :].rearrange("p (a b) -> p a b", a=16, b=256)
                    for blk in range(2):
                        pb, pbb = ps_main.next()
                        for kc in range(16):
                            MM(pb[:, :256], XN[:, kc, blk * 128:(blk + 1) * 128], sv[:, kc, :], kc == 0, kc == 15,
                               [sbuf_, BXN(kc, 0)], [pbb])
                        OP("act", "copy", [pbb], [BMV(layer)], out=MV[:, layer, blk, wt * 256:(wt + 1) * 256], in_=pb[:, :256])
                        r, rb = r32_ring.next()
                        OP("dve", "tensor_copy", [pbb], [rb], out=r[:, :256], in_=pb[:, :256])
                        DMA("sp", mv_o[layer, blk * 128:(blk + 1) * 128, wt * 256:(wt + 1) * 256], r[:, :256], rb, reads=[rb], final=True)

            stop_at(1, dump=False)
            for hf in range(2):
                emit_half(hf)

        def emit_half(hf):
            p.barrier()
            CB = REGB[:, 0:12 * NT].rearrange("p (a b) -> p a b", a=12, b=NT)
            XHN = REGB[:, 12 * NT:12 * NT + 512].rearrange("p (a b) -> p a b", a=16, b=32)
            UH = [REGF[:, 1056:2112]]
            UHS = REGF[:, 2112:2112 + 552].rearrange("p (a b) -> p a b", a=12, b=46)
            ACC = REGF[:, 2664:2664 + 1024]
            ACCS = REGF[:, 3688:3688 + 16]
            XH = REGF[:, 0:512].rearrange("p (a b) -> p a b", a=16, b=32)
            assert 3704 <= NRF
            BCB = Grid("CB"); BXHN = Grid("XHN"); BUH = [Buf("uh0")]; BUHS = Grid("UHS")
            BACC_A = Buf("acca"); BACC_B = Buf("accb"); BACC2 = [BACC_A, BACC_B]; BACCS = Buf("accs"); BXH = Grid("XH")
            uh_ring = Ring(list(zip(UH, BUH)))
            SM["MKS"] = REGB[:, 12992:14016].rearrange("p (a b) -> p a b", a=4, b=256)
            SM["MVS"] = REGB[:, 14016:15040].rearrange("p (a b) -> p a b", a=2, b=512)
            SM["BMKS"] = Buf("mks"); SM["BMVS"] = Buf("mvs")
            UHB = [REGB[:, 15040 + i * 1056:15040 + (i + 1) * 1056] for i in range(2)]
            DGM = [REGB[:, 17152 + i * 128:17152 + (i + 1) * 128] for i in range(8)]
            assert 17152 + 8 * 128 <= NRB
            uhb_ring = Ring(list(zip(UHB, [Buf("uhb0"), Buf("uhb1")])))
            dgm_ring = Ring(list(zip(DGM, [Buf("dgm") for _ in DGM])))

            xv = xp[hf].rearrange("(kc p) w -> p kc w", p=128)
            DMA("sp", XH[:, :, :], xv[:, :, 0:32], BXH("ld"), writes=[BXH(kc) for kc in range(16)])
            for q4 in range(4):
                for (off, w) in TL[:2]:
                    DMA("sp", Y[:, q4 * 4:(q4 + 1) * 4, off:off + w], xv[:, q4 * 4:(q4 + 1) * 4, 32 + off:32 + off + w], BY("ld", q4, off),
                        writes=[BY(kc, off) for kc in range(q4 * 4, q4 * 4 + 4)])
            DMA("sp", Y[:, :, 1024:1040], xs[hf].rearrange("(kc p) w -> p kc w", p=128), BY("lds"),
                writes=[BY(kc, 1024) for kc in range(16)])
            DMA("sp", UHS[:, :, 0:30], convst[hf].rearrange("(m p) w -> p m w", p=128), BUHS("ld"), writes=[BUHS(m) for m in range(12)])
            load_sample_mem(hf, 0)

            rmsnorm(srcY, dstXN, PV_GMIX0, TL)
            rmsnorm(lambda kc, off, w: (XH[:, kc, off:off + w], BXH(kc)), lambda kc, off, w: (XHN[:, kc, off:off + w], BXHN(kc)),
                    PV_GMIX0, [(0, 32)])
            for m in range(12):
                desc, _ = wtile_cols(w_in_conv, [(m * 128, 128), (1536 + m * 128, 128)])
                slot, sbuf_ = ws.next(desc)
                sva = slot[:, 0:2048].rearrange("p (a b) -> p a b", a=16, b=128)
                svg = slot[:, 2048:4096].rearrange("p (a b) -> p a b", a=16, b=128)
                uh, uhb = uh_ring.next()
                for ti in range(4):
                    if ti == 0:
                        w = 32
                        rhs = lambda kc: (XHN[:, kc, 0:32], BXHN(kc))
                        dst, dstb = uh[:, 0:32], uhb
                    else:
                        off, w = TL[ti - 1]
                        rhs = (lambda off, w: (lambda kc: (XN[:, kc, off:off + w], BXN(kc, off))))(off, w)
                        if ti < 3:
                            dst, dstb = uh[:, 32 + off:32 + off + w], uhb
                        else:
                            dst, dstb = UHS[:, m, 30:46], BUHS(m)
                    pa, pab = ps_main.next()
                    pg, pgb = ps_main.next()
                    for kc in range(16):
                        r_, rb_ = rhs(kc)
                        MM(pa[:, :w], sva[:, kc, :], r_, kc == 0, kc == 15, [sbuf_, rb_], [pab])
                    for kc in range(16):
                        r_, rb_ = rhs(kc)
                        MM(pg[:, :w], svg[:, kc, :], r_, kc == 0, kc == 15, [sbuf_, rb_], [pgb])
                    sg, sgb = r32_ring.next()
                    OP("act", "activation", [pgb], [sgb], out=sg[:, :w], in_=pg[:, :w], func=AF.Sigmoid)
                    OP("dve", "tensor_tensor", [pab, sgb], [dstb], out=dst, in0=pa[:, :w], in1=sg[:, :w], op=ALU.mult)
                cwc = lambda wi: pcol(PV_CW + wi * 12 + m)
                uhbf, uhbfb = uhb_ring.next()
                OP("act", "copy", [uhb], [uhbfb], out=uhbf, in_=uh[:, 0:1056])
                for (off, w) in TL[:2]:
                    cps, cpsb = ps_main.next()
                    for wi in range(31):
                        dgm, dgmb = dgm_ring.next()
                        OP("pool", "tensor_scalar", [Bconst], [dgmb], out=dgm, in0=ident_f[:], scalar1=cwc(wi), scalar2=0.0,
                           op0=ALU.mult, op1=ALU.add)
                        MM(cps[:, :512], dgm, uhbf[:, 2 + off + wi:2 + off + wi + 512], wi == 0, wi == 30, [dgmb, uhbfb], [cpsb])
                    OP("act", "activation", [cpsb, Bconst], [BCB(m, 0)], out=CB[:, m, off:off + 512], in_=cps[:, :512], func=AF.Identity,
                       bias=pcol(PV_CB + m), scale=1.0)
                for wi in range(31):
                    if wi == 0:
                        OP("dve", "tensor_scalar", [BUHS(m), Bconst], [BACCS], out=ACCS, in0=UHS[:, m, 0:16], scalar1=cwc(0), scalar2=pcol(PV_CB + m),
                           op0=ALU.mult, op1=ALU.add)
                    else:
                        OP("dve", "scalar_tensor_tensor", [BUHS(m), BACCS, Bconst], [BACCS], out=ACCS, in0=UHS[:, m, wi:wi + 16], scalar=cwc(wi),
                           in1=ACCS, op0=ALU.mult, op1=ALU.add)
                DMA("sp", convp_o[hf, m * 128:(m + 1) * 128, :], uh[:, 1026:1056], uhb, reads=[uhb], final=True)
                OP("act", "copy", [BACCS], [BCB(m, 1024)], out=CB[:, m, 1024:1040], in_=ACCS)
            DMA("sp", convs_o[hf].rearrange("(m p) w -> p m w", p=128), UHS[:, :, 16:46], BUHS("st"),
                reads=[BUHS(m) for m in range(12)], final=True)
            for wt in range(2):
                desc, _ = wtile_cols(w_in_conv, [(3072 + wt * 256, 256)])
                slot, sbuf_ = ws.next(desc)
                sv = slot[:, :].rearrange("p (a b) -> p a b", a=16, b=256)
                for mc in range(2):
                    h = wt * 2 + mc
                    for (off, w) in TL:
                        pb, pbb = ps_main.next()
                        for kc in range(16):
                            MM(pb[:, :w], sv[:, kc, mc * 128:(mc + 1) * 128], XN[:, kc, off:off + w], kc == 0, kc == 15,
                               [sbuf_, BXN(kc, off)], [pbb])
                        OP("act", "copy", [pbb], [BQX(h, off)], out=QX[:, h, off:off + w], in_=pb[:, :w])
            stop_at(2, hf)
            MEANB = ACC[:, 0:512]
            RSTDB = ACC[:, 512:1024]
            for (off, w) in TL:
                cboff = 0 if off < 1024 else 1024
                s_, sb_ = ps_aux.next()
                q_, qb_ = ps_aux.next()
                for m in range(12):
                    MM(s_[:, :w], ones_bf[:], CB[:, m, off:off + w], m == 0, m == 11, [Bconst, BCB(m, cboff)], [sb_])
                for m in range(12):
                    sq, sqb = sq_ring.next()
                    OP("act", "activation", [BCB(m, cboff)], [sqb], out=sq[:, :w], in_=CB[:, m, off:off + w], func=AF.Square)
                    MM(q_[:, :w], ones_bf[:], sq[:, :w], m == 0, m == 11, [Bconst, sqb], [qb_])
                OP("dve", "tensor_scalar", [sb_, *BACC2], [*BACC2], out=MEANB[:, :w], in0=s_[:, :w], scalar1=1.0 / 1536, scalar2=0.0,
                   op0=ALU.mult, op1=ALU.add)
                t_, tb_ = r32_ring.next()
                OP("dve", "tensor_tensor", [*BACC2], [tb_], out=t_[:, :w], in0=MEANB[:, :w], in1=MEANB[:, :w], op=ALU.mult)
                OP("dve", "scalar_tensor_tensor", [qb_, tb_, *BACC2], [*BACC2], out=RSTDB[:, :w], in0=q_[:, :w], scalar=1.0 / 1536, in1=t_[:, :w],
                   op0=ALU.mult, op1=ALU.subtract)
                OP("act", "activation", [*BACC2], [*BACC2], out=RSTDB[:, :w], in_=RSTDB[:, :w], func=AF.Sqrt, bias=EPS, scale=1.0)
                OP("dve", "reciprocal", [*BACC2], [*BACC2], out=RSTDB[:, :w], in_=RSTDB[:, :w])
                for m in range(12):
                    t_, tb_ = r32_ring.next()
                    OP("dve", "tensor_tensor", [BCB(m, cboff), *BACC2], [tb_], out=t_[:, :w], in0=CB[:, m, off:off + w], in1=MEANB[:, :w],
                       op=ALU.subtract)
                    OP("dve", "tensor_tensor", [tb_, *BACC2], [tb_], out=t_[:, :w], in0=t_[:, :w], in1=RSTDB[:, :w], op=ALU.mult)
                    OP("act", "activation", [tb_, Bconst], [BXN(m, off)], out=XN[:, m, off:off + w], in_=t_[:, :w], func=AF.Silu,
                       scale=pcol(PV_LNG + m), bias=pcol(PV_LNB + m))
            mem_attend(0)
            out_proj(0)
            stop_at(3, hf)
            p.barrier()
            mlp(0, PV_GMLP0)
            stop_at(4, hf)

            p.barrier()
            QT = REGB[:, 0:12 * NT].rearrange("p (a b) -> p a b", a=12, b=NT)
            o_ = 12 * NT
            KT = [REGB[:, o_ + i * 1024:o_ + (i + 1) * 1024] for i in range(2)]; o_ += 2048
            VT = [REGB[:, o_ + i * 1024:o_ + (i + 1) * 1024].rearrange("p (a b) -> p a b", a=8, b=128) for i in range(2)]; o_ += 2048
            PT = [REGB[:, o_ + i * 512:o_ + (i + 1) * 512] for i in range(4)]; o_ += 2048
            KSN = REGB[:, o_:o_ + 192].rearrange("p (a b) -> p a b", a=12, b=16); o_ += 192
            VSN = REGB[:, o_:o_ + 1536]; o_ += 1536
            assert o_ <= NRB, o_
            f_ = 0
            CQ = [REGF[:, f_ + i * 512:f_ + (i + 1) * 512] for i in range(2)]; f_ += 1024
            NCH = 4 if hf == 0 else 8
            LCS = REGF[:, f_:f_ + NCH * 96].rearrange("p (c t h) -> p c t h", c=NCH, t=8, h=12); f_ += 768
            OFF = REGF[:, f_:f_ + NCH * 96].rearrange("p (c t h) -> p c t h", c=NCH, t=8, h=12); f_ += 768
            TOT = REGF[:, f_:f_ + NCH * 12].rearrange("p (c h) -> p c h", c=NCH, h=12); f_ += 96
            GS = REGF[:, f_:f_ + 96].rearrange("p (c h) -> p c h", c=8, h=12); f_ += 96
            BASE = REGF[:, f_:f_ + 192].rearrange("p (c q h) -> p c q h", c=8, q=2, h=12); f_ += 192
            LGO = REGF[:, f_:f_ + 96].rearrange("p (t h) -> p t h", t=8, h=12); f_ += 96
            OFFO = REGF[:, f_:f_ + 108].rearrange("p (t h) -> p t h", t=9, h=12); f_ += 108
            CQREL = REGF[:, f_:f_ + 96].rearrange("p (t h) -> p t h", t=8, h=12); f_ += 96
            KBO = REGF[:, f_:f_ + 192].rearrange("p (q t h) -> p q t h", q=2, t=8, h=12); f_ += 192
            LGS = REGF[:, f_:f_ + 12]; f_ += 12
            LCSS = REGF[:, f_:f_ + 12]; f_ += 12
            NLCSS = REGF[:, f_:f_ + 12]; f_ += 12
            FB = REGF[:, f_:f_ + 12]; f_ += 12
            KBT = [REGF[:, f_ + i * 16:f_ + (i + 1) * 16].rearrange("p (q t) -> p q t", q=2, t=8) for i in range(2)]; f_ += 32
            DG = [REGF[:, f_ + i * 128:f_ + (i + 1) * 128] for i in range(2)]; f_ += 256
            VTMP = REGF[:, f_:f_ + 48].rearrange("p (c h) -> p c h", c=4, h=12); f_ += 48
            assert f_ <= NRF, f_
            BQT = Grid("QT"); BKT = [Buf("kt0"), Buf("kt1")]; BVT = [Buf("vt0"), Buf("vt1")]
            BPT = [Buf("pt") for _ in PT]; BCQ = [Buf("cq0"), Buf("cq1")]
            BL = Buf("logf"); BKSN = Buf("ksn"); BVSN = Buf("vsn"); BKBT = [Buf("kbt0"), Buf("kbt1")]; BDG = [Buf("dg0"), Buf("dg1")]
            kt_ring = Ring(list(zip(KT, BKT, VT, BVT)))
            pt_ring = Ring(list(zip(PT, BPT)))
            kbt_ring = Ring(list(zip(KBT, BKBT)))
            dg_ring = Ring(list(zip(DG, BDG)))
            ps_s = Ring(list(zip(PSB[0:4], BPS[0:4])))

            rmsnorm(srcY, dstXN, PV_GMIX1, TL)
            BLO = Buf("lo"); BLS = Buf("ls"); BLG = Buf("lg"); BFB = Buf("fb"); BstageK = []; BstageV = []
            for wt in range(6):
                desc, _ = wtile_cols(w_in_fox, [(1536 + wt * 256, 256)])
                slot, sbuf_ = ws.next(desc)
                sv = slot[:, :].rearrange("p (a b) -> p a b", a=16, b=256)
                for mc in range(2):
                    h = wt * 2 + mc
                    for (off, w) in TL:
                        pb, pbb = ps_main.next()
                        for kc in range(16):
                            MM(pb[:, :w], sv[:, kc, mc * 128:(mc + 1) * 128], XN[:, kc, off:off + w], kc == 0, kc == 15,
                               [sbuf_, BXN(kc, off)], [pbb])
                        r, rb = r32_ring.next()
                        OP("dve", "tensor_copy", [pbb], [rb], out=r[:, :w], in_=pb[:, :w])
                        if off < 1024:
                            DMA("sp", fk_o[hf, h * 128:(h + 1) * 128, off:off + w], r[:, :w], rb, reads=[rb], final=True)
                            kb_, kbb = sq_ring.next()
                            OP("act", "copy", [pbb], [kbb], out=kb_[:, :w], in_=pb[:, :w])
                            BstageK.append(Buf("stw"))
                            DMA("sp", kst[hf][h // 4][(h % 4) * 128:(h % 4 + 1) * 128, off:off + w], kb_[:, :w], kbb, reads=[kbb], writes=[BstageK[-1]])
                        else:
                            DMA("sp", fks_o[hf, h * 128:(h + 1) * 128, :], r[:, :w], rb, reads=[rb], final=True)
                            OP("act", "copy", [pbb], [BKSN], out=KSN[:, h, :], in_=pb[:, :w])
            for q in range(3):
                rec_ = p.op("pool", (lambda a_, b_: lambda e: e.collective_compute("AllGather", ALU.bypass, replica_groups=[[0, 1, 2, 3], [4, 5, 6, 7]],
                                                                                    ins=[a_.opt()], outs=[b_.opt()]))(kst[hf][q], kga[hf][q]), list(BstageK), [Bgath[hf](q, 0)])
                if rec_ is not None:
                    rec_.flag = True
            for wt in range(6):
                desc, _ = wtile_cols(w_in_fox, [(3072 + wt * 256, 256)])
                slot, sbuf_ = ws.next(desc)
                sv = slot[:, :].rearrange("p (a b) -> p a b", a=16, b=256)
                for tb in range(9):
                    off, mm = (tb * 128, 128) if tb < 8 else (1024, 16)
                    toff = 0 if off < 512 else (512 if off < 1024 else 1024)
                    pb, pbb = ps_main.next()
                    for kc in range(16):
                        MM(pb[0:mm, 0:256], XN[:, kc, off:off + mm], sv[:, kc, :], kc == 0, kc == 15, [sbuf_, BXN(kc, toff)], [pbb])
                    r, rb = r32_ring.next()
                    OP("dve", "tensor_copy", [pbb], [rb], out=r[0:mm, 0:256], in_=pb[0:mm, 0:256])
                    if tb < 8:
                        DMA("sp", fv_o[hf, off:off + 128, wt * 256:(wt + 1) * 256], r[:, 0:256], rb, reads=[rb], final=True)
                        vb_, vbb = sq_ring.next()
                        OP("act", "copy", [pbb], [vbb], out=vb_[:, 0:256], in_=pb[:, 0:256])
                        BstageV.append(Buf("stw"))
                        DMA("sp", vview(vst[hf][wt // 2], 0)[off:off + 128, (wt % 2) * 256:(wt % 2 + 1) * 256], vb_[:, 0:256], vbb, reads=[vbb],
                            writes=[BstageV[-1]])
                    else:
                        DMA("sp", fvs_o[hf, :, wt * 256:(wt + 1) * 256], r[0:16, 0:256], rb, reads=[rb], final=True)
                        OP("act", "copy", [pbb], [BVSN], out=VSN[0:16, wt * 256:(wt + 1) * 256], in_=pb[0:16, 0:256])
            for q in range(3):
                rec_ = p.op("pool", (lambda a_, b_: lambda e: e.collective_compute("AllGather", ALU.bypass, replica_groups=[[0, 1, 2, 3], [4, 5, 6, 7]],
                                                                                    ins=[a_.opt()], outs=[b_.opt()]))(vst[hf][q], vga[hf][q]), list(BstageV), [Bgath[hf](q, 1)])
                if rec_ is not None:
                    rec_.flag = True
            for tb in range(9):
                off, mm = (tb * 128, 128) if tb < 8 else (1024, 16)
                toff = 0 if off < 512 else (512 if off < 1024 else 1024)
                pb, pbb = ps_aux.next()
                for kc in range(16):
                    MM(pb[0:mm, 0:12], XN[:, kc, off:off + mm], Wf[:, kc, :], kc == 0, kc == 15, [Bconst, BXN(kc, toff)], [pbb])
                dst = LGO[:, tb, :] if tb < 8 else LGS[0:16, :]
                OP("dve", "tensor_tensor", [pbb, Bconst], [BFB], out=FB[0:mm, :], in0=pb[0:mm, 0:12], in1=bfox[0:mm, :], op=ALU.add)
                OP("act", "activation", [BFB], [BFB], out=FB[0:mm, :], in_=FB[0:mm, :], func=AF.Exp, scale=-1.0)
                OP("act", "activation", [BFB], [BFB], out=FB[0:mm, :], in_=FB[0:mm, :], func=AF.Ln, bias=1.0, scale=1.0)
                OP("dve", "tensor_scalar", [BFB], [BLO if tb < 8 else BLS], out=dst if tb < 8 else LGS[0:16, :], in0=FB[0:mm, :], scalar1=-1.0, scalar2=0.0,
                   op0=ALU.mult, op1=ALU.add)
            Bl2 = Buf("lgo_st")
            DMA("sp", fl_o[hf].rearrange("(t p) h -> p t h", p=128), LGO, Bl2, reads=[BLO], final=True)
            DMA("sp", fls_o[hf], LGS[0:16, :], Bl2, reads=[BLS], final=True)
            DMA("sp", lgst[hf].rearrange("(t p) h -> p t h", p=128), LGO, Bl2, reads=[BLO], writes=[Blgst[hf]])
            rec_ = p.op("pool", lambda e: e.collective_compute("AllGather", ALU.bypass, replica_groups=[[0, 1, 2, 3], [4, 5, 6, 7]],
                                                               ins=[lgst[hf].opt()], outs=[lgga[hf].opt()]), [Blgst[hf]], [Blgga[hf]])
            if rec_ is not None:
                rec_.flag = True
            for wt in range(6):
                desc, _ = wtile_cols(w_in_fox, [(wt * 256, 256)])
                slot, sbuf_ = ws.next(desc)
                sv = slot[:, :].rearrange("p (a b) -> p a b", a=16, b=256)
                for mc in range(2):
                    h = wt * 2 + mc
                    for (off, w) in TL:
                        pb, pbb = ps_main.next()
                        for kc in range(16):
                            MM(pb[:, :w], sv[:, kc, mc * 128:(mc + 1) * 128], XN[:, kc, off:off + w], kc == 0, kc == 15,
                               [sbuf_, BXN(kc, off)], [pbb])
                        OP("act", "copy", [pbb], [BQT(h, off)], out=QT[:, h, off:off + w], in_=pb[:, :w])
            for wt in range(2):
                desc, _ = wtile_cols(w_in_fox, [(4620 + wt * 256, 256)])
                slot, sbuf_ = ws.next(desc)
                sv = slot[:, :].rearrange("p (a b) -> p a b", a=16, b=256)
                for mc in range(2):
                    h = wt * 2 + mc
                    for (off, w) in TL:
                        pb, pbb = ps_main.next()
                        for kc in range(16):
                            MM(pb[:, :w], sv[:, kc, mc * 128:(mc + 1) * 128], XN[:, kc, off:off + w], kc == 0, kc == 15,
                               [sbuf_, BXN(kc, off)], [pbb])
                        OP("act", "copy", [pbb], [BQX(h, off)], out=QX[:, h, off:off + w], in_=pb[:, :w])
            stop_at(5, hf)
            if hf == 0:
                srcs = [(0, r) for r in range(3)]
            else:
                srcs = [(0, r) for r in range(4)] + [(1, r) for r in range(3)]
            nsrc = len(srcs)
            LG = LCS

            def chunk_cumsum(c0, c1, B_):
                ncol = (c1 - c0) * 96
                w_, wb_ = ps_aux.next()
                t_, tb_ = ps_aux.next()
                lgv = LG[:, c0:c1, :, :].rearrange("p c t h -> p (c t h)")
                MM(w_[:, :ncol], tri_f[:], lgv, True, True, [Bconst, B_], [wb_])
                MM(t_[:, :ncol], ones_f[:], lgv, True, True, [Bconst, B_], [tb_])
                tv = t_[:, :ncol].rearrange("p (c t h) -> p c t h", c=c1 - c0, t=8, h=12)
                OP("dve", "memset", [], [B_], ap=OFF[:, c0:c1, 0, :], constant=0.0)
                for tl in range(7):
                    OP("dve", "tensor_tensor", [tb_, B_], [B_], out=OFF[:, c0:c1, tl + 1, :], in0=tv[:, :, tl, :], in1=OFF[:, c0:c1, tl, :],
                       op=ALU.add)
                OP("dve", "tensor_tensor", [tb_, B_], [B_], out=TOT[:, c0:c1, :], in0=tv[:, :, 7, :], in1=OFF[:, c0:c1, 7, :], op=ALU.add)
                OP("dve", "tensor_tensor", [wb_, B_], [B_], out=lgv, in0=w_[:, :ncol],
                   in1=OFF[:, c0:c1, :, :].rearrange("p c t h -> p (c t h)"), op=ALU.add)

            w_, wb_ = ps_aux.next()
            t_, tb_ = ps_aux.next()
            lgov = LGO.rearrange("p t h -> p (t h)")
            MM(w_[:, :96], tri_f[:], lgov, True, True, [Bconst, BLO], [wb_])
            MM(t_[:, :96], ones_f[:], lgov, True, True, [Bconst, BLO], [tb_])
            tv = t_[:, :96].rearrange("p (t h) -> p t h", t=8, h=12)
            OP("dve", "memset", [], [BLO], ap=OFFO[:, 0, :], constant=0.0)
            for tl in range(8):
                OP("dve", "tensor_tensor", [tb_, BLO], [BLO], out=OFFO[:, tl + 1, :], in0=tv[:, tl, :], in1=OFFO[:, tl, :], op=ALU.add)
            OP("dve", "tensor_tensor", [wb_, BLO], [BLO], out=lgov, in0=w_[:, :96], in1=OFFO[:, 0:8, :].rearrange("p t h -> p (t h)"),
               op=ALU.add)
            LCO = LGO
            for qt in range(2):
                OP("dve", "tensor_tensor", [BLO], [BLO], out=KBO[:, qt, :, :], in0=OFFO[:, 4 * qt, :].unsqueeze(1).broadcast_to([128, 8, 12]),
                   in1=LCO, op=ALU.subtract)
                OP("dve", "tensor_tensor", [BLO], [BLO], out=CQREL[:, 4 * qt:4 * qt + 4, :], in0=LCO[:, 4 * qt:4 * qt + 4, :],
                   in1=OFFO[:, 4 * qt, :].unsqueeze(1).broadcast_to([128, 4, 12]), op=ALU.subtract)
            Bld = Buf("lgld")
            DMA("sp", LG[:, nsrc, :, :], cfl[hf].rearrange("(t p) h -> p t h", p=128), Bld, writes=[BLS])
            chunk_cumsum(nsrc, nsrc + 1, BLS)
            w_, wb_ = ps_aux.next()
            MM(w_[0:16, 0:12], tri_f[0:16, 0:16], LGS[0:16, :], True, True, [Bconst, BLS], [wb_])
            OP("dve", "tensor_copy", [wb_], [BLS], out=LCSS[0:16, :], in_=w_[0:16, 0:12])
            OP("dve", "tensor_scalar", [BLS], [BLS], out=NLCSS[0:16, :], in0=LCSS[0:16, :], scalar1=-1.0, scalar2=0.0, op0=ALU.mult, op1=ALU.add)
            stop_at(6, hf)
            OPS = [(PSB[4], BPS[4]), (PSB[5], BPS[5])]
            LPS = [(PSB[6], BPS[6]), (PSB[7], BPS[7])]
            AUXP = (PSB[3], BPS[3])

            for h in range(12):
                cqp, cqb = AUXP
                dg, dgb = dg_ring.next()
                OP("dve", "tensor_scalar", [Bconst, BLS], [dgb], out=dg[0:16, 0:16], in0=ident_f[0:16, 0:16], scalar1=LCSS[0:16, h:h + 1],
                   scalar2=0.0, op0=ALU.mult, op1=ALU.add)
                MM(cqp[:, 0:16], ones_f[0:16, :], dg[0:16, 0:16], True, True, [Bconst, dgb], [cqb])
                OP("act", "activation", [cqb], [BCQ[0]], out=CQ[0][:, 0:16], in_=cqp[:, 0:16], func=AF.Copy)
                kt_ap, ktb, vt_ap, vtb = kt_ring.next()
                DMA("pool", kt_ap, cfk[hf, h * 128:(h + 1) * 128, :], ktb, writes=[ktb])
                DMA("pool", vt_ap, cfv[hf][:, h * 128:(h + 1) * 128].rearrange("(t p) d -> p t d", p=128), vtb, writes=[vtb])
                kbt, kbtb = kbt_ring.next()
                OP("dve", "tensor_scalar", [BLS], [kbtb], out=kbt[:, 0, :], in0=LCS[:, nsrc, :, h], scalar1=-1.0,
                   scalar2=TOT[:, nsrc, h:h + 1], op0=ALU.mult, op1=ALU.add)
                o_, ob = OPS[0]
                l_, lb = LPS[0]
                pend = []
                for tl in range(9):
                    kk = 128 if tl < 8 else 16
                    sp_, spb = ps_s.next()
                    if tl < 8:
                        MM(sp_[:, 0:16], kt_ap[:, tl * 128:(tl + 1) * 128], QT[:, h, 1024:1040], True, True, [ktb, BQT(h, 1024)], [spb])
                    else:
                        MM(sp_[0:16, 0:16], KSN[:, h, :], QT[:, h, 1024:1040], True, True, [BKSN, BQT(h, 1024)], [spb])
                    OP("dve", "scalar_tensor_tensor", [spb, BCQ[0]], [spb], out=sp_[0:kk, 0:16], in0=sp_[0:kk, 0:16], scalar=SCALE,
                       in1=CQ[0][0:kk, 0:16], op0=ALU.mult, op1=ALU.add)
                    if tl == 8:
                        OP("dve", "tensor_tensor", [spb, Bconst], [spb], out=sp_[0:16, 0:16], in0=sp_[0:16, 0:16], in1=trimask[0:16, 0:16],
                           op=ALU.add)
                    pt, ptb = pt_ring.next()
                    bias_ap = kbt[:, 0, tl:tl + 1] if tl < 8 else NLCSS[0:16, h:h + 1]
                    OP("act", "activation", [spb, kbtb, BLS], [ptb], out=pt[0:kk, 0:16], in_=sp_[0:kk, 0:16], func=AF.Exp, bias=bias_ap,
                       scale=1.0)

                    def back(tl=tl, pt=pt, ptb=ptb):
                        if tl < 8:
                            MM(o_[:, 0:16], vt_ap[:, tl, :], pt[:, 0:16], tl == 0, False, [vtb, ptb], [ob])
                            MM(l_[:, 0:16], ones_bf[:], pt[:, 0:16], tl == 0, False, [Bconst, ptb], [lb])
                        else:
                            MM(o_[:, 0:16], VSN[0:16, h * 128:(h + 1) * 128], pt[0:16, 0:16], False, True, [BVSN, ptb], [ob])
                            MM(l_[:, 0:16], ones_bf[0:16, :], pt[0:16, 0:16], False, True, [Bconst, ptb], [lb])
                    pend.append(back)
                    while len(pend) > 3:
                        pend.pop(0)()
                while pend:
                    pend.pop(0)()
                rc, rcb = r32_ring.next()
                OP("dve", "reciprocal", [lb], [rcb], out=rc[:, 0:16], in_=l_[:, 0:16])
                OP("dve", "tensor_tensor", [ob, rcb], [BXN(h, 1024)], out=XN[:, h, 1024:1040], in0=o_[:, 0:16], in1=rc[:, 0:16], op=ALU.mult)

            SM["MKS"] = KT[0].rearrange("p (a b) -> p a b", a=4, b=256)
            SM["MVS"] = KT[1].rearrange("p (a b) -> p a b", a=2, b=512)
            SM["BMKS"] = BKT[0]; SM["BMVS"] = BKT[1]
            load_sample_mem(hf, 1)
            mem_attend(1)
            Bld2 = Buf("lgld2")
            for ci, (g, r) in enumerate(srcs):
                DMA("sp", LG[:, ci, :, :], lgga[g][r * 1024:(r + 1) * 1024, :].rearrange("(t p) h -> p t h", p=128), Bld2,
                    reads=[Blgga[g]], writes=[BLG])
            for c0 in range(0, nsrc, 4):
                chunk_cumsum(c0, min(nsrc, c0 + 4), BLG)

            def vt(ci, r):
                OP("dve", "tensor_tensor", [BLG, Bconst], [BLG], out=VTMP[:, r, :], in0=TOT[:, ci, :], in1=vmask[:, r, :], op=ALU.mult)
                return VTMP[:, r, :]
            last = None
            for ci in range(nsrc - 1, -1, -1):
                g, r = srcs[ci]
                masked = (hf == 0) or (g == 1)
                term = vt(ci, r) if masked else TOT[:, ci, :]
                if last is None:
                    OP("dve", "tensor_copy", [BLG], [BLG], out=GS[:, ci, :], in_=term)
                else:
                    OP("dve", "tensor_tensor", [BLG], [BLG], out=GS[:, ci, :], in0=term, in1=GS[:, last, :], op=ALU.add)
                last = ci
            for ci in range(nsrc):
                g, r = srcs[ci]
                masked = (hf == 0) or (g == 1)
                for qt in range(2):
                    OP("dve", "tensor_tensor", [BLG, BLO], [BLG], out=BASE[:, ci, qt, :], in0=GS[:, ci, :], in1=OFFO[:, 4 * qt, :], op=ALU.add)
                    if masked:
                        OP("dve", "tensor_tensor", [BLG, Bconst], [BLG], out=BASE[:, ci, qt, :], in0=BASE[:, ci, qt, :], in1=smask[:, r, :],
                           op=ALU.add)
            stop_at(6, hf)
            for h in range(12):
                for qt in range(2):
                    cqp, cqb = AUXP
                    for tt in range(4):
                        dg, dgb = dg_ring.next()
                        OP("dve", "tensor_scalar", [Bconst, BLO], [dgb], out=dg, in0=ident_f[:], scalar1=CQREL[:, 4 * qt + tt, h:h + 1],
                           scalar2=0.0, op0=ALU.mult, op1=ALU.add)
                        MM(cqp[:, tt * 128:(tt + 1) * 128], ones_f[:], dg, True, True, [Bconst, dgb], [cqb])
                    OP("act", "activation", [cqb], [BCQ[qt]], out=CQ[qt], in_=cqp[:, :], func=AF.Copy)
                first = [True, True]

                pending = []

                def flush(keep=0):
                    while len(pending) > keep:
                        pending.pop(0)()

                def tile_step(kt_ap, ktb, vt_ap, vtb, tl, qt, bias_ap, bias_bufs, c0, diag, lastflag):
                    sp_, spb = ps_s.next()
                    MM(sp_[:, c0:512], kt_ap[:, tl * 128:(tl + 1) * 128], QT[:, h, qt * 512 + c0:(qt + 1) * 512], True, True,
                       [ktb, BQT(h, qt * 512)], [spb])
                    OP("dve", "scalar_tensor_tensor", [spb, BCQ[qt]], [spb], out=sp_[:, c0:512], in0=sp_[:, c0:512], scalar=SCALE,
                       in1=CQ[qt][:, c0:512], op0=ALU.mult, op1=ALU.add)
                    if diag:
                        OP("dve", "tensor_tensor", [spb, Bconst], [spb], out=sp_[:, c0:c0 + 128], in0=sp_[:, c0:c0 + 128], in1=trimask[:],
                           op=ALU.add)
                    pt, ptb = pt_ring.next()
                    OP("act", "activation", [spb] + bias_bufs, [ptb], out=pt[:, c0:512], in_=sp_[:, c0:512], func=AF.Exp, bias=bias_ap,
                       scale=1.0)

                    def back():
                        o_, ob = OPS[qt]
                        l_, lb = LPS[qt]
                        MM(o_[:, c0:512], vt_ap[:, tl, :], pt[:, c0:512], first[qt], lastflag, [vtb, ptb], [ob])
                        MM(l_[:, c0:512], ones_bf[:], pt[:, c0:512], first[qt], lastflag, [Bconst, ptb], [lb])
                        first[qt] = False
                    pending.append(back)
                    flush(keep=3)

                for ci, (g, r) in enumerate(srcs):
                    kt_ap, ktb, vt_ap, vtb = kt_ring.next()
                    DMA("sp", kt_ap, kga[g][h // 4][r * 512 + (h % 4) * 128:r * 512 + (h % 4 + 1) * 128, :], ktb, reads=[Bgath[g](h // 4, 0)], writes=[ktb])
                    DMA("sp", vt_ap, vview(vga[g][h // 4], r)[:, (h % 4) * 128:(h % 4 + 1) * 128].rearrange("(t p) d -> p t d", p=128), vtb,
                        reads=[Bgath[g](h // 4, 1)], writes=[vtb])
                    kbt, kbtb = kbt_ring.next()
                    for qt in range(2):
                        OP("dve", "tensor_scalar", [BLG], [kbtb], out=kbt[:, qt, :], in0=LCS[:, ci, :, h], scalar1=-1.0,
                           scalar2=BASE[:, ci, qt, h:h + 1], op0=ALU.mult, op1=ALU.add)
                    for tl in range(8):
                        for qt in range(2):
                            tile_step(kt_ap, ktb, vt_ap, vtb, tl, qt, kbt[:, qt, tl:tl + 1], [kbtb], 0, False, False)
                kt_ap, ktb, vt_ap, vtb = kt_ring.next()
                DMA("sp", kt_ap, kst[hf][h // 4][(h % 4) * 128:(h % 4 + 1) * 128, :], ktb, reads=[Bgath[hf](h // 4, 0)], writes=[ktb])
                DMA("sp", vt_ap, vview(vst[hf][h // 4], 0)[:, (h % 4) * 128:(h % 4 + 1) * 128].rearrange("(t p) d -> p t d", p=128), vtb,
                    reads=[Bgath[hf](h // 4, 1)], writes=[vtb])
                for qt in range(2):
                    for tl in range(4 * qt):
                        tile_step(kt_ap, ktb, vt_ap, vtb, tl, qt, KBO[:, qt, tl, h:h + 1], [BLO], 0, False, False)
                    for j in range(4):
                        tl = 4 * qt + j
                        tile_step(kt_ap, ktb, vt_ap, vtb, tl, qt, KBO[:, qt, tl, h:h + 1], [BLO], 128 * j, True, j == 3)
                    flush()
                    o_, ob = OPS[qt]
                    l_, lb = LPS[qt]
                    rc, rcb = r32_ring.next()
                    OP("dve", "reciprocal", [lb], [rcb], out=rc[:, :], in_=l_[:, :])
                    OP("dve", "tensor_tensor", [ob, rcb], [BXN(h, qt * 512)], out=XN[:, h, qt * 512:(qt + 1) * 512], in0=o_[:, :],
                       in1=rc[:, :], op=ALU.mult)

            out_proj(1)
            p.barrier()
            stop_at(9, hf)
            mlp(1, PV_GMLP1)
            stop_at(10, hf)
            for (off, w) in TL:
                ss, ssb = ps_aux.next()
                for kc in range(16):
                    sq, sqb = sq_ring.next()
                    OP("act", "activation", [BY(kc, off)], [sqb], out=sq[:, :w], in_=Y[:, kc, off:off + w], func=AF.Square)
                    MM(ss[:, :w], ones_bf[:], sq[:, :w], kc == 0, kc == 15, [sqb, Bconst], [ssb])
                rs = MEAN_FIN[:, :w]
                OP("act", "activation", [ssb], [BFIN], out=rs, in_=ss[:, :w], func=AF.Sqrt, scale=1.0 / 2048.0, bias=EPS)
                OP("dve", "reciprocal", [BFIN], [BFIN], out=rs, in_=rs)
                for kc in range(16):
                    r, rb = r32_ring.next()
                    OP("dve", "scalar_tensor_tensor", [BY(kc, off), BFIN, Bconst], [rb], out=r[:, :w], in0=Y[:, kc, off:off + w],
                       scalar=pcol(PV_GFIN + kc), in1=rs, op0=ALU.mult, op1=ALU.mult)
                    if off < 1024:
                        DMA("sp", yp_o[hf, kc * 128:(kc + 1) * 128, off:off + w], r[:, :w], rb, reads=[rb], final=True)
                    else:
                        DMA("sp", ys_o[hf, kc * 128:(kc + 1) * 128, :], r[:, :w], rb, reads=[rb], final=True)

        def vview(ap2d, r):
            flat = ap2d[r * 512:(r + 1) * 512, :].rearrange("a b -> (a b)")
            return flat.rearrange("(t c) -> t c", c=512)

        MEAN_FIN = REGF[:, 0:512]
        BFIN = Buf("fin")

        p.dry = True
        try:
            emit()
        except StopEmit:
            pass
        p.dry = False
        wt_t = nc.dram_tensor("wt", [len(ws.uniq), 128, 4096], F32, kind="ExternalInput").ap()
        WT["ap"] = [wt_t[i] for i in range(len(ws.uniq))]
        try:
            emit()
        except StopEmit:
            pass
        p.build()
    nc._w_uniq = list(ws.uniq.keys())
    return nc


_NC = None


def kernel(x_prompt, x_sample, cache_mem_k, cache_mem_v, state_conv, cache_fox_k, cache_fox_v, cache_fox_logf,
           mem_prompt, g_mix, g_mem, w_mem_k, w_mem_v, w_in_conv, conv_w, conv_b, conv_ln_g, conv_ln_b,
           w_in_fox, b_fox_f, w_out, g_mlp, w_up, w_down, g_final):
    global _NC
    f32 = np.float32
    A = lambda a: np.ascontiguousarray(np.asarray(a, dtype=f32))
    x_prompt = A(x_prompt); x_sample = A(x_sample)
    if _NC is None:
        _NC = build_nc()
    nc = _NC

    def cols16(v):
        return np.asarray(v, f32).reshape(16, 128).T

    def cols12(v):
        return np.asarray(v, f32).reshape(12, 128).T

    pv = np.zeros((128, PV_N), f32)
    pv[:, PV_GMIX0:PV_GMIX0 + 16] = cols16(g_mix[0]); pv[:, PV_GMLP0:PV_GMLP0 + 16] = cols16(g_mlp[0])
    pv[:, PV_GMIX1:PV_GMIX1 + 16] = cols16(g_mix[1]); pv[:, PV_GMLP1:PV_GMLP1 + 16] = cols16(g_mlp[1])
    pv[:, PV_GFIN:PV_GFIN + 16] = cols16(g_final)
    pv[:, PV_GMEM0:PV_GMEM0 + 16] = cols16(g_mem[0]); pv[:, PV_GMEM1:PV_GMEM1 + 16] = cols16(g_mem[1])
    pv[:, PV_CB:PV_CB + 12] = cols12(conv_b[0]); pv[:, PV_LNG:PV_LNG + 12] = cols12(conv_ln_g[0]); pv[:, PV_LNB:PV_LNB + 12] = cols12(conv_ln_b[0])
    cw = np.asarray(conv_w[0], f32)
    for wi in range(31):
        pv[:, PV_CW + wi * 12:PV_CW + (wi + 1) * 12] = cols12(cw[wi])

    Wsrc = {"w_mem_k": A(w_mem_k), "w_mem_v": A(w_mem_v), "w_in_conv": A(w_in_conv), "w_in_fox": A(w_in_fox),
            "w_out": A(w_out), "w_up": A(w_up)}
    Wdown = A(w_down)
    uniq = nc._w_uniq
    wt = np.empty((len(uniq), 128, 4096), f32)
    for i, key in enumerate(uniq):
        if key[0] == "cols":
            W = Wsrc[key[1]][key[2]]
            off = 0
            for (c0, n) in key[3]:
                wt[i, :, off:off + 16 * n] = W[:, c0:c0 + n].reshape(16, 128, n).transpose(1, 0, 2).reshape(128, 16 * n)
                off += 16 * n
        else:
            _, layer, hg, dt_ = key
            wt[i] = Wdown[layer][hg * 512:(hg + 1) * 512, dt_ * 1024:(dt_ + 1) * 1024].reshape(4, 128, 1024).transpose(1, 0, 2).reshape(128, 4096)
    wf = Wsrc["w_in_fox"][0][:, 4608:4620].reshape(16, 128, 12).transpose(1, 0, 2).reshape(128, 192)
    shared = dict(pvec=pv, bfox=A(b_fox_f).reshape(1, 12), wt=wt, wf_d=np.ascontiguousarray(wf))
    in_maps = []
    for c in range(8):
        b, j = c // 4, c % 4
        xp = np.zeros((2, 2048, 1056), f32)
        xs = np.zeros((2, 2048, 16), f32)
        cst = np.zeros((2, 1536, 30), f32)
        cmk = np.zeros((2, 2, 512, 256), f32); cmv = np.zeros((2, 2, 256, 512), f32)
        cfk = np.zeros((2, 1536, 1024), f32); cfv = np.zeros((2, 1024, 1536), f32); cfl = np.zeros((2, 1024, 12), f32)
        for hf in range(2):
            ci = 4 * hf + j
            t0 = 1024 * ci
            xp[hf, :, 32:] = x_prompt[b, t0:t0 + 1024, :].T
            if ci > 0:
                xp[hf, :, 0:32] = x_prompt[b, t0 - 32:t0, :].T
            s = 2 * c + hf
            xs[hf] = x_sample[s].T
            cst[hf] = np.asarray(state_conv[0, s], f32).T
            for l in range(2):
                cmk[l, hf] = np.asarray(cache_mem_k[l, s], f32).transpose(1, 2, 0).reshape(512, 256)
                cmv[l, hf] = np.asarray(cache_mem_v[l, s], f32).reshape(256, 512)
            cfk[hf] = np.asarray(cache_fox_k[0, s], f32).transpose(1, 2, 0).reshape(1536, 1024)
            cfv[hf] = np.asarray(cache_fox_v[0, s], f32).reshape(1024, 1536)
            cfl[hf] = np.asarray(cache_fox_logf[0, s], f32)
        vm = np.zeros((1, 4, 12), f32); sm = np.zeros((1, 4, 12), f32)
        for i in range(4):
            vm[0, i, :] = 1.0 if i < j else 0.0
            sm[0, i, :] = 0.0 if i < j else -1.0e5
        m = dict(shared)
        m.update(xp=xp, xs=xs, memT=np.ascontiguousarray(np.asarray(mem_prompt[b], f32).T), vmask=vm.reshape(1, 48), smask=sm.reshape(1, 48),
                 convst=cst, cmk=cmk, cmv=cmv, cfk=cfk, cfv=cfv, cfl=cfl)
        if KSTOP == 1:
            for k in MINI_SKIP:
                m.pop(k)
        in_maps.append(m)

    res = run_bass_kernel_spmd(nc, in_maps, core_ids=list(range(8)))
    R = res.results

    y_prompt = np.zeros((2, 8192, 2048), f32); y_sample = np.zeros((16, 16, 2048), f32)
    new_mem_k = np.zeros((2, 2, 256, 4, 128), f32); new_mem_v = np.zeros((2, 2, 256, 4, 128), f32)
    conv_p = np.zeros((1, 2, 30, 1536), f32); conv_s = np.zeros((1, 16, 30, 1536), f32)
    fk_p = np.zeros((1, 2, 8192, 12, 128), f32); fv_p = np.zeros((1, 2, 8192, 12, 128), f32); fl_p = np.zeros((1, 2, 8192, 12), f32)
    fk_s = np.zeros((1, 16, 16, 12, 128), f32); fv_s = np.zeros((1, 16, 16, 12, 128), f32); fl_s = np.zeros((1, 16, 16, 12), f32)
    for c in range(8):
        b, j = c // 4, c % 4
        r = R[c]
        for hf in range(2):
            ci = 4 * hf + j
            t0 = 1024 * ci
            s = 2 * c + hf
            y_prompt[b, t0:t0 + 1024, :] = r["yp_o"][hf].T
            y_sample[s] = r["ys_o"][hf].T
            fk_p[0, b, t0:t0 + 1024] = r["fk_o"][hf].T.reshape(1024, 12, 128)
            fv_p[0, b, t0:t0 + 1024] = r["fv_o"][hf].reshape(1024, 12, 128)
            fl_p[0, b, t0:t0 + 1024] = r["fl_o"][hf]
            fk_s[0, s] = r["fks_o"][hf].T.reshape(16, 12, 128)
            fv_s[0, s] = r["fvs_o"][hf].reshape(16, 12, 128)
            fl_s[0, s] = r["fls_o"][hf]
            conv_s[0, s] = r["convs_o"][hf].T
            if ci == 7:
                conv_p[0, b] = r["convp_o"][hf].T
        if j == 0:
            for l in range(2):
                new_mem_k[l, b] = r["mk_o"][l].reshape(4, 128, 256).transpose(2, 0, 1)
                new_mem_v[l, b] = r["mv_o"][l].reshape(256, 4, 128)
    return (y_prompt, y_sample, new_mem_k, new_mem_v, conv_p, conv_s, fk_p, fv_p, fl_p, fk_s, fv_s, fl_s)
```
